# Optimizing a Trainium2 kernel written in Bass

```python
import jax, jax.numpy as jnp
from jax import lax
import numpy as np

D_MODEL = 1024
BATCH = 8
SEQ = 2048
DEPTH = 1
DEC_BATCH = 128
DEC_SEQ = 4
PAST_LEN = 16384
PAGE_SIZE = 128

PLE_DIM = 256
D_FF = 2816
EPS = 1e-6
SSD_HEADS = 16
SSD_HEADDIM = 64
SSD_DINNER = SSD_HEADS * SSD_HEADDIM
SSD_STATE = 128
SSD_GROUPS = 2
CONV_W = 4
SSD_CHUNK = 128
CONV_DIM = SSD_DINNER + 2 * SSD_GROUPS * SSD_STATE
HG_HEADS = 8
HG_DK = 128
HG_DV = 128
HG_WIDTH = HG_HEADS * HG_DV
HG_CHUNK = 16
D_MIX = SSD_DINNER + HG_WIDTH
D_IN_PROJ = SSD_DINNER + CONV_DIM + SSD_HEADS + 2 * HG_HEADS * HG_DK + 2 * HG_WIDTH

kernel_name = "hymba_ssd_hgrn2_macaron_step"


def _rmsnorm(x, w):
    xf = x.astype(jnp.float32)
    r = lax.rsqrt(jnp.mean(xf * xf, axis=-1, keepdims=True) + EPS)
    return (xf * r * w.astype(jnp.float32)).astype(x.dtype)


def _swiglu(h, w_up, w_down):
    g, u = jnp.split(h @ w_up, 2, axis=-1)
    return (jax.nn.silu(g) * u) @ w_down


def _pad_time(t, npad):
    if npad == 0:
        return t
    return jnp.pad(t, [(0, 0), (0, npad)] + [(0, 0)] * (t.ndim - 2))


def _ssd_chunked(x, dt, a, bm, cm, h0):
    b, L = x.shape[:2]
    G, R, P, N = SSD_GROUPS, SSD_HEADS // SSD_GROUPS, SSD_HEADDIM, SSD_STATE
    c = min(SSD_CHUNK, L)
    npad = (-L) % c
    x, dt, bm, cm = [_pad_time(t, npad) for t in (x, dt, bm, cm)]
    nc = (L + npad) // c
    x = x.reshape(b, nc, c, G, R, P)
    dt = dt.reshape(b, nc, c, G, R)
    bm = bm.reshape(b, nc, c, G, N)
    cm = cm.reshape(b, nc, c, G, N)
    acs = jnp.cumsum(dt * a.reshape(G, R), axis=2)
    xdt = x * dt[..., None]
    mask = jnp.tril(jnp.ones((c, c), dtype=bool))[None, None, :, :, None, None]
    diff = acs[:, :, :, None] - acs[:, :, None, :]
    decay = jnp.exp(jnp.where(mask, diff, -jnp.inf))
    cb = jnp.einsum('bctgn,bcsgn->bctsg', cm, bm)
    y_intra = jnp.einsum('bctsg,bctsgr,bcsgrp->bctgrp', cb, decay, xdt)
    dec_end = jnp.exp(acs[:, :, -1:] - acs)
    chunk_decay = jnp.exp(acs[:, :, -1])
    dec_in = jnp.exp(acs)

    def step(h, inp):
        b_c, c_c, x_c, de_c, di_c, dc_c = inp
        y_int = jnp.einsum('btgn,bgrpn,btgr->btgrp', c_c, h, di_c)
        h_new = h * dc_c[..., None, None] + jnp.einsum('bsgn,bsgr,bsgrp->bgrpn', b_c, de_c, x_c)
        return h_new, y_int

    mv = lambda t: jnp.moveaxis(t, 1, 0)
    h_fin, y_inter = lax.scan(step, h0.reshape(b, G, R, P, N),
                              (mv(bm), mv(cm), mv(xdt), mv(dec_end), mv(dec_in), mv(chunk_decay)))
    y = y_intra + jnp.moveaxis(y_inter, 0, 1)
    y = y.reshape(b, nc * c, SSD_HEADS, P)[:, :L]
    return y, h_fin.reshape(b, SSD_HEADS, P, N)


def _gla_chunked(q, k, v, logf, s0):
    b, L, H, K = q.shape
    V = v.shape[-1]
    c = min(HG_CHUNK, L)
    npad = (-L) % c
    q, k, v, logf = [_pad_time(t, npad) for t in (q, k, v, logf)]
    nc = (L + npad) // c
    q = q.reshape(b, nc, c, H, K)
    k = k.reshape(b, nc, c, H, K)
    v = v.reshape(b, nc, c, H, V)
    bcum = jnp.cumsum(logf.reshape(b, nc, c, H, K), axis=2)
    qe = q * jnp.exp(bcum)
    ke = k * jnp.exp(-bcum)
    mask = jnp.tril(jnp.ones((c, c), dtype=bool))
    att = jnp.where(mask, jnp.einsum('bcthk,bcshk->bchts', qe, ke), 0.0)
    o_intra = jnp.einsum('bchts,bcshv->bcthv', att, v)
    kd = k * jnp.exp(bcum[:, :, -1:] - bcum)
    chunk_decay = jnp.exp(bcum[:, :, -1])

    def step(S, inp):
        qe_c, kd_c, v_c, dc_c = inp
        o_int = jnp.einsum('bthk,bhkv->bthv', qe_c, S)
        S_new = S * dc_c[..., None] + jnp.einsum('bshk,bshv->bhkv', kd_c, v_c)
        return S_new, o_int

    mv = lambda t: jnp.moveaxis(t, 1, 0)
    s_fin, o_inter = lax.scan(step, s0, (mv(qe), mv(kd), mv(v), mv(chunk_decay)))
    o = o_intra + jnp.moveaxis(o_inter, 0, 1)
    return o.reshape(b, nc * c, H, V)[:, :L], s_fin


def _mixer(h, conv0, ssm0, hg0, lb, w_in, conv_w, conv_b, dt_bias, a_log, d_skip, ssd_norm, hg_norm, w_out):
    b, L, _ = h.shape
    f32 = jnp.float32
    proj = (h @ w_in).astype(f32)
    sizes = [SSD_DINNER, CONV_DIM, SSD_HEADS, HG_HEADS * HG_DK, HG_HEADS * HG_DK, HG_WIDTH, HG_WIDTH]
    points = [int(v) for v in np.cumsum(sizes)[:-1]]
    z, xbc, dt_raw, q, fr, iv, og = jnp.split(proj, points, axis=-1)
    xbc_full = jnp.concatenate([conv0.astype(f32), xbc], axis=1)
    cw = conv_w.astype(f32)
    conv = conv_b.astype(f32) + sum(cw[j] * xbc_full[:, j:j + L] for j in range(CONV_W))
    conv = jax.nn.silu(conv)
    new_conv = xbc_full[:, -(CONV_W - 1):]
    xs, bm, cm = jnp.split(conv, [SSD_DINNER, SSD_DINNER + SSD_GROUPS * SSD_STATE], axis=-1)
    xs = xs.reshape(b, L, SSD_HEADS, SSD_HEADDIM)
    bm = bm.reshape(b, L, SSD_GROUPS, SSD_STATE)
    cm = cm.reshape(b, L, SSD_GROUPS, SSD_STATE)
    dt = jax.nn.softplus(dt_raw + dt_bias.astype(f32))
    a = -jnp.exp(a_log.astype(f32))
    y, new_ssm = _ssd_chunked(xs, dt, a, bm, cm, ssm0.astype(f32))
    y = y + d_skip.astype(f32)[:, None] * xs
    yg = (y.reshape(b, L, SSD_DINNER) * jax.nn.silu(z)).reshape(b, L, SSD_GROUPS, SSD_DINNER // SSD_GROUPS)
    yg = yg * lax.rsqrt(jnp.mean(yg * yg, axis=-1, keepdims=True) + EPS)
    y_ssd = yg.reshape(b, L, SSD_DINNER) * ssd_norm.astype(f32)
    f = lb + (1.0 - lb) * jax.nn.sigmoid(fr)
    logf = jnp.log(f).reshape(b, L, HG_HEADS, HG_DK)
    kk = (1.0 - f).reshape(b, L, HG_HEADS, HG_DK)
    qq = jax.nn.silu(q).reshape(b, L, HG_HEADS, HG_DK)
    vv = iv.reshape(b, L, HG_HEADS, HG_DV)
    o, new_hg = _gla_chunked(qq, kk, vv, logf, hg0.astype(f32))
    o = o * lax.rsqrt(jnp.mean(o * o, axis=-1, keepdims=True) + EPS)
    o = o.reshape(b, L, HG_WIDTH) * hg_norm.astype(f32) * jax.nn.silu(og)
    mixed = jnp.concatenate([y_ssd, o], axis=-1).astype(h.dtype) @ w_out
    return mixed, new_conv, new_ssm, new_hg


def _layer(x, p, conv0, ssm0, hg0, lb, lw):
    (n_f1, w_f1u, w_f1d, n_mix, w_in, conv_w, conv_b, dt_bias, a_log, d_skip, ssd_norm, hg_norm, w_out,
     n_f2, w_f2u, w_f2d, n_ple, w_ple_gate, w_ple_proj, ple_post) = lw
    x = x + 0.5 * _swiglu(_rmsnorm(x, n_f1), w_f1u, w_f1d)
    m, new_conv, new_ssm, new_hg = _mixer(_rmsnorm(x, n_mix), conv0, ssm0, hg0, lb, w_in, conv_w, conv_b,
                                          dt_bias, a_log, d_skip, ssd_norm, hg_norm, w_out)
    x = x + m.astype(x.dtype)
    x = x + 0.5 * _swiglu(_rmsnorm(x, n_f2), w_f2u, w_f2d)
    gate = jax.nn.sigmoid((_rmsnorm(x, n_ple) @ w_ple_gate).astype(jnp.float32))
    e = _rmsnorm(p.astype(x.dtype) @ w_ple_proj, ple_post).astype(jnp.float32)
    x = x + (gate * e).astype(x.dtype)
    return x, new_conv, new_ssm, new_hg


def setup_inputs(seed: int = 0) -> dict:
    key = jax.random.key(seed)
    ks = iter(jax.random.split(key, 40))
    f32 = jnp.float32
    nrm = lambda shape, s: jax.random.normal(next(ks), shape, f32) * s
    gain = lambda shape: 1.0 + nrm(shape, 0.02)
    dt0 = jnp.exp(jax.random.uniform(next(ks), (DEPTH, SSD_HEADS), f32, np.log(1e-3), np.log(1e-1)))
    return {
        "x_prompt": nrm((BATCH, SEQ, D_MODEL), 1.0),
        "x_sample": nrm((DEC_BATCH, DEC_SEQ, D_MODEL), 1.0),
        "state_conv": nrm((DEPTH, DEC_BATCH, CONV_W - 1, CONV_DIM), 1.0),
        "state_ssm": nrm((DEPTH, DEC_BATCH, SSD_HEADS, SSD_HEADDIM, SSD_STATE), 0.1),
        "state_hgrn": nrm((DEPTH, DEC_BATCH, HG_HEADS, HG_DK, HG_DV), 0.3),
        "p_prompt": nrm((DEPTH, BATCH, SEQ, PLE_DIM), 1.0),
        "p_sample": nrm((DEPTH, DEC_BATCH, DEC_SEQ, PLE_DIM), 1.0),
        "norm_ffn1": gain((DEPTH, D_MODEL)),
        "w_ffn1_up": nrm((DEPTH, D_MODEL, 2 * D_FF), D_MODEL ** -0.5),
        "w_ffn1_down": nrm((DEPTH, D_FF, D_MODEL), D_FF ** -0.5),
        "norm_mix": gain((DEPTH, D_MODEL)),
        "w_in": nrm((DEPTH, D_MODEL, D_IN_PROJ), D_MODEL ** -0.5),
        "conv_w": nrm((DEPTH, CONV_W, CONV_DIM), CONV_W ** -0.5),
        "conv_b": nrm((DEPTH, CONV_DIM), 0.02),
        "dt_bias": dt0 + jnp.log(-jnp.expm1(-dt0)),
        "a_log": jnp.log(jax.random.uniform(next(ks), (DEPTH, SSD_HEADS), f32, 1.0, 16.0)),
        "d_skip": gain((DEPTH, SSD_HEADS)),
        "ssd_norm": gain((DEPTH, SSD_DINNER)),
        "hg_lb_logits": nrm((DEPTH + 1, HG_HEADS * HG_DK), 0.1),
        "hg_norm": gain((DEPTH, HG_WIDTH)),
        "w_out": nrm((DEPTH, D_MIX, D_MODEL), D_MIX ** -0.5),
        "norm_ffn2": gain((DEPTH, D_MODEL)),
        "w_ffn2_up": nrm((DEPTH, D_MODEL, 2 * D_FF), D_MODEL ** -0.5),
        "w_ffn2_down": nrm((DEPTH, D_FF, D_MODEL), D_FF ** -0.5),
        "norm_ple": gain((DEPTH, D_MODEL)),
        "w_ple_gate": nrm((DEPTH, D_MODEL, D_MODEL), D_MODEL ** -0.5),
        "w_ple_proj": nrm((DEPTH, PLE_DIM, D_MODEL), PLE_DIM ** -0.5),
        "ple_post_norm": gain((DEPTH, D_MODEL)),
        "norm_final": gain((D_MODEL,)),
    }


def reference(x_prompt, x_sample, state_conv, state_ssm, state_hgrn, p_prompt, p_sample,
              norm_ffn1, w_ffn1_up, w_ffn1_down, norm_mix, w_in, conv_w, conv_b, dt_bias, a_log, d_skip,
              ssd_norm, hg_lb_logits, hg_norm, w_out, norm_ffn2, w_ffn2_up, w_ffn2_down,
              norm_ple, w_ple_gate, w_ple_proj, ple_post_norm, norm_final):
    f32 = jnp.float32
    bp = x_prompt.shape[0]
    lb_all = jnp.cumsum(jax.nn.softmax(hg_lb_logits.astype(f32), axis=0), axis=0)
    xp, xs = x_prompt, x_sample
    conv_p, ssm_p, hg_p, conv_s, ssm_s, hg_s = [], [], [], [], [], []
    for i in range(DEPTH):
        lw = (norm_ffn1[i], w_ffn1_up[i], w_ffn1_down[i], norm_mix[i], w_in[i], conv_w[i], conv_b[i],
              dt_bias[i], a_log[i], d_skip[i], ssd_norm[i], hg_norm[i], w_out[i],
              norm_ffn2[i], w_ffn2_up[i], w_ffn2_down[i], norm_ple[i], w_ple_gate[i], w_ple_proj[i],
              ple_post_norm[i])
        lb = lb_all[i]
        xp, c_p, s_p, h_p = _layer(xp, p_prompt[i],
                                   jnp.zeros((bp, CONV_W - 1, CONV_DIM), f32),
                                   jnp.zeros((bp, SSD_HEADS, SSD_HEADDIM, SSD_STATE), f32),
                                   jnp.zeros((bp, HG_HEADS, HG_DK, HG_DV), f32), lb, lw)
        xs, c_s, s_s, h_s = _layer(xs, p_sample[i], state_conv[i], state_ssm[i], state_hgrn[i], lb, lw)
        conv_p.append(c_p.astype(state_conv.dtype))
        ssm_p.append(s_p.astype(state_ssm.dtype))
        hg_p.append(h_p.astype(state_hgrn.dtype))
        conv_s.append(c_s.astype(state_conv.dtype))
        ssm_s.append(s_s.astype(state_ssm.dtype))
        hg_s.append(h_s.astype(state_hgrn.dtype))
    y_prompt = _rmsnorm(xp, norm_final)
    y_sample = _rmsnorm(xs, norm_final)
    return (y_prompt, y_sample, jnp.stack(conv_p), jnp.stack(ssm_p), jnp.stack(hg_p),
            jnp.stack(conv_s), jnp.stack(ssm_s), jnp.stack(hg_s))
```

```python
from contextlib import ExitStack
import numpy as np
import concourse.bass as bass
import concourse.mybir as mybir
from concourse.bass_utils import run_bass_kernel_spmd

F32 = mybir.dt.float32
BF16 = mybir.dt.bfloat16
AF = mybir.ActivationFunctionType
ALU = mybir.AluOpType

ENGS = ("pe", "act", "dve", "pool", "sp")
NCORES = 8
EPS = 1e-6
NSIG = False
XF = (True, False, False)
PF = (False, False, False, False, False)


class Buf:
    def __init__(self, name, t):
        self.name = name
        self.t = t
        self.w = None
        self.readers = []
        self.cow = []

    def __getitem__(self, k):
        return self.t[k]


class DmaGroup:
    def __init__(self, sem, name):
        self.sem = sem
        self.count = 0
        self.name = name


class Node:
    __slots__ = ("eng", "fns", "preds", "succs", "idx", "cost", "group", "seq", "npred", "ready", "end", "xfer", "dval", "tbl", "prio")

    def __init__(self, eng, idx):
        self.eng = eng
        self.fns = []
        self.preds = {}
        self.succs = []
        self.idx = idx
        self.cost = 0.0
        self.group = None
        self.seq = 0
        self.npred = 0
        self.ready = 0.0
        self.end = 0.0
        self.xfer = 0.0
        self.dval = 0
        self.tbl = None


PRIO_CP = True
PRUNE = False
DRY = False
LAT = 110.0


class Prog:
    def __init__(self, nc, stack):
        self.nc = nc
        self.stack = stack
        self.nodes = []
        self.open_pe = None
        self.esem = {e: stack.enter_context(nc.semaphore("sem_" + e)) for e in ENGS}
        self.groups = []
        self.final_waits = []

    def sb(self, name, shape, dtype=F32):
        if DRY:
            return Buf(name, self.nc.dram_tensor("sb_" + name, list(shape), dtype).ap())
        t = self.stack.enter_context(self.nc.sbuf_tensor("sb_" + name, list(shape), dtype))
        return Buf(name, t)

    def ps(self, name, shape, dtype=F32):
        if DRY:
            return Buf(name, self.nc.dram_tensor("ps_" + name, list(shape), dtype).ap())
        t = self.stack.enter_context(self.nc.psum_tensor("ps_" + name, list(shape), dtype))
        return Buf(name, t)

    def sub(self, buf, n):
        return [Buf(f"{buf.name}_{i}", buf.t) for i in range(n)]

    def group(self, name):
        g = DmaGroup(self.stack.enter_context(self.nc.semaphore("dg_" + name)), name)
        self.groups.append(g)
        return g

    def _link(self, node, reads, writes, nowaw=False):
        def add(p, kind):
            if p is None or p is node:
                return
            node.preds[p] = node.preds.get(p, 0) | kind
        for b in reads:
            add(b.w, 1)
            for cw in b.cow:
                add(cw, 1)
        newcow = {}
        for b in writes:
            if nowaw and b.w is not None and b.w.group is not None and b.w.group is node.group:
                for p, kind in b.w.preds.items():
                    add(p, kind)
                newcow[id(b)] = b.cow + [b.w]
            else:
                add(b.w, 1)
                for cw in b.cow:
                    add(cw, 1)
                for r in b.readers:
                    add(r, 2)
                newcow[id(b)] = []
        for b in reads:
            if not b.readers or b.readers[-1] is not node:
                b.readers.append(node)
        for b in writes:
            b.cow = newcow[id(b)]
            b.w = node
            b.readers = []

    def op(self, eng, fn, reads=(), writes=(), sig=True, cost=100.0, tbl=None):
        if eng == "pe" and self.open_pe is not None:
            node = self.open_pe
        else:
            node = Node(eng, len(self.nodes))
            node.tbl = tbl
            self.nodes.append(node)
        node.fns.append(fn)
        node.cost += cost
        self._link(node, reads, writes)
        if eng == "pe":
            self.open_pe = None if sig else node

    def dma(self, eng, group, fn, reads=(), writes=(), nbytes=0, nowaw=False, after=()):
        node = Node(eng, len(self.nodes))
        self.nodes.append(node)
        node.fns.append(fn)
        node.group = group
        for b in after:
            for p in [b.w] + list(b.cow):
                if p is not None:
                    node.preds[p] = node.preds.get(p, 0) | 1
        node.cost = 1200.0 if eng == "pool" else 150.0
        node.xfer = 2000.0 + nbytes / 250.0
        self._link(node, reads, writes, nowaw=nowaw)

    def wait_all(self, eng, groups):
        self.final_waits.append((eng, groups))

    def schedule(self):
        assert self.open_pe is None
        nodes = self.nodes
        for n in nodes:
            n.npred = len(n.preds)
            for p in n.preds:
                p.succs.append(n)
        if PRIO_CP:
            cp = [0.0] * len(nodes)
            for n in reversed(nodes):
                m = 0.0
                for s_ in n.succs:
                    v = cp[s_.idx] + (0.0 if s_.eng == n.eng else LAT)
                    if v > m:
                        m = v
                cp[n.idx] = n.cost + n.xfer + m
            for n in nodes:
                n.prio = -cp[n.idx]
        else:
            for n in nodes:
                n.prio = float(n.idx)
        avail = {e: [] for e in ENGS}
        free = {e: 0.0 for e in ENGS}
        order = {e: [] for e in ENGS}
        for n in nodes:
            if n.npred == 0:
                avail[n.eng].append(n)
        left = len(nodes)
        seqlist = []
        self.seqlist = seqlist
        cur_tbl = None
        while left:
            best = None
            for e in ENGS:
                av = avail[e]
                if not av:
                    continue
                t = free[e]
                cand = None
                for n in av:
                    if n.ready <= t:
                        if e == "act":
                            key = (0 if (n.tbl is None or n.tbl == cur_tbl) else 1, n.prio, n.idx)
                        else:
                            key = (0, n.prio, n.idx)
                        if cand is None or cand[0] != 0 or key < cand[1]:
                            cand = (0, key, n)
                    elif cand is None or (cand[0] == 1 and (n.ready, n.idx) < cand[1]):
                        cand = (1, (n.ready, n.idx), n)
                n = cand[2]
                st = max(t, n.ready)
                if best is None or st < best[0] or (st == best[0] and n.idx < best[1].idx):
                    best = (st, n, e)
            st, n, e = best
            avail[e].remove(n)
            c = n.cost
            if e == "act" and n.tbl is not None:
                if cur_tbl is not None and n.tbl != cur_tbl:
                    c += 1300.0
                cur_tbl = n.tbl
            free[e] = st + c
            n.end = st + c + n.xfer
            order[e].append(n)
            seqlist.append(n)
            left -= 1
            for s_ in n.succs:
                r = n.end + (0.0 if s_.eng == n.eng else LAT)
                if r > s_.ready:
                    s_.ready = r
                s_.npred -= 1
                if s_.npred == 0:
                    avail[s_.eng].append(s_)
        self.order = order
        self.makespan = max(free.values())
        return order

    def emit(self):
        order = self.schedule()
        if DRY:
            return
        for e in ENGS:
            c = 0
            for n in order[e]:
                if n.group is None:
                    c += 1
                    n.seq = c
                else:
                    n.group.count += 16
                    n.dval = n.group.count
        nc = self.nc
        esem = self.esem
        Klast = {e: {} for e in ENGS}
        Kdone = {}
        waits_of = {}
        self.n_waits = 0
        for n in self.seqlist:
            e = n.eng
            base = Klast[e]
            need = {}
            for p, kind in n.preds.items():
                if p.group is not None:
                    key, val = p.group, p.dval
                else:
                    if p.eng == e and e == "pe":
                        continue
                    key, val = p.eng, p.seq
                if key not in need or need[key][0] < val:
                    need[key] = (val, p)
            waits = []
            for key, (val, p) in sorted(need.items(), key=lambda kv: -kv[1][1].end):
                if base.get(key, 0) >= val:
                    continue
                waits.append((esem[key] if isinstance(key, str) else key.sem, val))
                if PRUNE:
                    for k2, v2 in Kdone[p].items():
                        if base.get(k2, 0) < v2:
                            base[k2] = v2
                if base.get(key, 0) < val:
                    base[key] = val
            waits_of[n] = waits
            self.n_waits += len(waits)
            kd = dict(base)
            if n.group is None:
                if kd.get(e, 0) < n.seq:
                    kd[e] = n.seq
            else:
                kd[n.group] = max(kd.get(n.group, 0), n.dval)
            Kdone[n] = kd
        streams = {e: [(waits_of[n], n) for n in order[e]] for e in ENGS}
        finals = {e: [] for e in ENGS}
        for e, groups in self.final_waits:
            finals[e] += [(g.sem, g.count) for g in groups if g.count > 0]

        with nc.Block() as block:
            def run(engname):
                def body(eh):
                    for waits, n in streams[engname]:
                        for sem, val in waits:
                            eh.wait_ge(sem, val)
                        ins = None
                        for fn in n.fns:
                            ins = fn(eh)
                        if n.group is None:
                            ins.then_inc(esem[engname], 1)
                        else:
                            ins.then_inc(n.group.sem, 16)
                    for sem, val in finals[engname]:
                        eh.wait_ge(sem, val)
                return body
            block.tensor(run("pe"))
            block.scalar(run("act"))
            block.vector(run("dve"))
            block.gpsimd(run("pool"))
            block.sync(run("sp"))


C_ID, C_L, C_NEGM, C_MASK2, C_ONES, C_LS, C_NEGMS, C_ONES_S, C_SEG, C_SCAN = 0, 128, 256, 384, 512, 640, 704, 768, 832, 848
NCST = 848 + 576
NCSTB = 848
P_NF1, P_NMIX, P_NF2, P_NPLE, P_PPOST, P_NFIN, P_SSDN, P_HGN, P_CW, P_CB, P_LB0, P_LB1, P_DSK = 0, 8, 16, 24, 32, 40, 48, 56, 64, 112, 124, 132, 140
NPAR = 148

WI_Z, WI_XS, WI_B, WI_C, WI_DT, WI_Q, WI_FR, WI_IV, WI_OG = 0, 1024, 2048, 2304, 2560, 2576, 3600, 4624, 5648


def make_consts():
    c = np.zeros((128, NCST), np.float32)
    ii = np.arange(128)
    p, j = ii[:, None], ii[None, :]
    c[:, C_ID:C_ID + 128] = (p == j)
    c[:, C_L:C_L + 128] = (p <= j)
    c[:, C_NEGM:C_NEGM + 128] = np.where(p > j, -30000.0, 0.0)
    c[:, C_MASK2:C_MASK2 + 128] = (p <= j) & (p // 64 == j // 64)
    c[:, C_ONES:C_ONES + 128] = 1.0
    j64 = np.arange(64)[None, :]
    same = (p // 4 == j64 // 4) & (p < 64)
    c[:, C_LS:C_LS + 64] = same & (p <= j64)
    c[:, C_NEGMS:C_NEGMS + 64] = np.where(same & (p <= j64), 0.0, -30000.0)
    c[:, C_ONES_S:C_ONES_S + 64] = same
    c[:, C_SEG:C_SEG + 16] = (p // 4 == np.arange(16)[None, :]) & (p < 64)
    col = np.arange(576)
    sm = np.where(col < 512, (col % 64) != 0, ((col - 512) % 4) != 0).astype(np.float32)
    c[:, C_SCAN:C_SCAN + 576] = sm[None, :]
    return c


def fm(v):
    v = np.asarray(v, np.float32).reshape(-1, 128)
    return np.ascontiguousarray(v.T)


def _nfree(ap):
    n = 1
    for d in ap.shape[1:]:
        n *= d
    return n


def _nbytes(ap):
    n = 4
    for d in ap.shape:
        n *= d
    return n


_TBL = {AF.Exp: "el", AF.Ln: "el", AF.Silu: "silu", AF.Sigmoid: "sigm", AF.Sqrt: "sqrt"}


class K:
    def __init__(self, nc, st):
        self.nc = nc
        self.P = Prog(nc, st)
        self.mm_i = 0
        self.aux_i = 0

    def mm(self, out, lhsT, rhs, reads, writes, start=True, stop=True, sig=True):
        n = _nfree(rhs)
        c = (4.0 if rhs.dtype == F32 else 1.0) * max(n, 64) / 2.2 + (50.0 if n <= 128 else 12.0)
        self.P.op("pe", lambda e: e.matmul(out, lhsT=lhsT, rhs=rhs, start=start, stop=stop),
                  reads=reads, writes=writes, sig=sig, cost=c)

    def tr(self, out, in_, ident, reads, writes, sig=True):
        self.P.op("pe", lambda e: e.transpose(out, in_, ident), reads=reads, writes=writes, sig=sig, cost=(213.0 if in_.dtype == F32 else 107.0))

    def act(self, out, in_, func, reads, writes, bias=None, scale=1.0):
        c = 265.0 + _nfree(in_) * 0.52
        tbl = _TBL.get(func)
        if bias is None:
            self.P.op("act", lambda e: e.activation(out=out, in_=in_, func=func, scale=scale), reads=reads, writes=writes, cost=c, tbl=tbl)
        else:
            self.P.op("act", lambda e: e.activation(out=out, in_=in_, func=func, bias=bias, scale=scale), reads=reads, writes=writes, cost=c, tbl=tbl)

    def acopy(self, out, in_, reads, writes):
        self.P.op("act", lambda e: e.copy(out=out, in_=in_), reads=reads, writes=writes, cost=265.0 + _nfree(in_) * 0.52)

    def vcopy(self, out, in_, reads, writes):
        self.P.op("dve", lambda e: e.tensor_copy(out=out, in_=in_), reads=reads, writes=writes, cost=190.0 + _nfree(in_) * 0.8)

    def tt(self, out, in0, in1, op, reads, writes, eng="dve"):
        c = 190.0 + _nfree(in0) * 0.8 if eng == "dve" else 180.0 + _nfree(in0) * 2.05
        self.P.op(eng, lambda e: e.tensor_tensor(out=out, in0=in0, in1=in1, op=op), reads=reads, writes=writes, cost=c)

    def ts(self, out, in0, s1, op0, reads, writes, s2=None, op1=None, eng="dve"):
        c = 190.0 + _nfree(in0) * 0.8 if eng == "dve" else 180.0 + _nfree(in0) * 1.1
        if op1 is None:
            self.P.op(eng, lambda e: e.tensor_scalar(out=out, in0=in0, scalar1=s1, scalar2=None, op0=op0), reads=reads, writes=writes, cost=c)
        else:
            self.P.op(eng, lambda e: e.tensor_scalar(out=out, in0=in0, scalar1=s1, scalar2=s2, op0=op0, op1=op1), reads=reads, writes=writes, cost=c)

    def stt(self, out, in0, scalar, in1, op0, op1, reads, writes):
        self.P.op("dve", lambda e: e.scalar_tensor_tensor(out=out, in0=in0, scalar=scalar, in1=in1, op0=op0, op1=op1), reads=reads, writes=writes,
                  cost=190.0 + _nfree(in0) * 1.07)

    def recip(self, out, in_, reads, writes):
        self.P.op("dve", lambda e: e.reciprocal(out=out, in_=in_), reads=reads, writes=writes, cost=190.0 + _nfree(in_) * 0.8)

    def memset(self, ap, val, writes):
        self.P.op("dve", lambda e: e.memset(ap, val), writes=writes, cost=190.0 + _nfree(ap) * 0.8)

    def load(self, group, out, in_, writes, eng="sp"):
        self.P.dma(eng, group, lambda e: e.dma_start(out=out, in_=in_), writes=writes, nbytes=_nbytes(out))

    def store(self, group, out, in_, reads):
        self.P.dma("sp", group, lambda e: e.dma_start(out=out, in_=in_), reads=reads, nbytes=_nbytes(in_))

    def _alloc(self, pool, ptr):
        n = len(pool)
        for d in range(n):
            b = pool[(ptr + d) % n]
            if not getattr(b, "live", False):
                b.live = True
                return b, (ptr + d + 1) % n
        raise RuntimeError("psum pool exhausted")

    def pmm(self):
        b, self.mm_i = self._alloc(self.mmps, self.mm_i)
        return b

    def paux(self):
        b, self.aux_i = self._alloc(self.auxps, self.aux_i)
        return b

    def free(self, *bs):
        for b in bs:
            assert b.live
            b.live = False

    def wsetup(self, nslot):
        self.NSLOT = nslot
        self.wslots = [self.P.sb(f"wslot{i}", [128, 4096], BF16) for i in range(nslot)]
        self.wgroups = [self.P.group(f"w{i}") for i in range(nslot)]
        self.wblocks = []
        self.w_issued = 0
        self.w_used = 0

    def wview(self, slot, KC, ncols):
        return slot.t[:, 0:KC * ncols].rearrange("p (k n) -> p k n", k=KC)

    def wissue(self, upto):
        while self.w_issued < min(upto, len(self.wblocks)):
            i = self.w_issued
            KC, ncols, parts = self.wblocks[i]
            slot = self.wslots[i % self.NSLOT]
            g = self.wgroups[i % self.NSLOT]
            v = self.wview(slot, KC, ncols)
            extra = [self.wslots[(i - 2) % self.NSLOT]] if (len(parts) >= 4 and i >= 2) else []
            for pi, (d0, n, src) in enumerate(parts):
                self.P.dma("pool", g,
                           (lambda dst, s: (lambda e: e.dma_start(out=dst, in_=s)))(
                               v[:, :, d0:d0 + n], src.rearrange("(k p) n -> p k n", p=128)),
                           after=(extra if pi == 0 else []), writes=[slot], nbytes=KC * 128 * n * 4, nowaw=(pi > 0))
            self.w_issued += 1

    def wnext(self):
        i = self.w_used
        self.wissue(i + self.NSLOT)
        KC, ncols, parts = self.wblocks[i]
        slot = self.wslots[i % self.NSLOT]
        self.w_used += 1
        return slot, self.wview(slot, KC, ncols)


def build_program():
    nc = bass.Bass("TRN2", target_bir_lowering=False)
    din = lambda name, shape: nc.dram_tensor(name, list(shape), F32, kind="ExternalInput").ap()
    dout = lambda name, shape: nc.dram_tensor(name, list(shape), F32, kind="ExternalOutput").ap()
    xp = din("xp", [2048, 1024]); xsm = din("xsm", [64, 1024]); cv0 = din("cv0", [48, 1536])
    ssm0 = din("ssm0", [16, 1024, 128]); hg0 = din("hg0", [16, 1024, 128])
    pp = din("pp", [2048, 256]); psm = din("psm", [64, 256])
    wf1u = din("wf1u", [1024, 5632]); wf1d = din("wf1d", [2816, 1024]); win = din("win", [1024, 6672])
    wout = din("wout", [2048, 1024]); wf2u = din("wf2u", [1024, 5632]); wf2d = din("wf2d", [2816, 1024])
    wpg = din("wpg", [1024, 1024]); wpp = din("wpp", [256, 1024])
    cstd = din("cst", [128, NCST]); pard = din("par", [128, NPAR]); bcrd = din("bcr", [128, 32])
    yp = dout("yp", [2048, 1024]); ysm = dout("ysm", [64, 1024]); cvp = dout("cvp", [3, 1536])
    ssmp = dout("ssmp", [1024, 128]); hgp = dout("hgp", [1024, 128]); cvs = dout("cvs", [48, 1536])
    ssms = dout("ssms", [16, 1024, 128]); hgs = dout("hgs", [16, 1024, 128])

    with ExitStack() as st:
        k = K(nc, st)
        P = k.P
        WM = 576
        xT = P.sb("xT", [128, 8, WM]); xn = P.sb("xn", [128, 8, WM], BF16)
        BIGB = P.sb("BIGB", [128, 22, WM], BF16)
        FA = P.sb("FA", [128, 8, WM]); FB = P.sb("FB", [128, 4, 640])
        cst = P.sb("cst", [128, 128]); cstb = P.sb("cstb", [128, NCST], BF16)
        par = P.sb("par", [128, NPAR]); bcr = P.sb("bcr", [128, 32])
        lbt = P.sb("lbt", [128, 8]); oml = P.sb("oml", [128, 8]); noml = P.sb("noml", [128, 8]); abc = P.sb("abc", [128, 16])
        xin = [P.sb(f"xin{i}", [128, 1024]) for i in range(2)]
        pin = [P.sb("pin0", [128, 256])]
        yout = [P.sb("yout0", [128, 1024]), P.sb("yout1", [128, 1024])]
        pT = P.sb("pT", [128, 2, WM], BF16)
        S_ssm = P.sb("S_ssm", [128, 8, 128]); S_hg = P.sb("S_hg", [128, 8, 128])
        hTb = P.sb("hTb", [128, 8, 128], BF16); Shb2 = P.sb("Shb2", [128, 2, 2, 128], BF16)
        hist = P.sb("hist", [128, 12, 3])
        cvi = P.sb("cvi", [128, 128]); cvt = P.sb("cvt", [128, 48]); cvo = P.sb("cvo", [128, 128])
        XS = P.sb("XS", [128, 4096])
        sst_l = [Buf("sstA", XS.t[:, 0:1024].rearrange("p (b n) -> p b n", b=8)), Buf("sstB", XS.t[:, 1024:2048].rearrange("p (b n) -> p b n", b=8))]
        sstb = Buf("sstb", XS.t[:, 2048:3072].bitcast(BF16).rearrange("p (b n) -> p b n", b=16))
        xT_A = xT; xT_B = Buf("xTB", XS.t[:, :].rearrange("p (k w) -> p k w", k=8))
        vtok_l = [P.sb(f"vtok{i}", [128, 5, 128], BF16) for i in range(2)]
        dtt = P.sb("dtt", [128, 5, 16]); dta = P.sb("dta", [128, 5, 16]); nacs = P.sb("nacs", [128, 5, 16])
        dtd = P.sb("dtd", [128, 5, 16]); cdp = P.sb("cdp", [128, 4, 8]); cds = P.sb("cds", [128, 8, 16])
        t16 = P.sb("t16", [128, 16]); t16b = P.sb("t16b", [128, 16]); dtab = P.sb("dtab", [128, 5, 2, 16], BF16)
        BCT = P.sb("BCT", [128, 4, WM], BF16)
        Btok = P.sb("Btok", [128, 5, 2, 128], BF16); cbs = P.sb("cbs", [128, 5, 2, 128])
        EXPB = Buf("EXPB", XS.t[:, 3072:4096].bitcast(BF16))
        xdt_l = [P.sb(f"xdt{i}", [128, 512], BF16) for i in range(2)]; xdtd_l = [P.sb(f"xdtd{i}", [128, 512], BF16) for i in range(2)]
        Eh = [P.sb(f"Eh{i}", [128, 2, 128]) for i in range(2)]
        Mh = [P.sb(f"Mh{i}", [128, 2, 128], BF16) for i in range(2)]
        Ein_l = [P.sb(f"Ein{i}", [128, 128]) for i in range(2)]; yis_l = [P.sb(f"yis{i}", [128, 128]) for i in range(2)]
        y1_l = [P.sb(f"y1{i}", [128, 128]) for i in range(2)]
        SCR = P.sb("SCR", [128, 1024])
        acsTb_l = [P.sb(f"acsTb{i}", [128, 2, 128], BF16) for i in range(2)]; totb = P.sb("totb", [128, 2, 16], BF16)
        rs = P.sb("rs", [128, WM]); rs2 = P.sb("rs2", [128, WM])
        qe_l = [P.sb(f"qe{i}", [128, WM], BF16) for i in range(2)]; ke_l = [P.sb(f"ke{i}", [128, WM], BF16) for i in range(2)]
        kdT_l = [P.sb(f"kdT{i}", [128, WM], BF16) for i in range(2)]
        attm_l = [P.sb(f"attm{i}", [128, 128], BF16) for i in range(2)]; kdtok_l = [P.sb(f"kdtok{i}", [128, 128], BF16) for i in range(2)]
        dch_l = [P.sb(f"dch{i}", [128, 24]) for i in range(2)]
        xTc = P.sub(xT, 8); xTc_A = xTc; xTc_B = P.sub(xT_B, 8); guard = {"b": []}; FAc = P.sub(FA, 8); FBc = P.sub(FB, 4); BBc = P.sub(BIGB, 22); SCRh = P.sub(SCR, 2)
        ctr = {"xd": 0, "at": 0}
        S_hgc = P.sub(S_hg, 8); Shq = [P.sub(Shb2, 2), P.sub(Shb2, 2)]; S_ssmc = P.sub(S_ssm, 8); hTbc = P.sub(hTb, 8)
        k.mmps = [P.ps(f"pmm{i}", [128, 512]) for i in range(7)]
        k.auxps = k.mmps
        pb = P.ps("pauxb", [128, 1024], BF16)
        k.wsetup(4)

        g_c = [P.group("c0"), P.group("c1"), P.group("c2"), P.group("c3")]; g_x = [P.group("xin0"), P.group("xin1")]; g_p = [P.group("pin0"), P.group("pin1")]
        g_y = [P.group("yout0"), P.group("yout1")]; g_cvi = P.group("cvi"); g_cvo = P.group("cvo"); g_ss_l = [P.group("sstA"), P.group("sstB")]; g_yx = [P.group("yx0"), P.group("yx1")]
        g_so = P.group("stout")
        out_groups = g_y + g_yx + g_ss_l + [g_cvo, g_so]

        def ffn_blocks(wu, wd):
            bl = []
            for b in range(11):
                bl.append((8, 512, [(0, 256, wu[:, b * 256:(b + 1) * 256]), (256, 256, wu[:, 2816 + b * 256:2816 + (b + 1) * 256])]))
            for fo in range(8):
                bl.append((22, 128, [(0, 128, wd[:, fo * 128:(fo + 1) * 128])]))
            return bl

        def tile_blocks():
            bl = ffn_blocks(wf1u, wf1d)
            bl.append((8, 16, [(0, 16, win[:, WI_DT:WI_DT + 16])]))
            bl.append((8, 512, [(0, 512, win[:, WI_B:WI_B + 512])]))
            for kind, idx in MIX_ORDER:
                if kind == "gp":
                    g = idx
                    bl.append((8, 512, [(0, 512, win[:, WI_Z + g * 512:WI_Z + (g + 1) * 512])]))
                    bl.append((8, 512, [(0, 512, win[:, WI_XS + g * 512:WI_XS + (g + 1) * 512])]))
                elif kind == "h":
                    h = idx
                    bl.append((8, 512, [(0, 128, win[:, WI_Q + h * 128:WI_Q + (h + 1) * 128]),
                                        (128, 128, win[:, WI_FR + h * 128:WI_FR + (h + 1) * 128]),
                                        (256, 128, win[:, WI_OG + h * 128:WI_OG + (h + 1) * 128]),
                                        (384, 128, win[:, WI_IV + h * 128:WI_IV + (h + 1) * 128])]))
            for b in range(4):
                bl.append((16, 256, [(0, 256, wout[:, b * 256:(b + 1) * 256])]))
            bl += ffn_blocks(wf2u, wf2d)
            bl.append((2, 1024, [(0, 1024, wpp[:, :])]))
            for b in range(2):
                bl.append((8, 512, [(0, 512, wpg[:, b * 512:(b + 1) * 512])]))
            return bl

        NT = 4
        POOL_FK, POOL_M, POOL_YG, POOL_TAP0, POOL_QE = PF
        ACT_TAP0, DVE_KD, DVE_O = XF
        NATIVE_SIG = NSIG
        MIX_ORDER = [("gp", 0), ("gb", 0), ("gp", 1), ("gb", 1)] + [("h", h) for h in range(8)]
        for _ in range(NT):
            k.wblocks += tile_blocks()

        k.load(g_c[0], cst[:], cstd[:, C_ID:C_ID + 128], [cst]); k.load(g_c[3], cstb[:], cstd, [cstb], eng="pool"); k.load(g_c[1], par[:], pard, [par]); k.load(g_c[2], bcr[:], bcrd, [bcr])
        k.wissue(k.NSLOT)
        k.tt(lbt[:], par[:, P_LB0:P_LB0 + 8], par[:, P_LB1:P_LB1 + 8], ALU.subtract, [par], [lbt])
        k.act(lbt[:], lbt[:], AF.Sigmoid, [lbt], [lbt])
        k.ts(oml[:], lbt[:], -1.0, ALU.mult, [lbt], [oml], s2=1.0, op1=ALU.add)
        k.ts(noml[:], oml[:], -1.0, ALU.mult, [oml], [noml])
        k.act(abc[:], bcr[:, 16:32], AF.Exp, [bcr], [abc])
        k.ts(abc[:], abc[:], -1.0, ALU.mult, [abc], [abc])
        k.memset(hist[:], 0.0, [hist]); k.memset(S_ssm[:], 0.0, S_ssmc); k.memset(S_hg[:], 0.0, S_hgc)

        SelL = SCR[:, :].bitcast(BF16)
        k.vcopy(SelL[0:16, :].rearrange("p (h s) -> p h s", s=128), cstb[0:16, C_ID:C_ID + 16].unsqueeze(2).to_broadcast([16, 16, 128]), [cstb], SCRh)
        ident = cst[:, 0:128]
        identb = cstb[:, C_ID:C_ID + 128]
        onesb = cstb[:, C_ONES:C_ONES + 128]

        def colranges(W):
            return [(0, 512)] if W == 512 else [(0, 288), (288, W)]

        def linear(wv, slot, wc0, rhs, rhs_buf, KC, c0, c1):
            ps = k.pmm()
            for kc in range(KC):
                rb = rhs_buf[kc] if isinstance(rhs_buf, list) else rhs_buf
                k.mm(ps[:, 0:c1 - c0], wv[:, kc, wc0:wc0 + 128], rhs[:, kc, c0:c1], [slot, rb], [ps],
                     start=(kc == 0), stop=(kc == KC - 1), sig=(kc == KC - 1))
            return ps

        def rmsnorm_stats(src, srcb, nchunk, W, sqdst_c0, inv_n, out_rs):
            for i in range(nchunk):
                k.act(BIGB[:, sqdst_c0 + i, 0:W], src[:, i, 0:W], AF.Square, [srcb[i]], [BBc[sqdst_c0 + i]])
            for (c0, c1) in colranges(W):
                ps = k.pmm()
                for i in range(nchunk):
                    k.mm(ps[:, 0:c1 - c0], onesb, BIGB[:, sqdst_c0 + i, c0:c1], [cstb, BBc[sqdst_c0 + i]], [ps],
                         start=(i == 0), stop=(i == nchunk - 1), sig=(i == nchunk - 1))
                k.act(out_rs[:, c0:c1], ps[:, 0:c1 - c0], AF.Ln, [ps], [out_rs], bias=EPS, scale=inv_n)
                k.free(ps)
            k.act(out_rs[:, 0:W], out_rs[:, 0:W], AF.Exp, [out_rs], [out_rs], scale=-0.5)

        def norm_to_xn(pcol, W):
            rmsnorm_stats(xT, xTc, 8, W, 0, 1.0 / 1024.0, rs)
            for i in range(8):
                k.stt(xn[:, i, 0:W], xT[:, i, 0:W], par[:, pcol + i:pcol + i + 1], rs[:, 0:W], ALU.mult, ALU.mult,
                      [xTc[i], par, rs], [xn])

        def ffn(pcol, W):
            norm_to_xn(pcol, W)
            hid = BIGB
            tgl = 0
            for b in range(11):
                slot, wv = k.wnext()
                for jj in range(2):
                    j = 2 * b + jj
                    for (c0, c1) in colranges(W):
                        n = c1 - c0
                        pg = linear(wv, slot, jj * 128, xn, xn, 8, c0, c1)
                        pu = linear(wv, slot, 256 + jj * 128, xn, xn, 8, c0, c1)
                        sg = FA[:, tgl, 0:n]; sgb = FAc[tgl]; tgl ^= 1
                        k.act(sg, pg[:, 0:n], AF.Silu, [pg], [sgb])
                        k.tt(hid[:, j, c0:c1], sg, pu[:, 0:n], ALU.mult, [sgb, pu], [BBc[j]])
                        k.free(pg, pu)
            for fo in range(8):
                slot, wv = k.wnext()
                for (c0, c1) in colranges(W):
                    n = c1 - c0
                    ps = linear(wv, slot, 0, hid, BBc, 22, c0, c1)
                    k.stt(xT[:, fo, c0:c1], ps[:, 0:n], 0.5, xT[:, fo, c0:c1], ALU.mult, ALU.add, [ps, xTc[fo]], [xTc[fo]])
                    k.free(ps)

        def load_tile(ti, W):
            blocks = [(128, i * 128, xp[ti * 512 + i * 128: ti * 512 + (i + 1) * 128, :], pp[ti * 512 + i * 128: ti * 512 + (i + 1) * 128, :]) for i in range(4)]
            if W > 512:
                blocks.append((64, 512, xsm, psm))
            for bi, (R, c0, xsrc, psrc) in enumerate(blocks):
                xb = xin[bi % 2]; pbuf = pin[0]
                k.load(g_x[bi % 2], xb[0:R, :], xsrc, [xb])
                k.load(g_p[0], pbuf[0:R, :], psrc, [pbuf])
                for half in range(2):
                    ps = k.paux()
                    for q in range(4):
                        kc = half * 4 + q
                        k.tr(ps[:, q * 128:q * 128 + R], xb[0:R, kc * 128:(kc + 1) * 128], ident[0:R, 0:R], [xb, cst], [ps], sig=(q == 3))
                    k.acopy(xT[:, half * 4:half * 4 + 4, c0:c0 + R],
                            ps[:, :].rearrange("p (q r) -> p q r", q=4)[:, :, 0:R], [ps], xTc[half * 4:half * 4 + 4] + guard["b"])
                    k.free(ps)
                ps = k.paux()
                for q in range(2):
                    k.tr(ps[:, q * 128:q * 128 + R], pbuf[0:R, q * 128:(q + 1) * 128], ident[0:R, 0:R], [pbuf, cst], [ps], sig=(q == 1))
                k.vcopy(pT[:, :, c0:c0 + R], ps[:, 0:256].rearrange("p (q r) -> p q r", q=2)[:, :, 0:R], [ps], [pT])
                k.free(ps)

        def mixer(ti, W):
            has_s = W > 512
            first = (ti == 0)
            last = (ti == NT - 1)
            blocks = [(128, i * 128, i) for i in range(4)]
            if has_s:
                blocks.append((64, 512, 4))
            crs = colranges(W)
            mixed = BIGB
            norm_to_xn(P_NMIX, W)

            slot, wv = k.wnext()
            for (R, c0, bi) in blocks:
                smp = (bi == 4)
                ps = k.paux()
                for kc in range(8):
                    k.mm(ps[0:R, 0:16], xn[:, kc, c0:c0 + R], wv[:, kc, 0:16], [xn, slot], [ps], start=(kc == 0), stop=(kc == 7), sig=(kc == 7))
                k.tt(t16[0:R, :], ps[0:R, 0:16], bcr[0:R, 0:16], ALU.add, [ps, bcr], [t16])
                k.free(ps)
                k.act(t16[0:R, :], t16[0:R, :], AF.Exp, [t16], [t16])
                k.act(dtt[0:R, bi, :], t16[0:R, :], AF.Ln, [t16], [dtt], bias=1.0)
                k.tt(dta[0:R, bi, :], dtt[0:R, bi, :], abc[0:R, :], ALU.mult, [dtt, abc], [dta])
                k.vcopy(dtab[0:R, bi, 0, :], dta[0:R, bi, :], [dta], [dtab])
                k.tt(dtab[0:R, bi, 1, :], dta[0:R, bi, :], dtab[0:R, bi, 0, :], ALU.subtract, [dta, dtab], [dtab])
                Lm = cstb[0:R, C_LS:C_LS + R] if smp else cstb[0:R, C_L:C_L + R]
                On = cstb[0:R, C_ONES_S:C_ONES_S + R] if smp else cstb[0:R, C_ONES:C_ONES + R]
                ps2 = k.paux()
                k.mm(ps2[0:R, 0:16], Lm, dtab[0:R, bi, 0, :], [cstb, dtab], [ps2], start=True, stop=False, sig=False)
                k.mm(ps2[0:R, 0:16], Lm, dtab[0:R, bi, 1, :], [cstb, dtab], [ps2], start=False, stop=True, sig=False)
                k.mm(ps2[0:R, 16:32], On, dtab[0:R, bi, 0, :], [cstb, dtab], [ps2], start=True, stop=False, sig=False)
                k.mm(ps2[0:R, 16:32], On, dtab[0:R, bi, 1, :], [cstb, dtab], [ps2], start=False, stop=True)
                k.ts(nacs[0:R, bi, :], ps2[0:R, 0:16], -1.0, ALU.mult, [ps2], [nacs])
                k.tt(t16b[0:R, :], ps2[0:R, 16:32], nacs[0:R, bi, :], ALU.add, [ps2, nacs], [t16b])
                k.free(ps2)
                k.act(t16b[0:R, :], t16b[0:R, :], AF.Exp, [t16b], [t16b])
                k.tt(dtd[0:R, bi, :], dtt[0:R, bi, :], t16b[0:R, :], ALU.mult, [dtt, t16b], [dtd])
                ncol = 16 if smp else 2
                rhs_t = cstb[0:R, C_SEG:C_SEG + 16] if smp else cstb[0:R, C_ONES:C_ONES + 2]
                ptot = k.paux()
                k.mm(ptot[0:16, 0:ncol], dtab[0:R, bi, 0, :], rhs_t, [dtab, cstb], [ptot], start=True, stop=False, sig=False)
                k.mm(ptot[0:16, 0:ncol], dtab[0:R, bi, 1, :], rhs_t, [dtab, cstb], [ptot], start=False, stop=True)
                k.vcopy(totb[0:16, 0, 0:ncol], ptot[0:16, 0:ncol], [ptot], [totb])
                k.tt(totb[0:16, 1, 0:ncol], ptot[0:16, 0:ncol], totb[0:16, 0, 0:ncol], ALU.subtract, [ptot, totb], [totb])
                k.free(ptot)
                ps3 = k.paux()
                for h in range(16):
                    for part in range(2):
                        k.mm(ps3[(h % 2) * 64:(h % 2) * 64 + 64, (h // 2) * ncol:(h // 2) * ncol + ncol], SelL[0:16, h * 128:h * 128 + 64], totb[0:16, part, 0:ncol],
                             SCRh + [totb], [ps3], start=(part == 0), stop=(part == 1), sig=(h == 15 and part == 1))
                if not smp:
                    k.act(cdp[:, bi, :].unsqueeze(2), ps3[:, 0:16].rearrange("p (a b) -> p a b", b=2)[:, :, 0:1], AF.Exp, [ps3], [cdp])
                else:
                    k.act(cds[:, :, :], ps3[:, 0:128].rearrange("p (a b) -> p a b", b=16), AF.Exp, [ps3], [cds])
                k.free(ps3)

            def raw_sample_view(i):
                return FB[:, i, 515:627].rearrange("p (b j) -> p b j", j=7)

            def conv_stage(wv, slot, cis, dst, dstbufs, dst_c0):
                for i in range(4):
                    ci = cis[i]
                    fbb = FBc[i]; dstb = dstbufs[dst_c0 + i]
                    if has_s:
                        k.load(g_cvi, cvi[0:48, :], cv0[:, ci * 128:(ci + 1) * 128], [cvi])
                    for (c0, c1) in crs:
                        ps = linear(wv, slot, i * 128, xn, xn, 8, c0, c1)
                        p1 = min(c1, 512)
                        if p1 > c0:
                            k.acopy(FB[:, i, 3 + c0:3 + p1], ps[:, 0:p1 - c0], [ps], [fbb])
                        if c1 > 512:
                            k.acopy(raw_sample_view(i)[:, :, 3:7], ps[:, 512 - c0:576 - c0].rearrange("p (b j) -> p b j", j=4), [ps], [fbb])
                        k.free(ps)
                    k.vcopy(FB[:, i, 0:3], hist[:, ci, :], [hist], [fbb])
                    k.vcopy(hist[:, ci, :], FB[:, i, 512:515], [fbb], [hist])
                    if last:
                        pst = k.paux()
                        k.tr(pst[0:3, 0:128], hist[:, ci, :], ident, [hist, cst], [pst])
                        k.acopy(cvo[0:3, :], pst[0:3, 0:128], [pst], [cvo])
                        k.free(pst)
                        k.store(g_cvo, cvp[:, ci * 128:(ci + 1) * 128], cvo[0:3, :], [cvo])
                    if has_s:
                        pst = k.paux()
                        k.tr(pst[:, 0:48], cvi[0:48, :], ident[0:48, 0:48], [cvi, cst], [pst])
                        k.vcopy(raw_sample_view(i)[:, :, 0:3], pst[:, 0:48].rearrange("p (b j) -> p b j", j=3), [pst], [fbb])
                        k.free(pst)
                        k.vcopy(cvt[:, :].rearrange("p (b j) -> p b j", j=3), raw_sample_view(i)[:, :, 4:7], [fbb], [cvt])
                        pst = k.paux()
                        k.tr(pst[0:48, 0:128], cvt[:, :], ident, [cvt, cst], [pst])
                        k.acopy(cvo[0:48, :], pst[0:48, 0:128], [pst], [cvo])
                        k.free(pst)
                        k.store(g_cvo, cvs[:, ci * 128:(ci + 1) * 128], cvo[0:48, :], [cvo])
                    d = dst[:, dst_c0 + i, 0:512]
                    if ACT_TAP0:
                        k.act(d, FB[:, i, 0:512], AF.Identity, [fbb, par], [dstb], bias=par[:, P_CB + ci:P_CB + ci + 1], scale=par[:, P_CW + ci:P_CW + ci + 1])
                    else:
                        k.ts(d, FB[:, i, 0:512], par[:, P_CW + ci:P_CW + ci + 1], ALU.mult, [fbb, par], [dstb],
                             s2=par[:, P_CB + ci:P_CB + ci + 1], op1=ALU.add, eng=("pool" if POOL_TAP0 else "dve"))
                    for j in range(1, 4):
                        k.stt(d, FB[:, i, j:j + 512], par[:, P_CW + j * 12 + ci:P_CW + j * 12 + ci + 1], d, ALU.mult, ALU.add,
                              [fbb, par, dstb], [dstb])
                    if has_s:
                        d = dst[:, dst_c0 + i, 512:576].rearrange("p (b j) -> p b j", j=4)
                        rv = raw_sample_view(i)
                        k.ts(d, rv[:, :, 0:4], par[:, P_CW + ci:P_CW + ci + 1], ALU.mult, [fbb, par], [dstb],
                             s2=par[:, P_CB + ci:P_CB + ci + 1], op1=ALU.add)
                        for j in range(1, 4):
                            k.stt(d, rv[:, :, j:j + 4], par[:, P_CW + j * 12 + ci:P_CW + j * 12 + ci + 1], d, ALU.mult, ALU.add,
                                  [fbb, par, dstb], [dstb])

            slot, wv = k.wnext()
            conv_stage(wv, slot, [8, 9, 10, 11], FA, FAc, 0)
            for i in range(4):
                k.act(BCT[:, i, 0:W], FA[:, i, 0:W], AF.Silu, [FAc[i]], [BCT])
            for (R, c0, bi) in blocks:
                for g in range(2):
                    k.tr(pb[0:R, g * 128:(g + 1) * 128], BCT[:, g, c0:c0 + R], identb, [BCT, cstb], [pb], sig=(g == 1))
                k.acopy(Btok[0:R, bi, :, :], pb[0:R, 0:256].rearrange("p (g n) -> p g n", g=2), [pb], [Btok])
                ps = k.paux()
                for g in range(2):
                    k.mm(ps[0:R, g * 128:g * 128 + R], BCT[:, g, c0:c0 + R], BCT[:, 2 + g, c0:c0 + R], [BCT], [ps], sig=(g == 1))
                k.acopy(cbs[0:R, bi, :, 0:R], ps[0:R, 0:256].rearrange("p (g n) -> p g n", g=2)[:, :, 0:R], [ps], [cbs])
                k.free(ps)

            def ssd_proj(g):
                slot, wv = k.wnext()
                for i in range(4):
                    for (c0, c1) in crs:
                        ps = linear(wv, slot, i * 128, xn, xn, 8, c0, c1)
                        k.act(FA[:, i, c0:c1], ps[:, 0:c1 - c0], AF.Silu, [ps], [FAc[i]])
                        k.free(ps)
                slot, wv = k.wnext()
                conv_stage(wv, slot, [4 * g + i for i in range(4)], FA, FAc, 4)
                for i in range(4):
                    k.act(FA[:, 4 + i, 0:W], FA[:, 4 + i, 0:W], AF.Silu, [FAc[4 + i]], [FAc[4 + i]])
                if has_s:
                    k.tt(EXPB[0:64, :].rearrange("p (b n) -> p b n", b=16),
                         Btok[0:64, 4, g, :].unsqueeze(1).to_broadcast([64, 16, 128]),
                         cstb[0:64, C_SEG:C_SEG + 16].unsqueeze(2).to_broadcast([64, 16, 128]), ALU.mult, [Btok, cstb], [EXPB])
            def ssd_blocks(g):
                for (R, c0, bi) in blocks:
                    smp = (bi == 4)
                    have_state = smp or (not first) or (bi > 0)
                    pt = k.paux()
                    for i in range(4):
                        k.tr(pt[0:R, i * 128:(i + 1) * 128], FA[:, 4 + i, c0:c0 + R], ident, [FAc[4 + i], cst], [pt], sig=(i == 3))
                    xdt = xdt_l[ctr["xd"] % 2]; xdtd = xdtd_l[ctr["xd"] % 2]; ctr["xd"] += 1
                    k.tt(xdt[0:R, :].rearrange("p (h q) -> p h q", h=8), pt[0:R, :].rearrange("p (h q) -> p h q", h=8),
                         dtt[0:R, bi, 8 * g:8 * g + 8].unsqueeze(2).to_broadcast([R, 8, 64]), ALU.mult, [pt, dtt], [xdt])
                    k.tt(xdtd[0:R, :].rearrange("p (h q) -> p h q", h=8), pt[0:R, :].rearrange("p (h q) -> p h q", h=8),
                         dtd[0:R, bi, 8 * g:8 * g + 8].unsqueeze(2).to_broadcast([R, 8, 64]), ALU.mult, [pt, dtd], [xdtd])
                    k.free(pt)
                    Lm = cstb[0:R, C_LS:C_LS + R] if smp else cstb[0:R, C_L:C_L + R]
                    acsb = acsTb_l[ctr["xd"] % 2]
                    pacs = k.paux()
                    k.mm(pacs[0:16, 0:R], dtab[0:R, bi, 0, :], Lm, [dtab, cstb], [pacs], start=True, stop=False, sig=False)
                    k.mm(pacs[0:16, 0:R], dtab[0:R, bi, 1, :], Lm, [dtab, cstb], [pacs], start=False, stop=True)
                    k.vcopy(acsb[0:16, 0, 0:R], pacs[0:16, 0:R], [pacs], [acsb])
                    k.tt(acsb[0:16, 1, 0:R], pacs[0:16, 0:R], acsb[0:16, 0, 0:R], ALU.subtract, [pacs, acsb], [acsb])
                    k.free(pacs)
                    ngm = cstb[0:R, C_NEGMS:C_NEGMS + R] if smp else cstb[0:R, C_NEGM:C_NEGM + R]
                    for hp in range(4):
                        hp2 = 4 * g + hp
                        xsv = FA[:, 4 + hp, c0:c0 + R]
                        szv = FA[:, hp, c0:c0 + R]
                        ygv = FB[:, hp, c0:c0 + R]
                        dcol = par[:, P_DSK + hp2:P_DSK + hp2 + 1]
                        Ein = Ein_l[hp % 2]; yis = yis_l[hp % 2]; y1 = y1_l[hp % 2]
                        if smp:
                            for hb in range(2):
                                sst = sst_l[hb]
                                k.load(g_ss_l[hb], sst[:, :, :], ssm0[8 * hb:8 * hb + 8, hp2 * 128:(hp2 + 1) * 128, :].rearrange("b q n -> q b n"), [sst])
                                for bq in range(2):
                                    ptq = k.pmm()
                                    for b4 in range(4):
                                        k.tr(ptq[:, b4 * 128:(b4 + 1) * 128], sst[:, bq * 4 + b4, :], ident, [sst, cst], [ptq], sig=(b4 == 3))
                                    k.acopy(sstb[:, 8 * hb + bq * 4:8 * hb + bq * 4 + 4, :], ptq[:, :].rearrange("p (b n) -> p b n", b=4), [ptq], [sstb])
                                    k.free(ptq)
                        yp_ = k.pmm()
                        for q in range(2):
                            hl = 2 * hp + q
                            h = 8 * g + hl
                            pe_ = k.paux()
                            k.mm(pe_[0:R, 0:R], SelL[0:16, h * 128:h * 128 + R], acsb[0:16, 0, 0:R], SCRh + [acsb], [pe_], start=True, stop=False, sig=False)
                            k.mm(pe_[0:R, 0:R], SelL[0:16, h * 128:h * 128 + R], acsb[0:16, 1, 0:R], SCRh + [acsb], [pe_], start=False, stop=False, sig=False)
                            k.mm(pe_[0:R, 0:R], identb[0:R, 0:R], ngm, [cstb], [pe_], start=False, stop=True)
                            E = Eh[hp % 2]; M = Mh[hp % 2]
                            k.act(E[0:R, q, 0:R], pe_[0:R, 0:R], AF.Exp, [pe_, nacs], [E], bias=nacs[0:R, bi, h:h + 1])
                            k.free(pe_)
                        k.tt(M[0:R, :, 0:R], E[0:R, :, 0:R], cbs[0:R, bi, g, 0:R].unsqueeze(1).to_broadcast([R, 2, R]), ALU.mult, [E, cbs], [M], eng=("pool" if POOL_M else "dve"))
                        for q in range(2):
                            hl = 2 * hp + q
                            k.mm(yp_[q * 64:q * 64 + 64, 0:R], xdt[0:R, hl * 64:(hl + 1) * 64], M[0:R, q, 0:R], [xdt, M], [yp_], sig=True)
                        if have_state:
                            pyi = k.paux()
                            if not smp:
                                k.mm(pyi[:, 0:R], hTb[:, hp2, :], BCT[:, 2 + g, c0:c0 + R], [hTbc[hp2], BCT], [pyi])
                            else:
                                for b in range(16):
                                    k.mm(pyi[:, 4 * b:4 * b + 4], sstb[:, b, :], BCT[:, 2 + g, 512 + 4 * b:512 + 4 * b + 4], [sstb, BCT], [pyi], sig=(b == 15))
                            pe2 = k.paux()
                            for q in range(2):
                                hq = 2 * hp2 + q
                                k.mm(pe2[q * 64:q * 64 + 64, 0:R], SelL[0:16, hq * 128:hq * 128 + 64], acsb[0:16, 0, 0:R], SCRh + [acsb], [pe2], start=True, stop=False, sig=False)
                                k.mm(pe2[q * 64:q * 64 + 64, 0:R], SelL[0:16, hq * 128:hq * 128 + 64], acsb[0:16, 1, 0:R], SCRh + [acsb], [pe2], start=False, stop=True, sig=(q == 1))
                            k.act(Ein[:, 0:R], pe2[:, 0:R], AF.Exp, [pe2], [Ein])
                            k.free(pe2)
                            k.tt(yis[:, 0:R], pyi[:, 0:R], Ein[:, 0:R], ALU.mult, [pyi, Ein], [yis])
                            k.free(pyi)
                            k.tt(y1[:, 0:R], yp_[:, 0:R], yis[:, 0:R], ALU.add, [yp_, yis], [y1])
                            k.stt(y1[:, 0:R], xsv, dcol, y1[:, 0:R], ALU.mult, ALU.add, [FAc[4 + hp], par, y1], [y1])
                        else:
                            k.stt(y1[:, 0:R], xsv, dcol, yp_[:, 0:R], ALU.mult, ALU.add, [FAc[4 + hp], par, yp_], [y1])
                        k.free(yp_)
                        k.tt(ygv, y1[:, 0:R], szv, ALU.mult, [y1, FAc[hp]], [FBc[hp]], eng=("pool" if POOL_YG else "dve"))
                        if not smp:
                            pS = k.paux()
                            k.mm(pS[:, 0:128], xdtd[0:128, hp * 128:(hp + 1) * 128], Btok[:, bi, g, :], [xdtd, Btok], [pS])
                            k.stt(S_ssm[:, hp2, :], S_ssm[:, hp2, :], cdp[:, bi, hp2:hp2 + 1], pS[:, 0:128], ALU.mult, ALU.add, [S_ssmc[hp2], cdp, pS], [S_ssmc[hp2]])
                            k.free(pS)
                        else:
                            for hb in range(2):
                                sst = sst_l[hb]
                                k.tt(sst[:, :, :], sst[:, :, :], cds[:, hp2, 8 * hb:8 * hb + 8].unsqueeze(2).to_broadcast([128, 8, 128]), ALU.mult, [sst, cds], [sst])
                                for bq in range(2):
                                    pS = k.pmm()
                                    e0 = (8 * hb + 4 * bq) * 128
                                    k.mm(pS[:, 0:512], xdtd[0:64, hp * 128:(hp + 1) * 128], EXPB[0:64, e0:e0 + 512], [xdtd, EXPB], [pS])
                                    k.tt(sst[:, bq * 4:bq * 4 + 4, :], sst[:, bq * 4:bq * 4 + 4, :], pS[:, :].rearrange("p (b n) -> p b n", b=4), ALU.add, [sst, pS], [sst])
                                    k.free(pS)
                                k.store(g_ss_l[hb], ssms[8 * hb:8 * hb + 8, hp2 * 128:(hp2 + 1) * 128, :].rearrange("b q n -> q b n"), sst[:, :, :], [sst])
                    if not smp and not (last and bi == 3):
                        ptq = k.pmm()
                        for hp in range(4):
                            k.tr(ptq[:, hp * 128:(hp + 1) * 128], S_ssm[:, 4 * g + hp, :], ident, [S_ssmc[4 * g + hp], cst], [ptq], sig=(hp == 3))
                        k.acopy(hTb[:, 4 * g:4 * g + 4, :], ptq[:, :].rearrange("p (b n) -> p b n", b=4), [ptq], hTbc[4 * g:4 * g + 4])
                        k.free(ptq)
                rmsnorm_stats(FB, FBc, 4, W, 16, 1.0 / 512.0, rs2)
                for i in range(4):
                    k.stt(mixed[:, 4 * g + i, 0:W], FB[:, i, 0:W], par[:, P_SSDN + 4 * g + i:P_SSDN + 4 * g + i + 1], rs2[:, 0:W],
                          ALU.mult, ALU.mult, [FBc[i], par, rs2], [BBc[4 * g + i]])

            a = [FA[:, i, 0:W] for i in range(8)]
            A = FAc
            scan_mask = cstb[:, C_SCAN:C_SCAN + W]
            def hgrn_head(h):
                slot, wv = k.wnext()
                qe = qe_l[h % 2]; ke = ke_l[h % 2]; kdT = kdT_l[h % 2]; dch = dch_l[h % 2]; vtok = vtok_l[h % 2]
                i0, i1, i2, i3 = 4 * (h % 2), 4 * (h % 2) + 1, 4 * (h % 2) + 2, 4 * (h % 2) + 3
                hpar = h % 2; st = {"pp": 0}
                if not first:
                    k.acopy(Shb2[:, hpar, 0, :], S_hg[:, h, :], [S_hgc[h]], [Shq[hpar][0]])
                fa = lambda i_: FA[:, i_, 0:W]
                ob = FBc[h % 2]; o_ap = FB[:, h % 2, 0:W]
                sgb = FBc[2 + h % 2]; sog = FB[:, 2 + h % 2, 0:W]
                for (R, c0, bi) in blocks:
                    ps = k.pmm()
                    for kc in range(8):
                        k.mm(ps[0:R, 0:128], xn[:, kc, c0:c0 + R], wv[:, kc, 384:512], [xn, slot], [ps], start=(kc == 0), stop=(kc == 7), sig=(kc == 7))
                    k.vcopy(vtok[0:R, bi, :], ps[0:R, 0:128], [ps], [vtok])
                    k.free(ps)
                def el_sigmoid(dst_T, dst_ci, dst_buf, wi, keep):
                    live = []
                    for (c0, c1) in crs:
                        ps = linear(wv, slot, wi * 128, xn, xn, 8, c0, c1)
                        if NATIVE_SIG:
                            k.act(dst_T[:, dst_ci, c0:c1], ps[:, 0:c1 - c0], AF.Sigmoid, [ps], [dst_buf])
                        else:
                            k.act(dst_T[:, dst_ci, c0:c1], ps[:, 0:c1 - c0], AF.Exp, [ps], [dst_buf], scale=-1.0)
                        if keep:
                            live.append((ps, c0, c1))
                        else:
                            k.free(ps)
                    if not NATIVE_SIG:
                        dd = dst_T[:, dst_ci, 0:W]
                        k.act(dd, dd, AF.Ln, [dst_buf], [dst_buf], bias=1.0)
                        k.act(dd, dd, AF.Exp, [dst_buf], [dst_buf], scale=-1.0)
                    return live

                el_sigmoid(FA, i1, A[i1], 1, False)
                k.ts(fa(i2), fa(i1), oml[:, h:h + 1], ALU.mult, [A[i1], oml, lbt], [A[i2]], s2=lbt[:, h:h + 1], op1=ALU.add, eng=("pool" if POOL_FK else "dve"))
                k.act(fa(i2), fa(i2), AF.Ln, [A[i2]], [A[i2]])
                P.op("dve", (lambda o, m, d_: (lambda e: e.tensor_tensor_scan(out=o, data0=m, data1=d_, initial=0.0, op0=ALU.mult, op1=ALU.add)))(fa(i3), scan_mask, fa(i2)),
                     reads=[cstb, A[i2]], writes=[A[i3]], cost=120.0 + 2.2 * W)
                k.ts(fa(i1), fa(i1), noml[:, h:h + 1], ALU.mult, [A[i1], noml, oml], [A[i1]], s2=oml[:, h:h + 1], op1=ALU.add, eng=("pool" if POOL_FK else "dve"))
                for (ps, c0, c1) in el_sigmoid(FA, i0, A[i0], 0, True):
                    k.tt(FA[:, i0, c0:c1], FA[:, i0, c0:c1], ps[:, 0:c1 - c0], ALU.mult, [A[i0], ps], [A[i0]])
                    k.free(ps)
                k.act(fa(i2), fa(i3), AF.Exp, [A[i3]], [A[i2]])
                k.tt(qe[:, 0:W], fa(i0), fa(i2), ALU.mult, [A[i0], A[i2]], [qe], eng=("pool" if POOL_QE else "dve"))
                k.act(fa(i0), fa(i3), AF.Exp, [A[i3]], [A[i0]], scale=-1.0)
                k.tt(fa(i0), fa(i0), fa(i1), ALU.mult, [A[i0], A[i1]], [A[i0]])
                k.acopy(ke[:, 0:W], fa(i0), [A[i0]], [ke])
                bc_p = FA[:, i3, 0:512].rearrange("p (c q) -> p c q", q=64)
                k.act(dch[:, 0:8].unsqueeze(2), bc_p[:, :, 63:64], AF.Exp, [A[i3]], [dch])
                k.tt(kdT[:, 0:512].rearrange("p (c q) -> p c q", q=64), FA[:, i0, 0:512].rearrange("p (c q) -> p c q", q=64),
                     dch[:, 0:8].unsqueeze(2).to_broadcast([128, 8, 64]), ALU.mult, [A[i0], dch], [kdT])
                if has_s:
                    bc_s = FA[:, i3, 512:576].rearrange("p (c q) -> p c q", q=4)
                    k.act(dch[:, 8:24].unsqueeze(2), bc_s[:, :, 3:4], AF.Exp, [A[i3]], [dch])
                    k.tt(kdT[:, 512:576].rearrange("p (c q) -> p c q", q=4), FA[:, i0, 512:576].rearrange("p (c q) -> p c q", q=4),
                         dch[:, 8:24].unsqueeze(2).to_broadcast([128, 16, 4]), ALU.mult, [A[i0], dch], [kdT])
                for (ps, c0, c1) in el_sigmoid(FB, 2 + h % 2, sgb, 2, True):
                    k.tt(FB[:, 2 + h % 2, c0:c1], FB[:, 2 + h % 2, c0:c1], ps[:, 0:c1 - c0], ALU.mult, [sgb, ps], [sgb])
                    k.free(ps)
                for (R, c0, bi) in blocks:
                    smp = (bi == 4)
                    attm = attm_l[ctr["at"] % 2]; kdtok = kdtok_l[ctr["at"] % 2]; ctr["at"] += 1
                    pa = k.paux()
                    k.mm(pa[0:R, 0:R], ke[:, c0:c0 + R], qe[:, c0:c0 + R], [ke, qe], [pa])
                    msk = cstb[0:R, C_LS:C_LS + R] if smp else cstb[0:R, C_MASK2:C_MASK2 + R]
                    k.tt(attm[0:R, 0:R], pa[0:R, 0:R], msk, ALU.mult, [pa, cstb], [attm])
                    k.free(pa)
                    k.tr(pb[0:R, 512:640], kdT[:, c0:c0 + R], identb, [kdT, cstb], [pb])
                    (k.vcopy if DVE_KD else k.acopy)(kdtok[0:R, :], pb[0:R, 512:640], [pb], [kdtok])
                    if not smp:
                        have0 = (not first) or (bi > 0)
                        pSs = []
                        for sc in range(2):
                            r0 = sc * 64
                            pS = k.paux()
                            k.mm(pS[:, 0:128], kdtok[r0:r0 + 64, :], vtok[r0:r0 + 64, bi, :], [kdtok, vtok], [pS])
                            pSs.append(pS)
                        po = k.paux()
                        k.mm(po[:, 0:128], vtok[0:128, bi, :], attm[:, :], [vtok, attm], [po], start=True, stop=False, sig=(not have0))
                        if have0:
                            k.mm(po[:, 0:64], Shb2[:, hpar, st["pp"], :], qe[:, c0:c0 + 64], [Shq[hpar][st["pp"]], qe], [po], start=False, stop=False, sig=True)
                        for sc in range(2):
                            dcol_ = dch[:, 2 * bi + sc:2 * bi + sc + 1]
                            if sc == 1:
                                k.mm(po[:, 64:128], Shb2[:, hpar, st["pp"], :], qe[:, c0 + 64:c0 + 128], [Shq[hpar][st["pp"]], qe], [po], start=False, stop=True, sig=True)
                            if not (last and bi == 3 and sc == 1):
                                k.stt(Shb2[:, hpar, 1 - st["pp"], :], S_hg[:, h, :], dcol_, pSs[sc][:, 0:128], ALU.mult, ALU.add, [S_hgc[h], dch, pSs[sc]], [Shq[hpar][1 - st["pp"]]])
                                st["pp"] ^= 1
                            k.stt(S_hg[:, h, :], S_hg[:, h, :], dcol_, pSs[sc][:, 0:128], ALU.mult, ALU.add, [S_hgc[h], dch, pSs[sc]], [S_hgc[h]])
                            k.free(pSs[sc])
                    else:
                        for hb in range(2):
                            k.load(g_ss_l[hb], sst_l[hb][:, :, :], hg0[8 * hb:8 * hb + 8, h * 128:(h + 1) * 128, :].rearrange("b q n -> q b n"), [sst_l[hb]])
                            k.acopy(sstb[:, 8 * hb:8 * hb + 8, :], sst_l[hb][:, :, :], [sst_l[hb]], [sstb])
                        po = k.paux()
                        k.mm(po[:, 0:64], vtok[0:64, 4, :], attm[0:64, 0:64], [vtok, attm], [po], start=True, stop=False, sig=False)
                        for b in range(16):
                            k.mm(po[:, 4 * b:4 * b + 4], sstb[:, b, :], qe[:, 512 + 4 * b:512 + 4 * b + 4], [sstb, qe], [po], start=False, stop=(b == 15), sig=(b == 15))
                        k.tt(EXPB[0:64, :].rearrange("p (b n) -> p b n", b=16),
                             vtok[0:64, 4, :].unsqueeze(1).to_broadcast([64, 16, 128]),
                             cstb[0:64, C_SEG:C_SEG + 16].unsqueeze(2).to_broadcast([64, 16, 128]), ALU.mult, [vtok, cstb], [EXPB])
                        for hb in range(2):
                            sst = sst_l[hb]
                            k.tt(sst[:, :, :], sst[:, :, :], dch[:, 8 + 8 * hb:16 + 8 * hb].unsqueeze(2).to_broadcast([128, 8, 128]), ALU.mult, [sst, dch], [sst])
                            for bq in range(2):
                                pS = k.pmm()
                                e0 = (8 * hb + 4 * bq) * 128
                                k.mm(pS[:, 0:512], kdtok[0:64, :], EXPB[0:64, e0:e0 + 512], [kdtok, EXPB], [pS])
                                k.tt(sst[:, bq * 4:bq * 4 + 4, :], sst[:, bq * 4:bq * 4 + 4, :], pS[:, :].rearrange("p (b n) -> p b n", b=4), ALU.add, [sst, pS], [sst])
                                k.free(pS)
                            k.store(g_ss_l[hb], hgs[8 * hb:8 * hb + 8, h * 128:(h + 1) * 128, :].rearrange("b q n -> q b n"), sst[:, :, :], [sst])
                    (k.vcopy if DVE_O else k.acopy)(FB[:, h % 2, c0:c0 + R], po[:, 0:R], [po], [ob])
                    k.free(po)
                k.act(BIGB[:, 16 + h % 2, 0:W], o_ap, AF.Square, [ob], [BBc[16 + h % 2]])
                rsh = rs2 if h % 2 == 0 else rs
                for (c0, c1) in crs:
                    ps = k.pmm()
                    k.mm(ps[:, 0:c1 - c0], onesb, BIGB[:, 16 + h % 2, c0:c1], [cstb, BBc[16 + h % 2]], [ps])
                    k.act(rsh[:, c0:c1], ps[:, 0:c1 - c0], AF.Ln, [ps], [rsh], bias=EPS, scale=1.0 / 128.0)
                    k.free(ps)
                k.act(rsh[:, 0:W], rsh[:, 0:W], AF.Exp, [rsh], [rsh], scale=-0.5)
                k.stt(o_ap, o_ap, par[:, P_HGN + h:P_HGN + h + 1], rsh[:, 0:W], ALU.mult, ALU.mult, [ob, par, rsh], [ob])
                k.tt(mixed[:, 8 + h, 0:W], o_ap, sog, ALU.mult, [ob, sgb], [BBc[8 + h]])
            for kind, idx in MIX_ORDER:
                {"gp": ssd_proj, "gb": ssd_blocks, "h": hgrn_head}[kind](idx)
            if last:
                k.store(g_so, ssmp.rearrange("(a q) n -> q a n", q=128), S_ssm[:, :, :], S_ssmc)
                k.store(g_so, hgp.rearrange("(a q) n -> q a n", q=128), S_hg[:, :, :], S_hgc)

            for b in range(4):
                slot, wv = k.wnext()
                for f2 in range(2):
                    fo = 2 * b + f2
                    for (c0, c1) in crs:
                        ps = linear(wv, slot, f2 * 128, mixed, BBc, 16, c0, c1)
                        k.tt(xT[:, fo, c0:c1], xT[:, fo, c0:c1], ps[:, 0:c1 - c0], ALU.add, [xTc[fo], ps], [xTc[fo]])
                        k.free(ps)

        def ple_and_out(ti, W):
            crs = colranges(W)
            norm_to_xn(P_NPLE, W)
            slot, wv = k.wnext()
            for fo in range(8):
                for (c0, c1) in crs:
                    ps = linear(wv, slot, fo * 128, pT, pT, 2, c0, c1)
                    k.acopy(FA[:, fo, c0:c1], ps[:, 0:c1 - c0], [ps], [FAc[fo]])
                    k.free(ps)
            rmsnorm_stats(FA, FAc, 8, W, 0, 1.0 / 1024.0, rs2)
            for b in range(2):
                slot, wv = k.wnext()
                for f4 in range(4):
                    fo = 4 * b + f4
                    ge, en = 2 * (fo % 2), 2 * (fo % 2) + 1
                    k.stt(FB[:, en, 0:W], FA[:, fo, 0:W], par[:, P_PPOST + fo:P_PPOST + fo + 1], rs2[:, 0:W], ALU.mult, ALU.mult, [FAc[fo], par, rs2], [FBc[en]])
                    for (c0, c1) in crs:
                        ps = linear(wv, slot, f4 * 128, xn, xn, 8, c0, c1)
                        k.act(FB[:, ge, c0:c1], ps[:, 0:c1 - c0], AF.Sigmoid, [ps], [FBc[ge]])
                        k.free(ps)
                    k.tt(FB[:, en, 0:W], FB[:, en, 0:W], FB[:, ge, 0:W], ALU.mult, [FBc[en], FBc[ge]], [FBc[en]])
                    k.tt(xT[:, fo, 0:W], xT[:, fo, 0:W], FB[:, en, 0:W], ALU.add, [xTc[fo], FBc[en]], [xTc[fo]])
            rmsnorm_stats(xT, xTc, 8, W, 0, 1.0 / 1024.0, rs)
            for i in range(8):
                k.stt(FA[:, i, 0:W], xT[:, i, 0:W], par[:, P_NFIN + i:P_NFIN + i + 1], rs[:, 0:W], ALU.mult, ALU.mult, [xTc[i], par, rs], [FAc[i]])
            blocks = [(128, i * 128, yp[ti * 512 + i * 128: ti * 512 + (i + 1) * 128, :]) for i in range(4)]
            if W > 512:
                blocks.append((64, 512, ysm))
            for bi, (R, c0, dst) in enumerate(blocks):
                yb, ygrp = [(yout[0], g_y[0]), (yout[1], g_y[1])][bi % 2]
                for half in range(2):
                    ps = k.paux()
                    for q in range(4):
                        kc = half * 4 + q
                        k.tr(ps[0:R, q * 128:(q + 1) * 128], FA[:, kc, c0:c0 + R], ident, [FAc[kc], cst], [ps], sig=(q == 3))
                    k.acopy(yb[0:R, half * 512:(half + 1) * 512], ps[0:R, 0:512], [ps], [yb])
                    k.free(ps)
                k.store(ygrp, dst, yb[0:R, :], [yb])

        for ti in range(NT):
            W = 576 if ti == 0 else 512
            xT, xTc = (xT_A, xTc_A) if ti % 2 == 0 else (xT_B, xTc_B)
            guard["b"] = [sst_l[0], sst_l[1], sstb, EXPB] if ti == 1 else []
            load_tile(ti, W)
            ffn(P_NF1, W)
            mixer(ti, W)
            ffn(P_NF2, W)
            ple_and_out(ti, W)
        P.wait_all("sp", out_groups)
        P.emit()
    return nc


_NC_CACHE = {}


def kernel(**inp):
    f = lambda a: np.ascontiguousarray(np.asarray(a, dtype=np.float32))
    g = {k_: f(v) for k_, v in inp.items()}
    cst = make_consts()
    par = np.zeros((128, NPAR), np.float32)
    par[:, P_NF1:P_NF1 + 8] = fm(g["norm_ffn1"][0]); par[:, P_NMIX:P_NMIX + 8] = fm(g["norm_mix"][0])
    par[:, P_NF2:P_NF2 + 8] = fm(g["norm_ffn2"][0]); par[:, P_NPLE:P_NPLE + 8] = fm(g["norm_ple"][0])
    par[:, P_PPOST:P_PPOST + 8] = fm(g["ple_post_norm"][0]); par[:, P_NFIN:P_NFIN + 8] = fm(g["norm_final"])
    par[:, P_SSDN:P_SSDN + 8] = fm(g["ssd_norm"][0]); par[:, P_HGN:P_HGN + 8] = fm(g["hg_norm"][0])
    for j in range(4):
        par[:, P_CW + j * 12:P_CW + (j + 1) * 12] = fm(g["conv_w"][0, j])
    par[:, P_CB:P_CB + 12] = fm(g["conv_b"][0])
    par[:, P_LB0:P_LB0 + 8] = fm(g["hg_lb_logits"][0]); par[:, P_LB1:P_LB1 + 8] = fm(g["hg_lb_logits"][1])
    par[:, P_DSK:P_DSK + 8] = fm(np.repeat(g["d_skip"][0], 64))
    bcr = np.zeros((128, 32), np.float32)
    bcr[:, 0:16] = g["dt_bias"][0][None, :]; bcr[:, 16:32] = g["a_log"][0][None, :]
    shared = dict(wf1u=g["w_ffn1_up"][0], wf1d=g["w_ffn1_down"][0], win=g["w_in"][0], wout=g["w_out"][0],
                  wf2u=g["w_ffn2_up"][0], wf2d=g["w_ffn2_down"][0], wpg=g["w_ple_gate"][0], wpp=g["w_ple_proj"][0],
                  cst=cst, par=par, bcr=bcr)
    in_maps = []
    for c in range(NCORES):
        bs = slice(16 * c, 16 * c + 16)
        m = dict(shared)
        m.update(xp=g["x_prompt"][c], xsm=g["x_sample"][bs].reshape(64, 1024),
                 cv0=g["state_conv"][0, bs].reshape(48, 1536),
                 ssm0=g["state_ssm"][0, bs].reshape(16, 1024, 128), hg0=g["state_hgrn"][0, bs].reshape(16, 1024, 128),
                 pp=g["p_prompt"][0, c], psm=g["p_sample"][0, bs].reshape(64, 256))
        in_maps.append({k_: np.ascontiguousarray(v) for k_, v in m.items()})
    if "nc" not in _NC_CACHE:
        _NC_CACHE["nc"] = build_program()
    nc = _NC_CACHE["nc"]
    res = run_bass_kernel_spmd(nc, in_maps, core_ids=list(range(NCORES)))
    R = res.results
    cat = lambda name: [np.asarray(R[c][name], np.float32) for c in range(NCORES)]
    y_prompt = np.stack(cat("yp"), 0)
    y_sample = np.concatenate([a.reshape(16, 4, 1024) for a in cat("ysm")], 0)
    conv_p = np.stack(cat("cvp"), 0)[None]
    ssm_p = np.stack([a.reshape(16, 64, 128) for a in cat("ssmp")], 0)[None]
    hg_p = np.stack([a.reshape(8, 128, 128) for a in cat("hgp")], 0)[None]
    conv_s = np.concatenate([a.reshape(16, 3, 1536) for a in cat("cvs")], 0)[None]
    ssm_s = np.concatenate([a.reshape(16, 16, 64, 128) for a in cat("ssms")], 0)[None]
    hg_s = np.concatenate([a.reshape(16, 8, 128, 128) for a in cat("hgs")], 0)[None]
    return (y_prompt, y_sample, conv_p, ssm_p, hg_p, conv_s, ssm_s, hg_s)
```

```python
from contextlib import ExitStack
import numpy as np
import concourse.bass as bass
import concourse.mybir as mybir
from concourse.bass_utils import run_bass_kernel_spmd

F32 = mybir.dt.float32
BF16 = mybir.dt.bfloat16
AF = mybir.ActivationFunctionType
ALU = mybir.AluOpType

ENGS = ("pe", "act", "dve", "pool", "sp")
NCORES = 8
EPS = 1e-6
NSIG = False
XF = (True, False, False)
PF = (False, False, False, False, False)


class Buf:
    def __init__(self, name, t):
        self.name = name
        self.t = t
        self.w = None
        self.readers = []
        self.cow = []

    def __getitem__(self, k):
        return self.t[k]


class DmaGroup:
    def __init__(self, sem, name):
        self.sem = sem
        self.count = 0
        self.name = name


class Node:
    __slots__ = ("eng", "fns", "preds", "succs", "idx", "cost", "group", "seq", "npred", "ready", "end", "xfer", "dval", "tbl", "prio")

    def __init__(self, eng, idx):
        self.eng = eng
        self.fns = []
        self.preds = {}
        self.succs = []
        self.idx = idx
        self.cost = 0.0
        self.group = None
        self.seq = 0
        self.npred = 0
        self.ready = 0.0
        self.end = 0.0
        self.xfer = 0.0
        self.dval = 0
        self.tbl = None


PRIO_CP = True
PRUNE = False
DRY = False
LAT = 110.0


class Prog:
    def __init__(self, nc, stack):
        self.nc = nc
        self.stack = stack
        self.nodes = []
        self.open_pe = None
        self.esem = {e: stack.enter_context(nc.semaphore("sem_" + e)) for e in ENGS}
        self.groups = []
        self.final_waits = []

    def sb(self, name, shape, dtype=F32):
        if DRY:
            return Buf(name, self.nc.dram_tensor("sb_" + name, list(shape), dtype).ap())
        t = self.stack.enter_context(self.nc.sbuf_tensor("sb_" + name, list(shape), dtype))
        return Buf(name, t)

    def ps(self, name, shape, dtype=F32):
        if DRY:
            return Buf(name, self.nc.dram_tensor("ps_" + name, list(shape), dtype).ap())
        t = self.stack.enter_context(self.nc.psum_tensor("ps_" + name, list(shape), dtype))
        return Buf(name, t)

    def sub(self, buf, n):
        return [Buf(f"{buf.name}_{i}", buf.t) for i in range(n)]

    def group(self, name):
        g = DmaGroup(self.stack.enter_context(self.nc.semaphore("dg_" + name)), name)
        self.groups.append(g)
        return g

    def _link(self, node, reads, writes, nowaw=False):
        def add(p, kind):
            if p is None or p is node:
                return
            node.preds[p] = node.preds.get(p, 0) | kind
        for b in reads:
            add(b.w, 1)
            for cw in b.cow:
                add(cw, 1)
        newcow = {}
        for b in writes:
            if nowaw and b.w is not None and b.w.group is not None and b.w.group is node.group:
                for p, kind in b.w.preds.items():
                    add(p, kind)
                newcow[id(b)] = b.cow + [b.w]
            else:
                add(b.w, 1)
                for cw in b.cow:
                    add(cw, 1)
                for r in b.readers:
                    add(r, 2)
                newcow[id(b)] = []
        for b in reads:
            if not b.readers or b.readers[-1] is not node:
                b.readers.append(node)
        for b in writes:
            b.cow = newcow[id(b)]
            b.w = node
            b.readers = []

    def op(self, eng, fn, reads=(), writes=(), sig=True, cost=100.0, tbl=None):
        if eng == "pe" and self.open_pe is not None:
            node = self.open_pe
        else:
            node = Node(eng, len(self.nodes))
            node.tbl = tbl
            self.nodes.append(node)
        node.fns.append(fn)
        node.cost += cost
        self._link(node, reads, writes)
        if eng == "pe":
            self.open_pe = None if sig else node

    def dma(self, eng, group, fn, reads=(), writes=(), nbytes=0, nowaw=False, after=()):
        node = Node(eng, len(self.nodes))
        self.nodes.append(node)
        node.fns.append(fn)
        node.group = group
        for b in after:
            for p in [b.w] + list(b.cow):
                if p is not None:
                    node.preds[p] = node.preds.get(p, 0) | 1
        node.cost = 1200.0 if eng == "pool" else 150.0
        node.xfer = 2000.0 + nbytes / 250.0
        self._link(node, reads, writes, nowaw=nowaw)

    def wait_all(self, eng, groups):
        self.final_waits.append((eng, groups))

    def schedule(self):
        assert self.open_pe is None
        nodes = self.nodes
        for n in nodes:
            n.npred = len(n.preds)
            for p in n.preds:
                p.succs.append(n)
        if PRIO_CP:
            cp = [0.0] * len(nodes)
            for n in reversed(nodes):
                m = 0.0
                for s_ in n.succs:
                    v = cp[s_.idx] + (0.0 if s_.eng == n.eng else LAT)
                    if v > m:
                        m = v
                cp[n.idx] = n.cost + n.xfer + m
            for n in nodes:
                n.prio = -cp[n.idx]
        else:
            for n in nodes:
                n.prio = float(n.idx)
        avail = {e: [] for e in ENGS}
        free = {e: 0.0 for e in ENGS}
        order = {e: [] for e in ENGS}
        for n in nodes:
            if n.npred == 0:
                avail[n.eng].append(n)
        left = len(nodes)
        seqlist = []
        self.seqlist = seqlist
        cur_tbl = None
        while left:
            best = None
            for e in ENGS:
                av = avail[e]
                if not av:
                    continue
                t = free[e]
                cand = None
                for n in av:
                    if n.ready <= t:
                        if e == "act":
                            key = (0 if (n.tbl is None or n.tbl == cur_tbl) else 1, n.prio, n.idx)
                        else:
                            key = (0, n.prio, n.idx)
                        if cand is None or cand[0] != 0 or key < cand[1]:
                            cand = (0, key, n)
                    elif cand is None or (cand[0] == 1 and (n.ready, n.idx) < cand[1]):
                        cand = (1, (n.ready, n.idx), n)
                n = cand[2]
                st = max(t, n.ready)
                if best is None or st < best[0] or (st == best[0] and n.idx < best[1].idx):
                    best = (st, n, e)
            st, n, e = best
            avail[e].remove(n)
            c = n.cost
            if e == "act" and n.tbl is not None:
                if cur_tbl is not None and n.tbl != cur_tbl:
                    c += 1300.0
                cur_tbl = n.tbl
            free[e] = st + c
            n.end = st + c + n.xfer
            order[e].append(n)
            seqlist.append(n)
            left -= 1
            for s_ in n.succs:
                r = n.end + (0.0 if s_.eng == n.eng else LAT)
                if r > s_.ready:
                    s_.ready = r
                s_.npred -= 1
                if s_.npred == 0:
                    avail[s_.eng].append(s_)
        self.order = order
        self.makespan = max(free.values())
        return order

    def emit(self):
        order = self.schedule()
        if DRY:
            return
        for e in ENGS:
            c = 0
            for n in order[e]:
                if n.group is None:
                    c += 1
                    n.seq = c
                else:
                    n.group.count += 16
                    n.dval = n.group.count
        nc = self.nc
        esem = self.esem
        Klast = {e: {} for e in ENGS}
        Kdone = {}
        waits_of = {}
        self.n_waits = 0
        for n in self.seqlist:
            e = n.eng
            base = Klast[e]
            need = {}
            for p, kind in n.preds.items():
                if p.group is not None:
                    key, val = p.group, p.dval
                else:
                    if p.eng == e and e == "pe":
                        continue
                    key, val = p.eng, p.seq
                if key not in need or need[key][0] < val:
                    need[key] = (val, p)
            waits = []
            for key, (val, p) in sorted(need.items(), key=lambda kv: -kv[1][1].end):
                if base.get(key, 0) >= val:
                    continue
                waits.append((esem[key] if isinstance(key, str) else key.sem, val))
                if PRUNE:
                    for k2, v2 in Kdone[p].items():
                        if base.get(k2, 0) < v2:
                            base[k2] = v2
                if base.get(key, 0) < val:
                    base[key] = val
            waits_of[n] = waits
            self.n_waits += len(waits)
            kd = dict(base)
            if n.group is None:
                if kd.get(e, 0) < n.seq:
                    kd[e] = n.seq
            else:
                kd[n.group] = max(kd.get(n.group, 0), n.dval)
            Kdone[n] = kd
        streams = {e: [(waits_of[n], n) for n in order[e]] for e in ENGS}
        finals = {e: [] for e in ENGS}
        for e, groups in self.final_waits:
            finals[e] += [(g.sem, g.count) for g in groups if g.count > 0]

        with nc.Block() as block:
            def run(engname):
                def body(eh):
                    for waits, n in streams[engname]:
                        for sem, val in waits:
                            eh.wait_ge(sem, val)
                        ins = None
                        for fn in n.fns:
                            ins = fn(eh)
                        if n.group is None:
                            ins.then_inc(esem[engname], 1)
                        else:
                            ins.then_inc(n.group.sem, 16)
                    for sem, val in finals[engname]:
                        eh.wait_ge(sem, val)
                return body
            block.tensor(run("pe"))
            block.scalar(run("act"))
            block.vector(run("dve"))
            block.gpsimd(run("pool"))
            block.sync(run("sp"))


C_ID, C_L, C_NEGM, C_MASK2, C_ONES, C_LS, C_NEGMS, C_ONES_S, C_SEG, C_SCAN = 0, 128, 256, 384, 512, 640, 704, 768, 832, 848
NCST = 848 + 576
NCSTB = 848
P_NF1, P_NMIX, P_NF2, P_NPLE, P_PPOST, P_NFIN, P_SSDN, P_HGN, P_CW, P_CB, P_LB0, P_LB1, P_DSK = 0, 8, 16, 24, 32, 40, 48, 56, 64, 112, 124, 132, 140
NPAR = 148

WI_Z, WI_XS, WI_B, WI_C, WI_DT, WI_Q, WI_FR, WI_IV, WI_OG = 0, 1024, 2048, 2304, 2560, 2576, 3600, 4624, 5648


def make_consts():
    c = np.zeros((128, NCST), np.float32)
    ii = np.arange(128)
    p, j = ii[:, None], ii[None, :]
    c[:, C_ID:C_ID + 128] = (p == j)
    c[:, C_L:C_L + 128] = (p <= j)
    c[:, C_NEGM:C_NEGM + 128] = np.where(p > j, -30000.0, 0.0)
    c[:, C_MASK2:C_MASK2 + 128] = (p <= j) & (p // 64 == j // 64)
    c[:, C_ONES:C_ONES + 128] = 1.0
    j64 = np.arange(64)[None, :]
    same = (p // 4 == j64 // 4) & (p < 64)
    c[:, C_LS:C_LS + 64] = same & (p <= j64)
    c[:, C_NEGMS:C_NEGMS + 64] = np.where(same & (p <= j64), 0.0, -30000.0)
    c[:, C_ONES_S:C_ONES_S + 64] = same
    c[:, C_SEG:C_SEG + 16] = (p // 4 == np.arange(16)[None, :]) & (p < 64)
    col = np.arange(576)
    sm = np.where(col < 512, (col % 64) != 0, ((col - 512) % 4) != 0).astype(np.float32)
    c[:, C_SCAN:C_SCAN + 576] = sm[None, :]
    return c


def fm(v):
    v = np.asarray(v, np.float32).reshape(-1, 128)
    return np.ascontiguousarray(v.T)


def _nfree(ap):
    n = 1
    for d in ap.shape[1:]:
        n *= d
    return n


def _nbytes(ap):
    n = 4
    for d in ap.shape:
        n *= d
    return n


_TBL = {AF.Exp: "el", AF.Ln: "el", AF.Silu: "silu", AF.Sigmoid: "sigm", AF.Sqrt: "sqrt"}


class K:
    def __init__(self, nc, st):
        self.nc = nc
        self.P = Prog(nc, st)
        self.mm_i = 0
        self.aux_i = 0

    def mm(self, out, lhsT, rhs, reads, writes, start=True, stop=True, sig=True):
        n = _nfree(rhs)
        c = (4.0 if rhs.dtype == F32 else 1.0) * max(n, 64) / 2.2 + (50.0 if n <= 128 else 12.0)
        self.P.op("pe", lambda e: e.matmul(out, lhsT=lhsT, rhs=rhs, start=start, stop=stop),
                  reads=reads, writes=writes, sig=sig, cost=c)

    def tr(self, out, in_, ident, reads, writes, sig=True):
        self.P.op("pe", lambda e: e.transpose(out, in_, ident), reads=reads, writes=writes, sig=sig, cost=(213.0 if in_.dtype == F32 else 107.0))

    def act(self, out, in_, func, reads, writes, bias=None, scale=1.0):
        c = 265.0 + _nfree(in_) * 0.52
        tbl = _TBL.get(func)
        if bias is None:
            self.P.op("act", lambda e: e.activation(out=out, in_=in_, func=func, scale=scale), reads=reads, writes=writes, cost=c, tbl=tbl)
        else:
            self.P.op("act", lambda e: e.activation(out=out, in_=in_, func=func, bias=bias, scale=scale), reads=reads, writes=writes, cost=c, tbl=tbl)

    def acopy(self, out, in_, reads, writes):
        self.P.op("act", lambda e: e.copy(out=out, in_=in_), reads=reads, writes=writes, cost=265.0 + _nfree(in_) * 0.52)

    def vcopy(self, out, in_, reads, writes):
        self.P.op("dve", lambda e: e.tensor_copy(out=out, in_=in_), reads=reads, writes=writes, cost=190.0 + _nfree(in_) * 0.8)

    def tt(self, out, in0, in1, op, reads, writes, eng="dve"):
        c = 190.0 + _nfree(in0) * 0.8 if eng == "dve" else 180.0 + _nfree(in0) * 2.05
        self.P.op(eng, lambda e: e.tensor_tensor(out=out, in0=in0, in1=in1, op=op), reads=reads, writes=writes, cost=c)

    def ts(self, out, in0, s1, op0, reads, writes, s2=None, op1=None, eng="dve"):
        c = 190.0 + _nfree(in0) * 0.8 if eng == "dve" else 180.0 + _nfree(in0) * 1.1
        if op1 is None:
            self.P.op(eng, lambda e: e.tensor_scalar(out=out, in0=in0, scalar1=s1, scalar2=None, op0=op0), reads=reads, writes=writes, cost=c)
        else:
            self.P.op(eng, lambda e: e.tensor_scalar(out=out, in0=in0, scalar1=s1, scalar2=s2, op0=op0, op1=op1), reads=reads, writes=writes, cost=c)

    def stt(self, out, in0, scalar, in1, op0, op1, reads, writes):
        self.P.op("dve", lambda e: e.scalar_tensor_tensor(out=out, in0=in0, scalar=scalar, in1=in1, op0=op0, op1=op1), reads=reads, writes=writes,
                  cost=190.0 + _nfree(in0) * 1.07)

    def recip(self, out, in_, reads, writes):
        self.P.op("dve", lambda e: e.reciprocal(out=out, in_=in_), reads=reads, writes=writes, cost=190.0 + _nfree(in_) * 0.8)

    def memset(self, ap, val, writes):
        self.P.op("dve", lambda e: e.memset(ap, val), writes=writes, cost=190.0 + _nfree(ap) * 0.8)

    def load(self, group, out, in_, writes, eng="sp"):
        self.P.dma(eng, group, lambda e: e.dma_start(out=out, in_=in_), writes=writes, nbytes=_nbytes(out))

    def store(self, group, out, in_, reads):
        self.P.dma("sp", group, lambda e: e.dma_start(out=out, in_=in_), reads=reads, nbytes=_nbytes(in_))

    def _alloc(self, pool, ptr):
        n = len(pool)
        for d in range(n):
            b = pool[(ptr + d) % n]
            if not getattr(b, "live", False):
                b.live = True
                return b, (ptr + d + 1) % n
        raise RuntimeError("psum pool exhausted")

    def pmm(self):
        b, self.mm_i = self._alloc(self.mmps, self.mm_i)
        return b

    def paux(self):
        b, self.aux_i = self._alloc(self.auxps, self.aux_i)
        return b

    def free(self, *bs):
        for b in bs:
            assert b.live
            b.live = False

    def wsetup(self, nslot):
        self.NSLOT = nslot
        self.wslots = [self.P.sb(f"wslot{i}", [128, 4096], BF16) for i in range(nslot)]
        self.wgroups = [self.P.group(f"w{i}") for i in range(nslot)]
        self.wblocks = []
        self.w_issued = 0
        self.w_used = 0

    def wview(self, slot, KC, ncols):
        return slot.t[:, 0:KC * ncols].rearrange("p (k n) -> p k n", k=KC)

    def wissue(self, upto):
        while self.w_issued < min(upto, len(self.wblocks)):
            i = self.w_issued
            KC, ncols, parts = self.wblocks[i]
            slot = self.wslots[i % self.NSLOT]
            g = self.wgroups[i % self.NSLOT]
            v = self.wview(slot, KC, ncols)
            extra = [self.wslots[(i - 2) % self.NSLOT]] if (len(parts) >= 4 and i >= 2) else []
            for pi, (d0, n, src) in enumerate(parts):
                self.P.dma("pool", g,
                           (lambda dst, s: (lambda e: e.dma_start(out=dst, in_=s)))(
                               v[:, :, d0:d0 + n], src.rearrange("(k p) n -> p k n", p=128)),
                           after=(extra if pi == 0 else []), writes=[slot], nbytes=KC * 128 * n * 4, nowaw=(pi > 0))
            self.w_issued += 1

    def wnext(self):
        i = self.w_used
        self.wissue(i + self.NSLOT)
        KC, ncols, parts = self.wblocks[i]
        slot = self.wslots[i % self.NSLOT]
        self.w_used += 1
        return slot, self.wview(slot, KC, ncols)


def build_program():
    nc = bass.Bass("TRN2", target_bir_lowering=False)
    din = lambda name, shape: nc.dram_tensor(name, list(shape), F32, kind="ExternalInput").ap()
    dout = lambda name, shape: nc.dram_tensor(name, list(shape), F32, kind="ExternalOutput").ap()
    xp = din("xp", [2048, 1024]); xsm = din("xsm", [64, 1024]); cv0 = din("cv0", [48, 1536])
    ssm0 = din("ssm0", [16, 1024, 128]); hg0 = din("hg0", [16, 1024, 128])
    pp = din("pp", [2048, 256]); psm = din("psm", [64, 256])
    wf1u = din("wf1u", [1024, 5632]); wf1d = din("wf1d", [2816, 1024]); win = din("win", [1024, 6672])
    wout = din("wout", [2048, 1024]); wf2u = din("wf2u", [1024, 5632]); wf2d = din("wf2d", [2816, 1024])
    wpg = din("wpg", [1024, 1024]); wpp = din("wpp", [256, 1024])
    cstd = din("cst", [128, NCST]); pard = din("par", [128, NPAR]); bcrd = din("bcr", [128, 32])
    yp = dout("yp", [2048, 1024]); ysm = dout("ysm", [64, 1024]); cvp = dout("cvp", [3, 1536])
    ssmp = dout("ssmp", [1024, 128]); hgp = dout("hgp", [1024, 128]); cvs = dout("cvs", [48, 1536])
    ssms = dout("ssms", [16, 1024, 128]); hgs = dout("hgs", [16, 1024, 128])

    with ExitStack() as st:
        k = K(nc, st)
        P = k.P
        WM = 576
        xT = P.sb("xT", [128, 8, WM]); xn = P.sb("xn", [128, 8, WM], BF16)
        BIGB = P.sb("BIGB", [128, 22, WM], BF16)
        FA = P.sb("FA", [128, 8, WM]); FB = P.sb("FB", [128, 4, 640])
        cst = P.sb("cst", [128, 128]); cstb = P.sb("cstb", [128, NCST], BF16)
        par = P.sb("par", [128, NPAR]); bcr = P.sb("bcr", [128, 32])
        lbt = P.sb("lbt", [128, 8]); oml = P.sb("oml", [128, 8]); noml = P.sb("noml", [128, 8]); abc = P.sb("abc", [128, 16])
        xin = [P.sb(f"xin{i}", [128, 1024]) for i in range(2)]
        pin = [P.sb("pin0", [128, 256])]
        yout = [P.sb("yout0", [128, 1024]), P.sb("yout1", [128, 1024])]
        pT = P.sb("pT", [128, 2, WM], BF16)
        S_ssm = P.sb("S_ssm", [128, 8, 128]); S_hg = P.sb("S_hg", [128, 8, 128])
        hTb = P.sb("hTb", [128, 8, 128], BF16); Shb2 = P.sb("Shb2", [128, 2, 2, 128], BF16)
        hist = P.sb("hist", [128, 12, 3])
        cvi = P.sb("cvi", [128, 128]); cvt = P.sb("cvt", [128, 48]); cvo = P.sb("cvo", [128, 128])
        XS = P.sb("XS", [128, 4096])
        sst_l = [Buf("sstA", XS.t[:, 0:1024].rearrange("p (b n) -> p b n", b=8)), Buf("sstB", XS.t[:, 1024:2048].rearrange("p (b n) -> p b n", b=8))]
        sstb = Buf("sstb", XS.t[:, 2048:3072].bitcast(BF16).rearrange("p (b n) -> p b n", b=16))
        xT_A = xT; xT_B = Buf("xTB", XS.t[:, :].rearrange("p (k w) -> p k w", k=8))
        vtok_l = [P.sb(f"vtok{i}", [128, 5, 128], BF16) for i in range(2)]
        dtt = P.sb("dtt", [128, 5, 16]); dta = P.sb("dta", [128, 5, 16]); nacs = P.sb("nacs", [128, 5, 16])
        dtd = P.sb("dtd", [128, 5, 16]); cdp = P.sb("cdp", [128, 4, 8]); cds = P.sb("cds", [128, 8, 16])
        t16 = P.sb("t16", [128, 16]); t16b = P.sb("t16b", [128, 16]); dtab = P.sb("dtab", [128, 5, 2, 16], BF16)
        BCT = P.sb("BCT", [128, 4, WM], BF16)
        Btok = P.sb("Btok", [128, 5, 2, 128], BF16); cbs = P.sb("cbs", [128, 5, 2, 128])
        EXPB = Buf("EXPB", XS.t[:, 3072:4096].bitcast(BF16))
        xdt_l = [P.sb(f"xdt{i}", [128, 512], BF16) for i in range(2)]; xdtd_l = [P.sb(f"xdtd{i}", [128, 512], BF16) for i in range(2)]
        Eh = [P.sb(f"Eh{i}", [128, 2, 128]) for i in range(2)]
        Mh = [P.sb(f"Mh{i}", [128, 2, 128], BF16) for i in range(2)]
        Ein_l = [P.sb(f"Ein{i}", [128, 128]) for i in range(2)]; yis_l = [P.sb(f"yis{i}", [128, 128]) for i in range(2)]
        y1_l = [P.sb(f"y1{i}", [128, 128]) for i in range(2)]
        SCR = P.sb("SCR", [128, 1024])
        acsTb_l = [P.sb(f"acsTb{i}", [128, 2, 128], BF16) for i in range(2)]; totb = P.sb("totb", [128, 2, 16], BF16)
        rs = P.sb("rs", [128, WM]); rs2 = P.sb("rs2", [128, WM])
        qe_l = [P.sb(f"qe{i}", [128, WM], BF16) for i in range(2)]; ke_l = [P.sb(f"ke{i}", [128, WM], BF16) for i in range(2)]
        kdT_l = [P.sb(f"kdT{i}", [128, WM], BF16) for i in range(2)]
        attm_l = [P.sb(f"attm{i}", [128, 128], BF16) for i in range(2)]; kdtok_l = [P.sb(f"kdtok{i}", [128, 128], BF16) for i in range(2)]
        dch_l = [P.sb(f"dch{i}", [128, 24]) for i in range(2)]
        xTc = P.sub(xT, 8); xTc_A = xTc; xTc_B = P.sub(xT_B, 8); guard = {"b": []}; FAc = P.sub(FA, 8); FBc = P.sub(FB, 4); BBc = P.sub(BIGB, 22); SCRh = P.sub(SCR, 2)
        ctr = {"xd": 0, "at": 0}
        S_hgc = P.sub(S_hg, 8); Shq = [P.sub(Shb2, 2), P.sub(Shb2, 2)]; S_ssmc = P.sub(S_ssm, 8); hTbc = P.sub(hTb, 8)
        k.mmps = [P.ps(f"pmm{i}", [128, 512]) for i in range(7)]
        k.auxps = k.mmps
        pb = P.ps("pauxb", [128, 1024], BF16)
        k.wsetup(4)

        g_c = [P.group("c0"), P.group("c1"), P.group("c2"), P.group("c3")]; g_x = [P.group("xin0"), P.group("xin1")]; g_p = [P.group("pin0"), P.group("pin1")]
        g_y = [P.group("yout0"), P.group("yout1")]; g_cvi = P.group("cvi"); g_cvo = P.group("cvo"); g_ss_l = [P.group("sstA"), P.group("sstB")]; g_yx = [P.group("yx0"), P.group("yx1")]
        g_so = P.group("stout")
        out_groups = g_y + g_yx + g_ss_l + [g_cvo, g_so]

        def ffn_blocks(wu, wd):
            bl = []
            for b in range(11):
                bl.append((8, 512, [(0, 256, wu[:, b * 256:(b + 1) * 256]), (256, 256, wu[:, 2816 + b * 256:2816 + (b + 1) * 256])]))
            for fo in range(8):
                bl.append((22, 128, [(0, 128, wd[:, fo * 128:(fo + 1) * 128])]))
            return bl

        def tile_blocks():
            bl = ffn_blocks(wf1u, wf1d)
            bl.append((8, 16, [(0, 16, win[:, WI_DT:WI_DT + 16])]))
            bl.append((8, 512, [(0, 512, win[:, WI_B:WI_B + 512])]))
            for kind, idx in MIX_ORDER:
                if kind == "gp":
                    g = idx
                    bl.append((8, 512, [(0, 512, win[:, WI_Z + g * 512:WI_Z + (g + 1) * 512])]))
                    bl.append((8, 512, [(0, 512, win[:, WI_XS + g * 512:WI_XS + (g + 1) * 512])]))
                elif kind == "h":
                    h = idx
                    bl.append((8, 512, [(0, 128, win[:, WI_Q + h * 128:WI_Q + (h + 1) * 128]),
                                        (128, 128, win[:, WI_FR + h * 128:WI_FR + (h + 1) * 128]),
                                        (256, 128, win[:, WI_OG + h * 128:WI_OG + (h + 1) * 128]),
                                        (384, 128, win[:, WI_IV + h * 128:WI_IV + (h + 1) * 128])]))
            for b in range(4):
                bl.append((16, 256, [(0, 256, wout[:, b * 256:(b + 1) * 256])]))
            bl += ffn_blocks(wf2u, wf2d)
            bl.append((2, 1024, [(0, 1024, wpp[:, :])]))
            for b in range(2):
                bl.append((8, 512, [(0, 512, wpg[:, b * 512:(b + 1) * 512])]))
            return bl

        NT = 4
        POOL_FK, POOL_M, POOL_YG, POOL_TAP0, POOL_QE = PF
        ACT_TAP0, DVE_KD, DVE_O = XF
        NATIVE_SIG = NSIG
        MIX_ORDER = [("gp", 0), ("gb", 0), ("gp", 1), ("gb", 1)] + [("h", h) for h in range(8)]
        for _ in range(NT):
            k.wblocks += tile_blocks()

        k.load(g_c[0], cst[:], cstd[:, C_ID:C_ID + 128], [cst]); k.load(g_c[3], cstb[:], cstd, [cstb], eng="pool"); k.load(g_c[1], par[:], pard, [par]); k.load(g_c[2], bcr[:], bcrd, [bcr])
        k.wissue(k.NSLOT)
        k.tt(lbt[:], par[:, P_LB0:P_LB0 + 8], par[:, P_LB1:P_LB1 + 8], ALU.subtract, [par], [lbt])
        k.act(lbt[:], lbt[:], AF.Sigmoid, [lbt], [lbt])
        k.ts(oml[:], lbt[:], -1.0, ALU.mult, [lbt], [oml], s2=1.0, op1=ALU.add)
        k.ts(noml[:], oml[:], -1.0, ALU.mult, [oml], [noml])
        k.act(abc[:], bcr[:, 16:32], AF.Exp, [bcr], [abc])
        k.ts(abc[:], abc[:], -1.0, ALU.mult, [abc], [abc])
        k.memset(hist[:], 0.0, [hist]); k.memset(S_ssm[:], 0.0, S_ssmc); k.memset(S_hg[:], 0.0, S_hgc)

        SelL = SCR[:, :].bitcast(BF16)
        k.vcopy(SelL[0:16, :].rearrange("p (h s) -> p h s", s=128), cstb[0:16, C_ID:C_ID + 16].unsqueeze(2).to_broadcast([16, 16, 128]), [cstb], SCRh)
        ident = cst[:, 0:128]
        identb = cstb[:, C_ID:C_ID + 128]
        onesb = cstb[:, C_ONES:C_ONES + 128]

        def colranges(W):
            return [(0, 512)] if W == 512 else [(0, 288), (288, W)]

        def linear(wv, slot, wc0, rhs, rhs_buf, KC, c0, c1):
            ps = k.pmm()
            for kc in range(KC):
                rb = rhs_buf[kc] if isinstance(rhs_buf, list) else rhs_buf
                k.mm(ps[:, 0:c1 - c0], wv[:, kc, wc0:wc0 + 128], rhs[:, kc, c0:c1], [slot, rb], [ps],
                     start=(kc == 0), stop=(kc == KC - 1), sig=(kc == KC - 1))
            return ps

        def rmsnorm_stats(src, srcb, nchunk, W, sqdst_c0, inv_n, out_rs):
            for i in range(nchunk):
                k.act(BIGB[:, sqdst_c0 + i, 0:W], src[:, i, 0:W], AF.Square, [srcb[i]], [BBc[sqdst_c0 + i]])
            for (c0, c1) in colranges(W):
                ps = k.pmm()
                for i in range(nchunk):
                    k.mm(ps[:, 0:c1 - c0], onesb, BIGB[:, sqdst_c0 + i, c0:c1], [cstb, BBc[sqdst_c0 + i]], [ps],
                         start=(i == 0), stop=(i == nchunk - 1), sig=(i == nchunk - 1))
                k.act(out_rs[:, c0:c1], ps[:, 0:c1 - c0], AF.Ln, [ps], [out_rs], bias=EPS, scale=inv_n)
                k.free(ps)
            k.act(out_rs[:, 0:W], out_rs[:, 0:W], AF.Exp, [out_rs], [out_rs], scale=-0.5)

        def norm_to_xn(pcol, W, sq0=0):
            rmsnorm_stats(xT, xTc, 8, W, sq0, 1.0 / 1024.0, rs)
            for i in range(8):
                k.stt(xn[:, i, 0:W], xT[:, i, 0:W], par[:, pcol + i:pcol + i + 1], rs[:, 0:W], ALU.mult, ALU.mult,
                      [xTc[i], par, rs], [xn])

        def ffn(pcol, W, sq0=0):
            norm_to_xn(pcol, W, sq0)
            hid = BIGB
            tgl = 0
            for b in range(11):
                slot, wv = k.wnext()
                for jj in range(2):
                    j = 2 * b + jj
                    for (c0, c1) in colranges(W):
                        n = c1 - c0
                        pg = linear(wv, slot, jj * 128, xn, xn, 8, c0, c1)
                        pu = linear(wv, slot, 256 + jj * 128, xn, xn, 8, c0, c1)
                        sg = FA[:, tgl, 0:n]; sgb = FAc[tgl]; tgl ^= 1
                        k.act(sg, pg[:, 0:n], AF.Silu, [pg], [sgb])
                        k.tt(hid[:, j, c0:c1], sg, pu[:, 0:n], ALU.mult, [sgb, pu], [BBc[j]])
                        k.free(pg, pu)
            for fo in range(8):
                slot, wv = k.wnext()
                for (c0, c1) in colranges(W):
                    n = c1 - c0
                    ps = linear(wv, slot, 0, hid, BBc, 22, c0, c1)
                    k.stt(xT[:, fo, c0:c1], ps[:, 0:n], 0.5, xT[:, fo, c0:c1], ALU.mult, ALU.add, [ps, xTc[fo]], [xTc[fo]])
                    k.free(ps)

        def load_tile(ti, W):
            blocks = [(128, i * 128, xp[ti * 512 + i * 128: ti * 512 + (i + 1) * 128, :], pp[ti * 512 + i * 128: ti * 512 + (i + 1) * 128, :]) for i in range(4)]
            if W > 512:
                blocks.append((64, 512, xsm, psm))
            for bi, (R, c0, xsrc, psrc) in enumerate(blocks):
                xb = xin[bi % 2]; pbuf = pin[0]
                k.load(g_x[bi % 2], xb[0:R, :], xsrc, [xb])
                k.load(g_p[0], pbuf[0:R, :], psrc, [pbuf])
                for half in range(2):
                    ps = k.paux()
                    for q in range(4):
                        kc = half * 4 + q
                        k.tr(ps[:, q * 128:q * 128 + R], xb[0:R, kc * 128:(kc + 1) * 128], ident[0:R, 0:R], [xb, cst], [ps], sig=(q == 3))
                    k.acopy(xT[:, half * 4:half * 4 + 4, c0:c0 + R],
                            ps[:, :].rearrange("p (q r) -> p q r", q=4)[:, :, 0:R], [ps], xTc[half * 4:half * 4 + 4] + guard["b"])
                    k.free(ps)
                ps = k.paux()
                for q in range(2):
                    k.tr(ps[:, q * 128:q * 128 + R], pbuf[0:R, q * 128:(q + 1) * 128], ident[0:R, 0:R], [pbuf, cst], [ps], sig=(q == 1))
                k.vcopy(pT[:, :, c0:c0 + R], ps[:, 0:256].rearrange("p (q r) -> p q r", q=2)[:, :, 0:R], [ps], [pT])
                k.free(ps)

        def mixer(ti, W):
            has_s = W > 512
            first = (ti == 0)
            last = (ti == NT - 1)
            blocks = [(128, i * 128, i) for i in range(4)]
            if has_s:
                blocks.append((64, 512, 4))
            crs = colranges(W)
            mixed = BIGB
            norm_to_xn(P_NMIX, W)

            slot, wv = k.wnext()
            for (R, c0, bi) in blocks:
                smp = (bi == 4)
                ps = k.paux()
                for kc in range(8):
                    k.mm(ps[0:R, 0:16], xn[:, kc, c0:c0 + R], wv[:, kc, 0:16], [xn, slot], [ps], start=(kc == 0), stop=(kc == 7), sig=(kc == 7))
                k.tt(t16[0:R, :], ps[0:R, 0:16], bcr[0:R, 0:16], ALU.add, [ps, bcr], [t16])
                k.free(ps)
                k.act(t16[0:R, :], t16[0:R, :], AF.Exp, [t16], [t16])
                k.act(dtt[0:R, bi, :], t16[0:R, :], AF.Ln, [t16], [dtt], bias=1.0)
                k.tt(dta[0:R, bi, :], dtt[0:R, bi, :], abc[0:R, :], ALU.mult, [dtt, abc], [dta])
                k.vcopy(dtab[0:R, bi, 0, :], dta[0:R, bi, :], [dta], [dtab])
                k.tt(dtab[0:R, bi, 1, :], dta[0:R, bi, :], dtab[0:R, bi, 0, :], ALU.subtract, [dta, dtab], [dtab])
                Lm = cstb[0:R, C_LS:C_LS + R] if smp else cstb[0:R, C_L:C_L + R]
                On = cstb[0:R, C_ONES_S:C_ONES_S + R] if smp else cstb[0:R, C_ONES:C_ONES + R]
                ps2 = k.paux()
                k.mm(ps2[0:R, 0:16], Lm, dtab[0:R, bi, 0, :], [cstb, dtab], [ps2], start=True, stop=False, sig=False)
                k.mm(ps2[0:R, 0:16], Lm, dtab[0:R, bi, 1, :], [cstb, dtab], [ps2], start=False, stop=True, sig=False)
                k.mm(ps2[0:R, 16:32], On, dtab[0:R, bi, 0, :], [cstb, dtab], [ps2], start=True, stop=False, sig=False)
                k.mm(ps2[0:R, 16:32], On, dtab[0:R, bi, 1, :], [cstb, dtab], [ps2], start=False, stop=True)
                k.ts(nacs[0:R, bi, :], ps2[0:R, 0:16], -1.0, ALU.mult, [ps2], [nacs])
                k.tt(t16b[0:R, :], ps2[0:R, 16:32], nacs[0:R, bi, :], ALU.add, [ps2, nacs], [t16b])
                k.free(ps2)
                k.act(t16b[0:R, :], t16b[0:R, :], AF.Exp, [t16b], [t16b])
                k.tt(dtd[0:R, bi, :], dtt[0:R, bi, :], t16b[0:R, :], ALU.mult, [dtt, t16b], [dtd])
                ncol = 16 if smp else 2
                rhs_t = cstb[0:R, C_SEG:C_SEG + 16] if smp else cstb[0:R, C_ONES:C_ONES + 2]
                ptot = k.paux()
                k.mm(ptot[0:16, 0:ncol], dtab[0:R, bi, 0, :], rhs_t, [dtab, cstb], [ptot], start=True, stop=False, sig=False)
                k.mm(ptot[0:16, 0:ncol], dtab[0:R, bi, 1, :], rhs_t, [dtab, cstb], [ptot], start=False, stop=True)
                k.vcopy(totb[0:16, 0, 0:ncol], ptot[0:16, 0:ncol], [ptot], [totb])
                k.tt(totb[0:16, 1, 0:ncol], ptot[0:16, 0:ncol], totb[0:16, 0, 0:ncol], ALU.subtract, [ptot, totb], [totb])
                k.free(ptot)
                ps3 = k.paux()
                for h in range(16):
                    for part in range(2):
                        k.mm(ps3[(h % 2) * 64:(h % 2) * 64 + 64, (h // 2) * ncol:(h // 2) * ncol + ncol], SelL[0:16, h * 128:h * 128 + 64], totb[0:16, part, 0:ncol],
                             SCRh + [totb], [ps3], start=(part == 0), stop=(part == 1), sig=(h == 15 and part == 1))
                if not smp:
                    k.act(cdp[:, bi, :].unsqueeze(2), ps3[:, 0:16].rearrange("p (a b) -> p a b", b=2)[:, :, 0:1], AF.Exp, [ps3], [cdp])
                else:
                    k.act(cds[:, :, :], ps3[:, 0:128].rearrange("p (a b) -> p a b", b=16), AF.Exp, [ps3], [cds])
                k.free(ps3)

            def raw_sample_view(i):
                return FB[:, i, 515:627].rearrange("p (b j) -> p b j", j=7)

            def conv_stage(wv, slot, cis, dst, dstbufs, dst_c0):
                for i in range(4):
                    ci = cis[i]
                    fbb = FBc[i]; dstb = dstbufs[dst_c0 + i]
                    if has_s:
                        k.load(g_cvi, cvi[0:48, :], cv0[:, ci * 128:(ci + 1) * 128], [cvi])
                    for (c0, c1) in crs:
                        ps = linear(wv, slot, i * 128, xn, xn, 8, c0, c1)
                        p1 = min(c1, 512)
                        if p1 > c0:
                            k.acopy(FB[:, i, 3 + c0:3 + p1], ps[:, 0:p1 - c0], [ps], [fbb])
                        if c1 > 512:
                            k.acopy(raw_sample_view(i)[:, :, 3:7], ps[:, 512 - c0:576 - c0].rearrange("p (b j) -> p b j", j=4), [ps], [fbb])
                        k.free(ps)
                    k.vcopy(FB[:, i, 0:3], hist[:, ci, :], [hist], [fbb])
                    k.vcopy(hist[:, ci, :], FB[:, i, 512:515], [fbb], [hist])
                    if last:
                        pst = k.paux()
                        k.tr(pst[0:3, 0:128], hist[:, ci, :], ident, [hist, cst], [pst])
                        k.acopy(cvo[0:3, :], pst[0:3, 0:128], [pst], [cvo])
                        k.free(pst)
                        k.store(g_cvo, cvp[:, ci * 128:(ci + 1) * 128], cvo[0:3, :], [cvo])
                    if has_s:
                        pst = k.paux()
                        k.tr(pst[:, 0:48], cvi[0:48, :], ident[0:48, 0:48], [cvi, cst], [pst])
                        k.vcopy(raw_sample_view(i)[:, :, 0:3], pst[:, 0:48].rearrange("p (b j) -> p b j", j=3), [pst], [fbb])
                        k.free(pst)
                        k.vcopy(cvt[:, :].rearrange("p (b j) -> p b j", j=3), raw_sample_view(i)[:, :, 4:7], [fbb], [cvt])
                        pst = k.paux()
                        k.tr(pst[0:48, 0:128], cvt[:, :], ident, [cvt, cst], [pst])
                        k.acopy(cvo[0:48, :], pst[0:48, 0:128], [pst], [cvo])
                        k.free(pst)
                        k.store(g_cvo, cvs[:, ci * 128:(ci + 1) * 128], cvo[0:48, :], [cvo])
                    d = dst[:, dst_c0 + i, 0:512]
                    if ACT_TAP0:
                        k.act(d, FB[:, i, 0:512], AF.Identity, [fbb, par], [dstb], bias=par[:, P_CB + ci:P_CB + ci + 1], scale=par[:, P_CW + ci:P_CW + ci + 1])
                    else:
                        k.ts(d, FB[:, i, 0:512], par[:, P_CW + ci:P_CW + ci + 1], ALU.mult, [fbb, par], [dstb],
                             s2=par[:, P_CB + ci:P_CB + ci + 1], op1=ALU.add, eng=("pool" if POOL_TAP0 else "dve"))
                    for j in range(1, 4):
                        k.stt(d, FB[:, i, j:j + 512], par[:, P_CW + j * 12 + ci:P_CW + j * 12 + ci + 1], d, ALU.mult, ALU.add,
                              [fbb, par, dstb], [dstb])
                    if has_s:
                        d = dst[:, dst_c0 + i, 512:576].rearrange("p (b j) -> p b j", j=4)
                        rv = raw_sample_view(i)
                        k.ts(d, rv[:, :, 0:4], par[:, P_CW + ci:P_CW + ci + 1], ALU.mult, [fbb, par], [dstb],
                             s2=par[:, P_CB + ci:P_CB + ci + 1], op1=ALU.add)
                        for j in range(1, 4):
                            k.stt(d, rv[:, :, j:j + 4], par[:, P_CW + j * 12 + ci:P_CW + j * 12 + ci + 1], d, ALU.mult, ALU.add,
                                  [fbb, par, dstb], [dstb])

            slot, wv = k.wnext()
            conv_stage(wv, slot, [8, 9, 10, 11], FA, FAc, 0)
            for i in range(4):
                k.act(BCT[:, i, 0:W], FA[:, i, 0:W], AF.Silu, [FAc[i]], [BCT])
            for (R, c0, bi) in blocks:
                for g in range(2):
                    k.tr(pb[0:R, g * 128:(g + 1) * 128], BCT[:, g, c0:c0 + R], identb, [BCT, cstb], [pb], sig=(g == 1))
                k.acopy(Btok[0:R, bi, :, :], pb[0:R, 0:256].rearrange("p (g n) -> p g n", g=2), [pb], [Btok])
                ps = k.paux()
                for g in range(2):
                    k.mm(ps[0:R, g * 128:g * 128 + R], BCT[:, g, c0:c0 + R], BCT[:, 2 + g, c0:c0 + R], [BCT], [ps], sig=(g == 1))
                k.acopy(cbs[0:R, bi, :, 0:R], ps[0:R, 0:256].rearrange("p (g n) -> p g n", g=2)[:, :, 0:R], [ps], [cbs])
                k.free(ps)

            def ssd_proj(g):
                slot, wv = k.wnext()
                for i in range(4):
                    for (c0, c1) in crs:
                        ps = linear(wv, slot, i * 128, xn, xn, 8, c0, c1)
                        k.act(FA[:, i, c0:c1], ps[:, 0:c1 - c0], AF.Silu, [ps], [FAc[i]])
                        k.free(ps)
                slot, wv = k.wnext()
                conv_stage(wv, slot, [4 * g + i for i in range(4)], FA, FAc, 4)
                for i in range(4):
                    k.act(FA[:, 4 + i, 0:W], FA[:, 4 + i, 0:W], AF.Silu, [FAc[4 + i]], [FAc[4 + i]])
                if has_s:
                    k.tt(EXPB[0:64, :].rearrange("p (b n) -> p b n", b=16),
                         Btok[0:64, 4, g, :].unsqueeze(1).to_broadcast([64, 16, 128]),
                         cstb[0:64, C_SEG:C_SEG + 16].unsqueeze(2).to_broadcast([64, 16, 128]), ALU.mult, [Btok, cstb], [EXPB])
            def ssd_blocks(g):
                for (R, c0, bi) in blocks:
                    smp = (bi == 4)
                    have_state = smp or (not first) or (bi > 0)
                    pt = k.paux()
                    for i in range(4):
                        k.tr(pt[0:R, i * 128:(i + 1) * 128], FA[:, 4 + i, c0:c0 + R], ident, [FAc[4 + i], cst], [pt], sig=(i == 3))
                    xdt = xdt_l[ctr["xd"] % 2]; xdtd = xdtd_l[ctr["xd"] % 2]; ctr["xd"] += 1
                    k.tt(xdt[0:R, :].rearrange("p (h q) -> p h q", h=8), pt[0:R, :].rearrange("p (h q) -> p h q", h=8),
                         dtt[0:R, bi, 8 * g:8 * g + 8].unsqueeze(2).to_broadcast([R, 8, 64]), ALU.mult, [pt, dtt], [xdt])
                    k.tt(xdtd[0:R, :].rearrange("p (h q) -> p h q", h=8), pt[0:R, :].rearrange("p (h q) -> p h q", h=8),
                         dtd[0:R, bi, 8 * g:8 * g + 8].unsqueeze(2).to_broadcast([R, 8, 64]), ALU.mult, [pt, dtd], [xdtd])
                    k.free(pt)
                    Lm = cstb[0:R, C_LS:C_LS + R] if smp else cstb[0:R, C_L:C_L + R]
                    acsb = acsTb_l[ctr["xd"] % 2]
                    pacs = k.paux()
                    k.mm(pacs[0:16, 0:R], dtab[0:R, bi, 0, :], Lm, [dtab, cstb], [pacs], start=True, stop=False, sig=False)
                    k.mm(pacs[0:16, 0:R], dtab[0:R, bi, 1, :], Lm, [dtab, cstb], [pacs], start=False, stop=True)
                    k.vcopy(acsb[0:16, 0, 0:R], pacs[0:16, 0:R], [pacs], [acsb])
                    k.tt(acsb[0:16, 1, 0:R], pacs[0:16, 0:R], acsb[0:16, 0, 0:R], ALU.subtract, [pacs, acsb], [acsb])
                    k.free(pacs)
                    ngm = cstb[0:R, C_NEGMS:C_NEGMS + R] if smp else cstb[0:R, C_NEGM:C_NEGM + R]
                    for hp in range(4):
                        hp2 = 4 * g + hp
                        xsv = FA[:, 4 + hp, c0:c0 + R]
                        szv = FA[:, hp, c0:c0 + R]
                        ygv = FB[:, hp, c0:c0 + R]
                        dcol = par[:, P_DSK + hp2:P_DSK + hp2 + 1]
                        Ein = Ein_l[hp % 2]; yis = yis_l[hp % 2]; y1 = y1_l[hp % 2]
                        if smp:
                            for hb in range(2):
                                sst = sst_l[hb]
                                k.load(g_ss_l[hb], sst[:, :, :], ssm0[8 * hb:8 * hb + 8, hp2 * 128:(hp2 + 1) * 128, :].rearrange("b q n -> q b n"), [sst])
                                for bq in range(2):
                                    ptq = k.pmm()
                                    for b4 in range(4):
                                        k.tr(ptq[:, b4 * 128:(b4 + 1) * 128], sst[:, bq * 4 + b4, :], ident, [sst, cst], [ptq], sig=(b4 == 3))
                                    k.acopy(sstb[:, 8 * hb + bq * 4:8 * hb + bq * 4 + 4, :], ptq[:, :].rearrange("p (b n) -> p b n", b=4), [ptq], [sstb])
                                    k.free(ptq)
                        yp_ = k.pmm()
                        for q in range(2):
                            hl = 2 * hp + q
                            h = 8 * g + hl
                            pe_ = k.paux()
                            k.mm(pe_[0:R, 0:R], SelL[0:16, h * 128:h * 128 + R], acsb[0:16, 0, 0:R], SCRh + [acsb], [pe_], start=True, stop=False, sig=False)
                            k.mm(pe_[0:R, 0:R], SelL[0:16, h * 128:h * 128 + R], acsb[0:16, 1, 0:R], SCRh + [acsb], [pe_], start=False, stop=False, sig=False)
                            k.mm(pe_[0:R, 0:R], identb[0:R, 0:R], ngm, [cstb], [pe_], start=False, stop=True)
                            E = Eh[hp % 2]; M = Mh[hp % 2]
                            k.act(E[0:R, q, 0:R], pe_[0:R, 0:R], AF.Exp, [pe_, nacs], [E], bias=nacs[0:R, bi, h:h + 1])
                            k.free(pe_)
                        k.tt(M[0:R, :, 0:R], E[0:R, :, 0:R], cbs[0:R, bi, g, 0:R].unsqueeze(1).to_broadcast([R, 2, R]), ALU.mult, [E, cbs], [M], eng=("pool" if POOL_M else "dve"))
                        for q in range(2):
                            hl = 2 * hp + q
                            k.mm(yp_[q * 64:q * 64 + 64, 0:R], xdt[0:R, hl * 64:(hl + 1) * 64], M[0:R, q, 0:R], [xdt, M], [yp_], sig=True)
                        if have_state:
                            pyi = k.paux()
                            if not smp:
                                k.mm(pyi[:, 0:R], hTb[:, hp2, :], BCT[:, 2 + g, c0:c0 + R], [hTbc[hp2], BCT], [pyi])
                            else:
                                for b in range(16):
                                    k.mm(pyi[:, 4 * b:4 * b + 4], sstb[:, b, :], BCT[:, 2 + g, 512 + 4 * b:512 + 4 * b + 4], [sstb, BCT], [pyi], sig=(b == 15))
                            pe2 = k.paux()
                            for q in range(2):
                                hq = 2 * hp2 + q
                                k.mm(pe2[q * 64:q * 64 + 64, 0:R], SelL[0:16, hq * 128:hq * 128 + 64], acsb[0:16, 0, 0:R], SCRh + [acsb], [pe2], start=True, stop=False, sig=False)
                                k.mm(pe2[q * 64:q * 64 + 64, 0:R], SelL[0:16, hq * 128:hq * 128 + 64], acsb[0:16, 1, 0:R], SCRh + [acsb], [pe2], start=False, stop=True, sig=(q == 1))
                            k.act(Ein[:, 0:R], pe2[:, 0:R], AF.Exp, [pe2], [Ein])
                            k.free(pe2)
                            k.tt(yis[:, 0:R], pyi[:, 0:R], Ein[:, 0:R], ALU.mult, [pyi, Ein], [yis])
                            k.free(pyi)
                            k.tt(y1[:, 0:R], yp_[:, 0:R], yis[:, 0:R], ALU.add, [yp_, yis], [y1])
                            k.stt(y1[:, 0:R], xsv, dcol, y1[:, 0:R], ALU.mult, ALU.add, [FAc[4 + hp], par, y1], [y1])
                        else:
                            k.stt(y1[:, 0:R], xsv, dcol, yp_[:, 0:R], ALU.mult, ALU.add, [FAc[4 + hp], par, yp_], [y1])
                        k.free(yp_)
                        k.tt(ygv, y1[:, 0:R], szv, ALU.mult, [y1, FAc[hp]], [FBc[hp]], eng=("pool" if POOL_YG else "dve"))
                        if not smp:
                            pS = k.paux()
                            k.mm(pS[:, 0:128], xdtd[0:128, hp * 128:(hp + 1) * 128], Btok[:, bi, g, :], [xdtd, Btok], [pS])
                            k.stt(S_ssm[:, hp2, :], S_ssm[:, hp2, :], cdp[:, bi, hp2:hp2 + 1], pS[:, 0:128], ALU.mult, ALU.add, [S_ssmc[hp2], cdp, pS], [S_ssmc[hp2]])
                            k.free(pS)
                        else:
                            for hb in range(2):
                                sst = sst_l[hb]
                                k.tt(sst[:, :, :], sst[:, :, :], cds[:, hp2, 8 * hb:8 * hb + 8].unsqueeze(2).to_broadcast([128, 8, 128]), ALU.mult, [sst, cds], [sst])
                                for bq in range(2):
                                    pS = k.pmm()
                                    e0 = (8 * hb + 4 * bq) * 128
                                    k.mm(pS[:, 0:512], xdtd[0:64, hp * 128:(hp + 1) * 128], EXPB[0:64, e0:e0 + 512], [xdtd, EXPB], [pS])
                                    k.tt(sst[:, bq * 4:bq * 4 + 4, :], sst[:, bq * 4:bq * 4 + 4, :], pS[:, :].rearrange("p (b n) -> p b n", b=4), ALU.add, [sst, pS], [sst])
                                    k.free(pS)
                                k.store(g_ss_l[hb], ssms[8 * hb:8 * hb + 8, hp2 * 128:(hp2 + 1) * 128, :].rearrange("b q n -> q b n"), sst[:, :, :], [sst])
                    if not smp and not (last and bi == 3):
                        ptq = k.pmm()
                        for hp in range(4):
                            k.tr(ptq[:, hp * 128:(hp + 1) * 128], S_ssm[:, 4 * g + hp, :], ident, [S_ssmc[4 * g + hp], cst], [ptq], sig=(hp == 3))
                        k.acopy(hTb[:, 4 * g:4 * g + 4, :], ptq[:, :].rearrange("p (b n) -> p b n", b=4), [ptq], hTbc[4 * g:4 * g + 4])
                        k.free(ptq)
                rmsnorm_stats(FB, FBc, 4, W, 16, 1.0 / 512.0, rs2)
                for i in range(4):
                    k.stt(mixed[:, 4 * g + i, 0:W], FB[:, i, 0:W], par[:, P_SSDN + 4 * g + i:P_SSDN + 4 * g + i + 1], rs2[:, 0:W],
                          ALU.mult, ALU.mult, [FBc[i], par, rs2], [BBc[4 * g + i]])

            a = [FA[:, i, 0:W] for i in range(8)]
            A = FAc
            scan_mask = cstb[:, C_SCAN:C_SCAN + W]
            def hgrn_head(h):
                slot, wv = k.wnext()
                qe = qe_l[h % 2]; ke = ke_l[h % 2]; kdT = kdT_l[h % 2]; dch = dch_l[h % 2]; vtok = vtok_l[h % 2]
                i0, i1, i2, i3 = 4 * (h % 2), 4 * (h % 2) + 1, 4 * (h % 2) + 2, 4 * (h % 2) + 3
                hpar = h % 2; st = {"pp": 0}
                if not first:
                    k.acopy(Shb2[:, hpar, 0, :], S_hg[:, h, :], [S_hgc[h]], [Shq[hpar][0]])
                fa = lambda i_: FA[:, i_, 0:W]
                ob = FBc[h % 2]; o_ap = FB[:, h % 2, 0:W]
                sgb = FBc[2 + h % 2]; sog = FB[:, 2 + h % 2, 0:W]
                for (R, c0, bi) in blocks:
                    ps = k.pmm()
                    for kc in range(8):
                        k.mm(ps[0:R, 0:128], xn[:, kc, c0:c0 + R], wv[:, kc, 384:512], [xn, slot], [ps], start=(kc == 0), stop=(kc == 7), sig=(kc == 7))
                    k.vcopy(vtok[0:R, bi, :], ps[0:R, 0:128], [ps], [vtok])
                    k.free(ps)
                def el_sigmoid(dst_T, dst_ci, dst_buf, wi, keep):
                    live = []
                    for (c0, c1) in crs:
                        ps = linear(wv, slot, wi * 128, xn, xn, 8, c0, c1)
                        if NATIVE_SIG:
                            k.act(dst_T[:, dst_ci, c0:c1], ps[:, 0:c1 - c0], AF.Sigmoid, [ps], [dst_buf])
                        else:
                            k.act(dst_T[:, dst_ci, c0:c1], ps[:, 0:c1 - c0], AF.Exp, [ps], [dst_buf], scale=-1.0)
                        if keep:
                            live.append((ps, c0, c1))
                        else:
                            k.free(ps)
                    if not NATIVE_SIG:
                        dd = dst_T[:, dst_ci, 0:W]
                        k.act(dd, dd, AF.Ln, [dst_buf], [dst_buf], bias=1.0)
                        k.act(dd, dd, AF.Exp, [dst_buf], [dst_buf], scale=-1.0)
                    return live

                el_sigmoid(FA, i1, A[i1], 1, False)
                k.ts(fa(i2), fa(i1), oml[:, h:h + 1], ALU.mult, [A[i1], oml, lbt], [A[i2]], s2=lbt[:, h:h + 1], op1=ALU.add, eng=("pool" if POOL_FK else "dve"))
                k.act(fa(i2), fa(i2), AF.Ln, [A[i2]], [A[i2]])
                P.op("dve", (lambda o, m, d_: (lambda e: e.tensor_tensor_scan(out=o, data0=m, data1=d_, initial=0.0, op0=ALU.mult, op1=ALU.add)))(fa(i3), scan_mask, fa(i2)),
                     reads=[cstb, A[i2]], writes=[A[i3]], cost=120.0 + 2.2 * W)
                k.ts(fa(i1), fa(i1), noml[:, h:h + 1], ALU.mult, [A[i1], noml, oml], [A[i1]], s2=oml[:, h:h + 1], op1=ALU.add, eng=("pool" if POOL_FK else "dve"))
                for (ps, c0, c1) in el_sigmoid(FA, i0, A[i0], 0, True):
                    k.tt(FA[:, i0, c0:c1], FA[:, i0, c0:c1], ps[:, 0:c1 - c0], ALU.mult, [A[i0], ps], [A[i0]])
                    k.free(ps)
                k.act(fa(i2), fa(i3), AF.Exp, [A[i3]], [A[i2]])
                k.tt(qe[:, 0:W], fa(i0), fa(i2), ALU.mult, [A[i0], A[i2]], [qe], eng=("pool" if POOL_QE else "dve"))
                k.act(fa(i0), fa(i3), AF.Exp, [A[i3]], [A[i0]], scale=-1.0)
                k.tt(fa(i0), fa(i0), fa(i1), ALU.mult, [A[i0], A[i1]], [A[i0]])
                k.acopy(ke[:, 0:W], fa(i0), [A[i0]], [ke])
                bc_p = FA[:, i3, 0:512].rearrange("p (c q) -> p c q", q=64)
                k.act(dch[:, 0:8].unsqueeze(2), bc_p[:, :, 63:64], AF.Exp, [A[i3]], [dch])
                k.tt(kdT[:, 0:512].rearrange("p (c q) -> p c q", q=64), FA[:, i0, 0:512].rearrange("p (c q) -> p c q", q=64),
                     dch[:, 0:8].unsqueeze(2).to_broadcast([128, 8, 64]), ALU.mult, [A[i0], dch], [kdT])
                if has_s:
                    bc_s = FA[:, i3, 512:576].rearrange("p (c q) -> p c q", q=4)
                    k.act(dch[:, 8:24].unsqueeze(2), bc_s[:, :, 3:4], AF.Exp, [A[i3]], [dch])
                    k.tt(kdT[:, 512:576].rearrange("p (c q) -> p c q", q=4), FA[:, i0, 512:576].rearrange("p (c q) -> p c q", q=4),
                         dch[:, 8:24].unsqueeze(2).to_broadcast([128, 16, 4]), ALU.mult, [A[i0], dch], [kdT])
                for (ps, c0, c1) in el_sigmoid(FB, 2 + h % 2, sgb, 2, True):
                    k.tt(FB[:, 2 + h % 2, c0:c1], FB[:, 2 + h % 2, c0:c1], ps[:, 0:c1 - c0], ALU.mult, [sgb, ps], [sgb])
                    k.free(ps)
                for (R, c0, bi) in blocks:
                    smp = (bi == 4)
                    attm = attm_l[ctr["at"] % 2]; kdtok = kdtok_l[ctr["at"] % 2]; ctr["at"] += 1
                    pa = k.paux()
                    k.mm(pa[0:R, 0:R], ke[:, c0:c0 + R], qe[:, c0:c0 + R], [ke, qe], [pa])
                    msk = cstb[0:R, C_LS:C_LS + R] if smp else cstb[0:R, C_MASK2:C_MASK2 + R]
                    k.tt(attm[0:R, 0:R], pa[0:R, 0:R], msk, ALU.mult, [pa, cstb], [attm])
                    k.free(pa)
                    k.tr(pb[0:R, 512:640], kdT[:, c0:c0 + R], identb, [kdT, cstb], [pb])
                    (k.vcopy if DVE_KD else k.acopy)(kdtok[0:R, :], pb[0:R, 512:640], [pb], [kdtok])
                    if not smp:
                        have0 = (not first) or (bi > 0)
                        pSs = []
                        for sc in range(2):
                            r0 = sc * 64
                            pS = k.paux()
                            k.mm(pS[:, 0:128], kdtok[r0:r0 + 64, :], vtok[r0:r0 + 64, bi, :], [kdtok, vtok], [pS])
                            pSs.append(pS)
                        po = k.paux()
                        k.mm(po[:, 0:128], vtok[0:128, bi, :], attm[:, :], [vtok, attm], [po], start=True, stop=False, sig=(not have0))
                        if have0:
                            k.mm(po[:, 0:64], Shb2[:, hpar, st["pp"], :], qe[:, c0:c0 + 64], [Shq[hpar][st["pp"]], qe], [po], start=False, stop=False, sig=True)
                        for sc in range(2):
                            dcol_ = dch[:, 2 * bi + sc:2 * bi + sc + 1]
                            if sc == 1:
                                k.mm(po[:, 64:128], Shb2[:, hpar, st["pp"], :], qe[:, c0 + 64:c0 + 128], [Shq[hpar][st["pp"]], qe], [po], start=False, stop=True, sig=True)
                            if not (last and bi == 3 and sc == 1):
                                k.stt(Shb2[:, hpar, 1 - st["pp"], :], S_hg[:, h, :], dcol_, pSs[sc][:, 0:128], ALU.mult, ALU.add, [S_hgc[h], dch, pSs[sc]], [Shq[hpar][1 - st["pp"]]])
                                st["pp"] ^= 1
                            k.stt(S_hg[:, h, :], S_hg[:, h, :], dcol_, pSs[sc][:, 0:128], ALU.mult, ALU.add, [S_hgc[h], dch, pSs[sc]], [S_hgc[h]])
                            k.free(pSs[sc])
                    else:
                        for hb in range(2):
                            k.load(g_ss_l[hb], sst_l[hb][:, :, :], hg0[8 * hb:8 * hb + 8, h * 128:(h + 1) * 128, :].rearrange("b q n -> q b n"), [sst_l[hb]])
                            k.acopy(sstb[:, 8 * hb:8 * hb + 8, :], sst_l[hb][:, :, :], [sst_l[hb]], [sstb])
                        po = k.paux()
                        k.mm(po[:, 0:64], vtok[0:64, 4, :], attm[0:64, 0:64], [vtok, attm], [po], start=True, stop=False, sig=False)
                        for b in range(16):
                            k.mm(po[:, 4 * b:4 * b + 4], sstb[:, b, :], qe[:, 512 + 4 * b:512 + 4 * b + 4], [sstb, qe], [po], start=False, stop=(b == 15), sig=(b == 15))
                        k.tt(EXPB[0:64, :].rearrange("p (b n) -> p b n", b=16),
                             vtok[0:64, 4, :].unsqueeze(1).to_broadcast([64, 16, 128]),
                             cstb[0:64, C_SEG:C_SEG + 16].unsqueeze(2).to_broadcast([64, 16, 128]), ALU.mult, [vtok, cstb], [EXPB])
                        for hb in range(2):
                            sst = sst_l[hb]
                            k.tt(sst[:, :, :], sst[:, :, :], dch[:, 8 + 8 * hb:16 + 8 * hb].unsqueeze(2).to_broadcast([128, 8, 128]), ALU.mult, [sst, dch], [sst])
                            for bq in range(2):
                                pS = k.pmm()
                                e0 = (8 * hb + 4 * bq) * 128
                                k.mm(pS[:, 0:512], kdtok[0:64, :], EXPB[0:64, e0:e0 + 512], [kdtok, EXPB], [pS])
                                k.tt(sst[:, bq * 4:bq * 4 + 4, :], sst[:, bq * 4:bq * 4 + 4, :], pS[:, :].rearrange("p (b n) -> p b n", b=4), ALU.add, [sst, pS], [sst])
                                k.free(pS)
                            k.store(g_ss_l[hb], hgs[8 * hb:8 * hb + 8, h * 128:(h + 1) * 128, :].rearrange("b q n -> q b n"), sst[:, :, :], [sst])
                    (k.vcopy if DVE_O else k.acopy)(FB[:, h % 2, c0:c0 + R], po[:, 0:R], [po], [ob])
                    k.free(po)
                k.act(BIGB[:, 16 + h % 2, 0:W], o_ap, AF.Square, [ob], [BBc[16 + h % 2]])
                rsh = rs2 if h % 2 == 0 else rs
                for (c0, c1) in crs:
                    ps = k.pmm()
                    k.mm(ps[:, 0:c1 - c0], onesb, BIGB[:, 16 + h % 2, c0:c1], [cstb, BBc[16 + h % 2]], [ps])
                    k.act(rsh[:, c0:c1], ps[:, 0:c1 - c0], AF.Ln, [ps], [rsh], bias=EPS, scale=1.0 / 128.0)
                    k.free(ps)
                k.act(rsh[:, 0:W], rsh[:, 0:W], AF.Exp, [rsh], [rsh], scale=-0.5)
                k.stt(o_ap, o_ap, par[:, P_HGN + h:P_HGN + h + 1], rsh[:, 0:W], ALU.mult, ALU.mult, [ob, par, rsh], [ob])
                k.tt(mixed[:, 8 + h, 0:W], o_ap, sog, ALU.mult, [ob, sgb], [BBc[8 + h]])
            for kind, idx in MIX_ORDER:
                {"gp": ssd_proj, "gb": ssd_blocks, "h": hgrn_head}[kind](idx)
            if last:
                k.store(g_so, ssmp.rearrange("(a q) n -> q a n", q=128), S_ssm[:, :, :], S_ssmc)
                k.store(g_so, hgp.rearrange("(a q) n -> q a n", q=128), S_hg[:, :, :], S_hgc)

            for b in range(4):
                slot, wv = k.wnext()
                for f2 in range(2):
                    fo = 2 * b + f2
                    for (c0, c1) in crs:
                        ps = linear(wv, slot, f2 * 128, mixed, BBc, 16, c0, c1)
                        k.tt(xT[:, fo, c0:c1], xT[:, fo, c0:c1], ps[:, 0:c1 - c0], ALU.add, [xTc[fo], ps], [xTc[fo]])
                        k.free(ps)

        def ple_and_out(ti, W):
            crs = colranges(W)
            norm_to_xn(P_NPLE, W)
            slot, wv = k.wnext()
            for fo in range(8):
                for (c0, c1) in crs:
                    ps = linear(wv, slot, fo * 128, pT, pT, 2, c0, c1)
                    k.acopy(FA[:, fo, c0:c1], ps[:, 0:c1 - c0], [ps], [FAc[fo]])
                    k.free(ps)
            rmsnorm_stats(FA, FAc, 8, W, 0, 1.0 / 1024.0, rs2)
            for b in range(2):
                slot, wv = k.wnext()
                for f4 in range(4):
                    fo = 4 * b + f4
                    ge, en = 2 * (fo % 2), 2 * (fo % 2) + 1
                    k.stt(FB[:, en, 0:W], FA[:, fo, 0:W], par[:, P_PPOST + fo:P_PPOST + fo + 1], rs2[:, 0:W], ALU.mult, ALU.mult, [FAc[fo], par, rs2], [FBc[en]])
                    for (c0, c1) in crs:
                        ps = linear(wv, slot, f4 * 128, xn, xn, 8, c0, c1)
                        k.act(FB[:, ge, c0:c1], ps[:, 0:c1 - c0], AF.Sigmoid, [ps], [FBc[ge]])
                        k.free(ps)
                    k.tt(FB[:, en, 0:W], FB[:, en, 0:W], FB[:, ge, 0:W], ALU.mult, [FBc[en], FBc[ge]], [FBc[en]])
                    k.tt(xT[:, fo, 0:W], xT[:, fo, 0:W], FB[:, en, 0:W], ALU.add, [xTc[fo], FBc[en]], [xTc[fo]])
            rmsnorm_stats(xT, xTc, 8, W, 0, 1.0 / 1024.0, rs2)
            for i in range(8):
                k.stt(FA[:, i, 0:W], xT[:, i, 0:W], par[:, P_NFIN + i:P_NFIN + i + 1], rs2[:, 0:W], ALU.mult, ALU.mult, [xTc[i], par, rs2], [FAc[i]])
            blocks = [(128, i * 128, yp[ti * 512 + i * 128: ti * 512 + (i + 1) * 128, :]) for i in range(4)]
            if W > 512:
                blocks.append((64, 512, ysm))
            for bi, (R, c0, dst) in enumerate(blocks):
                yb, ygrp = [(yout[0], g_y[0]), (yout[1], g_y[1])][bi % 2]
                for half in range(2):
                    ps = k.paux()
                    for q in range(4):
                        kc = half * 4 + q
                        k.tr(ps[0:R, q * 128:(q + 1) * 128], FA[:, kc, c0:c0 + R], ident, [FAc[kc], cst], [ps], sig=(q == 3))
                    k.acopy(yb[0:R, half * 512:(half + 1) * 512], ps[0:R, 0:512], [ps], [yb])
                    k.free(ps)
                k.store(ygrp, dst, yb[0:R, :], [yb])

        for ti in range(NT):
            W = 576 if ti == 0 else 512
            xT, xTc = (xT_A, xTc_A) if ti % 2 == 0 else (xT_B, xTc_B)
            guard["b"] = [sst_l[0], sst_l[1], sstb, EXPB] if ti == 1 else []
            load_tile(ti, W)
            ffn(P_NF1, W, sq0=8)
            mixer(ti, W)
            ffn(P_NF2, W)
            ple_and_out(ti, W)
        P.wait_all("sp", out_groups)
        P.emit()
    return nc


_NC_CACHE = {}


def kernel(**inp):
    f = lambda a: np.ascontiguousarray(np.asarray(a, dtype=np.float32))
    g = {k_: f(v) for k_, v in inp.items()}
    cst = make_consts()
    par = np.zeros((128, NPAR), np.float32)
    par[:, P_NF1:P_NF1 + 8] = fm(g["norm_ffn1"][0]); par[:, P_NMIX:P_NMIX + 8] = fm(g["norm_mix"][0])
    par[:, P_NF2:P_NF2 + 8] = fm(g["norm_ffn2"][0]); par[:, P_NPLE:P_NPLE + 8] = fm(g["norm_ple"][0])
    par[:, P_PPOST:P_PPOST + 8] = fm(g["ple_post_norm"][0]); par[:, P_NFIN:P_NFIN + 8] = fm(g["norm_final"])
    par[:, P_SSDN:P_SSDN + 8] = fm(g["ssd_norm"][0]); par[:, P_HGN:P_HGN + 8] = fm(g["hg_norm"][0])
    for j in range(4):
        par[:, P_CW + j * 12:P_CW + (j + 1) * 12] = fm(g["conv_w"][0, j])
    par[:, P_CB:P_CB + 12] = fm(g["conv_b"][0])
    par[:, P_LB0:P_LB0 + 8] = fm(g["hg_lb_logits"][0]); par[:, P_LB1:P_LB1 + 8] = fm(g["hg_lb_logits"][1])
    par[:, P_DSK:P_DSK + 8] = fm(np.repeat(g["d_skip"][0], 64))
    bcr = np.zeros((128, 32), np.float32)
    bcr[:, 0:16] = g["dt_bias"][0][None, :]; bcr[:, 16:32] = g["a_log"][0][None, :]
    shared = dict(wf1u=g["w_ffn1_up"][0], wf1d=g["w_ffn1_down"][0], win=g["w_in"][0], wout=g["w_out"][0],
                  wf2u=g["w_ffn2_up"][0], wf2d=g["w_ffn2_down"][0], wpg=g["w_ple_gate"][0], wpp=g["w_ple_proj"][0],
                  cst=cst, par=par, bcr=bcr)
    in_maps = []
    for c in range(NCORES):
        bs = slice(16 * c, 16 * c + 16)
        m = dict(shared)
        m.update(xp=g["x_prompt"][c], xsm=g["x_sample"][bs].reshape(64, 1024),
                 cv0=g["state_conv"][0, bs].reshape(48, 1536),
                 ssm0=g["state_ssm"][0, bs].reshape(16, 1024, 128), hg0=g["state_hgrn"][0, bs].reshape(16, 1024, 128),
                 pp=g["p_prompt"][0, c], psm=g["p_sample"][0, bs].reshape(64, 256))
        in_maps.append({k_: np.ascontiguousarray(v) for k_, v in m.items()})
    if "nc" not in _NC_CACHE:
        _NC_CACHE["nc"] = build_program()
    nc = _NC_CACHE["nc"]
    res = run_bass_kernel_spmd(nc, in_maps, core_ids=list(range(NCORES)))
    R = res.results
    cat = lambda name: [np.asarray(R[c][name], np.float32) for c in range(NCORES)]
    y_prompt = np.stack(cat("yp"), 0)
    y_sample = np.concatenate([a.reshape(16, 4, 1024) for a in cat("ysm")], 0)
    conv_p = np.stack(cat("cvp"), 0)[None]
    ssm_p = np.stack([a.reshape(16, 64, 128) for a in cat("ssmp")], 0)[None]
    hg_p = np.stack([a.reshape(8, 128, 128) for a in cat("hgp")], 0)[None]
    conv_s = np.concatenate([a.reshape(16, 3, 1536) for a in cat("cvs")], 0)[None]
    ssm_s = np.concatenate([a.reshape(16, 16, 64, 128) for a in cat("ssms")], 0)[None]
    hg_s = np.concatenate([a.reshape(16, 8, 128, 128) for a in cat("hgs")], 0)[None]
    return (y_prompt, y_sample, conv_p, ssm_p, hg_p, conv_s, ssm_s, hg_s)
```

```python
from contextlib import ExitStack
import numpy as np
import concourse.bass as bass
import concourse.mybir as mybir
from concourse.bass_utils import run_bass_kernel_spmd

F32 = mybir.dt.float32
BF16 = mybir.dt.bfloat16
AF = mybir.ActivationFunctionType
ALU = mybir.AluOpType

ENGS = ("pe", "act", "dve", "pool", "sp")
NCORES = 8
EPS = 1e-6
NSIG = False
XF = (True, False, False)
PF = (False, False, False, False, False)


class Buf:
    def __init__(self, name, t):
        self.name = name
        self.t = t
        self.w = None
        self.readers = []
        self.cow = []

    def __getitem__(self, k):
        return self.t[k]


class DmaGroup:
    def __init__(self, sem, name):
        self.sem = sem
        self.count = 0
        self.name = name


class Node:
    __slots__ = ("eng", "fns", "preds", "succs", "idx", "cost", "group", "seq", "npred", "ready", "end", "xfer", "dval", "tbl", "prio")

    def __init__(self, eng, idx):
        self.eng = eng
        self.fns = []
        self.preds = {}
        self.succs = []
        self.idx = idx
        self.cost = 0.0
        self.group = None
        self.seq = 0
        self.npred = 0
        self.ready = 0.0
        self.end = 0.0
        self.xfer = 0.0
        self.dval = 0
        self.tbl = None


PRIO_CP = True
PRUNE = False
DRY = False
LAT = 110.0


class Prog:
    def __init__(self, nc, stack):
        self.nc = nc
        self.stack = stack
        self.nodes = []
        self.open_pe = None
        self.esem = {e: stack.enter_context(nc.semaphore("sem_" + e)) for e in ENGS}
        self.groups = []
        self.final_waits = []

    def sb(self, name, shape, dtype=F32):
        if DRY:
            return Buf(name, self.nc.dram_tensor("sb_" + name, list(shape), dtype).ap())
        t = self.stack.enter_context(self.nc.sbuf_tensor("sb_" + name, list(shape), dtype))
        return Buf(name, t)

    def ps(self, name, shape, dtype=F32):
        if DRY:
            return Buf(name, self.nc.dram_tensor("ps_" + name, list(shape), dtype).ap())
        t = self.stack.enter_context(self.nc.psum_tensor("ps_" + name, list(shape), dtype))
        return Buf(name, t)

    def sub(self, buf, n):
        return [Buf(f"{buf.name}_{i}", buf.t) for i in range(n)]

    def group(self, name):
        g = DmaGroup(self.stack.enter_context(self.nc.semaphore("dg_" + name)), name)
        self.groups.append(g)
        return g

    def _link(self, node, reads, writes, nowaw=False):
        def add(p, kind):
            if p is None or p is node:
                return
            node.preds[p] = node.preds.get(p, 0) | kind
        for b in reads:
            add(b.w, 1)
            for cw in b.cow:
                add(cw, 1)
        newcow = {}
        for b in writes:
            if nowaw and b.w is not None and b.w.group is not None and b.w.group is node.group:
                for p, kind in b.w.preds.items():
                    add(p, kind)
                newcow[id(b)] = b.cow + [b.w]
            else:
                add(b.w, 1)
                for cw in b.cow:
                    add(cw, 1)
                for r in b.readers:
                    add(r, 2)
                newcow[id(b)] = []
        for b in reads:
            if not b.readers or b.readers[-1] is not node:
                b.readers.append(node)
        for b in writes:
            b.cow = newcow[id(b)]
            b.w = node
            b.readers = []

    def op(self, eng, fn, reads=(), writes=(), sig=True, cost=100.0, tbl=None):
        if eng == "pe" and self.open_pe is not None:
            node = self.open_pe
        else:
            node = Node(eng, len(self.nodes))
            node.tbl = tbl
            self.nodes.append(node)
        node.fns.append(fn)
        node.cost += cost
        self._link(node, reads, writes)
        if eng == "pe":
            self.open_pe = None if sig else node

    def dma(self, eng, group, fn, reads=(), writes=(), nbytes=0, nowaw=False, after=()):
        node = Node(eng, len(self.nodes))
        self.nodes.append(node)
        node.fns.append(fn)
        node.group = group
        for b in after:
            for p in [b.w] + list(b.cow):
                if p is not None:
                    node.preds[p] = node.preds.get(p, 0) | 1
        node.cost = 1200.0 if eng == "pool" else 150.0
        node.xfer = 2000.0 + nbytes / 250.0
        self._link(node, reads, writes, nowaw=nowaw)

    def wait_all(self, eng, groups):
        self.final_waits.append((eng, groups))

    def schedule(self):
        assert self.open_pe is None
        nodes = self.nodes
        for n in nodes:
            n.npred = len(n.preds)
            for p in n.preds:
                p.succs.append(n)
        if PRIO_CP:
            cp = [0.0] * len(nodes)
            for n in reversed(nodes):
                m = 0.0
                for s_ in n.succs:
                    v = cp[s_.idx] + (0.0 if s_.eng == n.eng else LAT)
                    if v > m:
                        m = v
                cp[n.idx] = n.cost + n.xfer + m
            for n in nodes:
                n.prio = -cp[n.idx]
        else:
            for n in nodes:
                n.prio = float(n.idx)
        avail = {e: [] for e in ENGS}
        free = {e: 0.0 for e in ENGS}
        order = {e: [] for e in ENGS}
        for n in nodes:
            if n.npred == 0:
                avail[n.eng].append(n)
        left = len(nodes)
        seqlist = []
        self.seqlist = seqlist
        cur_tbl = None
        while left:
            best = None
            for e in ENGS:
                av = avail[e]
                if not av:
                    continue
                t = free[e]
                cand = None
                for n in av:
                    if n.ready <= t:
                        if e == "act":
                            key = (0 if (n.tbl is None or n.tbl == cur_tbl) else 1, n.prio, n.idx)
                        else:
                            key = (0, n.prio, n.idx)
                        if cand is None or cand[0] != 0 or key < cand[1]:
                            cand = (0, key, n)
                    elif cand is None or (cand[0] == 1 and (n.ready, n.idx) < cand[1]):
                        cand = (1, (n.ready, n.idx), n)
                n = cand[2]
                st = max(t, n.ready)
                if best is None or st < best[0] or (st == best[0] and n.idx < best[1].idx):
                    best = (st, n, e)
            st, n, e = best
            avail[e].remove(n)
            c = n.cost
            if e == "act" and n.tbl is not None:
                if cur_tbl is not None and n.tbl != cur_tbl:
                    c += 1300.0
                cur_tbl = n.tbl
            free[e] = st + c
            n.end = st + c + n.xfer
            order[e].append(n)
            seqlist.append(n)
            left -= 1
            for s_ in n.succs:
                r = n.end + (0.0 if s_.eng == n.eng else LAT)
                if r > s_.ready:
                    s_.ready = r
                s_.npred -= 1
                if s_.npred == 0:
                    avail[s_.eng].append(s_)
        self.order = order
        self.makespan = max(free.values())
        return order

    def emit(self):
        order = self.schedule()
        if DRY:
            return
        for e in ENGS:
            c = 0
            for n in order[e]:
                if n.group is None:
                    c += 1
                    n.seq = c
                else:
                    n.group.count += 16
                    n.dval = n.group.count
        nc = self.nc
        esem = self.esem
        Klast = {e: {} for e in ENGS}
        Kdone = {}
        waits_of = {}
        self.n_waits = 0
        for n in self.seqlist:
            e = n.eng
            base = Klast[e]
            need = {}
            for p, kind in n.preds.items():
                if p.group is not None:
                    key, val = p.group, p.dval
                else:
                    if p.eng == e and e == "pe":
                        continue
                    key, val = p.eng, p.seq
                if key not in need or need[key][0] < val:
                    need[key] = (val, p)
            waits = []
            for key, (val, p) in sorted(need.items(), key=lambda kv: -kv[1][1].end):
                if base.get(key, 0) >= val:
                    continue
                waits.append((esem[key] if isinstance(key, str) else key.sem, val))
                if PRUNE:
                    for k2, v2 in Kdone[p].items():
                        if base.get(k2, 0) < v2:
                            base[k2] = v2
                if base.get(key, 0) < val:
                    base[key] = val
            waits_of[n] = waits
            self.n_waits += len(waits)
            kd = dict(base)
            if n.group is None:
                if kd.get(e, 0) < n.seq:
                    kd[e] = n.seq
            else:
                kd[n.group] = max(kd.get(n.group, 0), n.dval)
            Kdone[n] = kd
        streams = {e: [(waits_of[n], n) for n in order[e]] for e in ENGS}
        finals = {e: [] for e in ENGS}
        for e, groups in self.final_waits:
            finals[e] += [(g.sem, g.count) for g in groups if g.count > 0]

        with nc.Block() as block:
            def run(engname):
                def body(eh):
                    for waits, n in streams[engname]:
                        for sem, val in waits:
                            eh.wait_ge(sem, val)
                        ins = None
                        for fn in n.fns:
                            ins = fn(eh)
                        if n.group is None:
                            ins.then_inc(esem[engname], 1)
                        else:
                            ins.then_inc(n.group.sem, 16)
                    for sem, val in finals[engname]:
                        eh.wait_ge(sem, val)
                return body
            block.tensor(run("pe"))
            block.scalar(run("act"))
            block.vector(run("dve"))
            block.gpsimd(run("pool"))
            block.sync(run("sp"))


C_ID, C_L, C_NEGM, C_MASK2, C_ONES, C_LS, C_NEGMS, C_ONES_S, C_SEG, C_SCAN = 0, 128, 256, 384, 512, 640, 704, 768, 832, 848
NCST = 848 + 576
NCSTB = 848
P_NF1, P_NMIX, P_NF2, P_NPLE, P_PPOST, P_NFIN, P_SSDN, P_HGN, P_CW, P_CB, P_LB0, P_LB1, P_DSK = 0, 8, 16, 24, 32, 40, 48, 56, 64, 112, 124, 132, 140
NPAR = 148

WI_Z, WI_XS, WI_B, WI_C, WI_DT, WI_Q, WI_FR, WI_IV, WI_OG = 0, 1024, 2048, 2304, 2560, 2576, 3600, 4624, 5648


def make_consts():
    c = np.zeros((128, NCST), np.float32)
    ii = np.arange(128)
    p, j = ii[:, None], ii[None, :]
    c[:, C_ID:C_ID + 128] = (p == j)
    c[:, C_L:C_L + 128] = (p <= j)
    c[:, C_NEGM:C_NEGM + 128] = np.where(p > j, -30000.0, 0.0)
    c[:, C_MASK2:C_MASK2 + 128] = (p <= j) & (p // 64 == j // 64)
    c[:, C_ONES:C_ONES + 128] = 1.0
    j64 = np.arange(64)[None, :]
    same = (p // 4 == j64 // 4) & (p < 64)
    c[:, C_LS:C_LS + 64] = same & (p <= j64)
    c[:, C_NEGMS:C_NEGMS + 64] = np.where(same & (p <= j64), 0.0, -30000.0)
    c[:, C_ONES_S:C_ONES_S + 64] = same
    c[:, C_SEG:C_SEG + 16] = (p // 4 == np.arange(16)[None, :]) & (p < 64)
    col = np.arange(576)
    sm = np.where(col < 512, (col % 64) != 0, ((col - 512) % 4) != 0).astype(np.float32)
    c[:, C_SCAN:C_SCAN + 576] = sm[None, :]
    return c


def fm(v):
    v = np.asarray(v, np.float32).reshape(-1, 128)
    return np.ascontiguousarray(v.T)


def _nfree(ap):
    n = 1
    for d in ap.shape[1:]:
        n *= d
    return n


def _nbytes(ap):
    n = 4
    for d in ap.shape:
        n *= d
    return n


_TBL = {AF.Exp: "el", AF.Ln: "el", AF.Silu: "silu", AF.Sigmoid: "sigm", AF.Sqrt: "sqrt"}


class K:
    def __init__(self, nc, st):
        self.nc = nc
        self.P = Prog(nc, st)
        self.mm_i = 0
        self.aux_i = 0

    def mm(self, out, lhsT, rhs, reads, writes, start=True, stop=True, sig=True):
        n = _nfree(rhs)
        c = (4.0 if rhs.dtype == F32 else 1.0) * max(n, 64) / 2.2 + (50.0 if n <= 128 else 12.0)
        self.P.op("pe", lambda e: e.matmul(out, lhsT=lhsT, rhs=rhs, start=start, stop=stop),
                  reads=reads, writes=writes, sig=sig, cost=c)

    def tr(self, out, in_, ident, reads, writes, sig=True):
        self.P.op("pe", lambda e: e.transpose(out, in_, ident), reads=reads, writes=writes, sig=sig, cost=(213.0 if in_.dtype == F32 else 107.0))

    def act(self, out, in_, func, reads, writes, bias=None, scale=1.0):
        c = 265.0 + _nfree(in_) * 0.52
        tbl = _TBL.get(func)
        if bias is None:
            self.P.op("act", lambda e: e.activation(out=out, in_=in_, func=func, scale=scale), reads=reads, writes=writes, cost=c, tbl=tbl)
        else:
            self.P.op("act", lambda e: e.activation(out=out, in_=in_, func=func, bias=bias, scale=scale), reads=reads, writes=writes, cost=c, tbl=tbl)

    def acopy(self, out, in_, reads, writes):
        self.P.op("act", lambda e: e.copy(out=out, in_=in_), reads=reads, writes=writes, cost=265.0 + _nfree(in_) * 0.52)

    def vcopy(self, out, in_, reads, writes):
        self.P.op("dve", lambda e: e.tensor_copy(out=out, in_=in_), reads=reads, writes=writes, cost=190.0 + _nfree(in_) * 0.8)

    def tt(self, out, in0, in1, op, reads, writes, eng="dve"):
        c = 190.0 + _nfree(in0) * 0.8 if eng == "dve" else 180.0 + _nfree(in0) * 2.05
        self.P.op(eng, lambda e: e.tensor_tensor(out=out, in0=in0, in1=in1, op=op), reads=reads, writes=writes, cost=c)

    def ts(self, out, in0, s1, op0, reads, writes, s2=None, op1=None, eng="dve"):
        c = 190.0 + _nfree(in0) * 0.8 if eng == "dve" else 180.0 + _nfree(in0) * 1.1
        if op1 is None:
            self.P.op(eng, lambda e: e.tensor_scalar(out=out, in0=in0, scalar1=s1, scalar2=None, op0=op0), reads=reads, writes=writes, cost=c)
        else:
            self.P.op(eng, lambda e: e.tensor_scalar(out=out, in0=in0, scalar1=s1, scalar2=s2, op0=op0, op1=op1), reads=reads, writes=writes, cost=c)

    def stt(self, out, in0, scalar, in1, op0, op1, reads, writes):
        self.P.op("dve", lambda e: e.scalar_tensor_tensor(out=out, in0=in0, scalar=scalar, in1=in1, op0=op0, op1=op1), reads=reads, writes=writes,
                  cost=190.0 + _nfree(in0) * 1.07)

    def recip(self, out, in_, reads, writes):
        self.P.op("dve", lambda e: e.reciprocal(out=out, in_=in_), reads=reads, writes=writes, cost=190.0 + _nfree(in_) * 0.8)

    def memset(self, ap, val, writes):
        self.P.op("dve", lambda e: e.memset(ap, val), writes=writes, cost=190.0 + _nfree(ap) * 0.8)

    def load(self, group, out, in_, writes, eng="sp"):
        self.P.dma(eng, group, lambda e: e.dma_start(out=out, in_=in_), writes=writes, nbytes=_nbytes(out))

    def store(self, group, out, in_, reads):
        self.P.dma("sp", group, lambda e: e.dma_start(out=out, in_=in_), reads=reads, nbytes=_nbytes(in_))

    def _alloc(self, pool, ptr):
        n = len(pool)
        for d in range(n):
            b = pool[(ptr + d) % n]
            if not getattr(b, "live", False):
                b.live = True
                return b, (ptr + d + 1) % n
        raise RuntimeError("psum pool exhausted")

    def pmm(self):
        b, self.mm_i = self._alloc(self.mmps, self.mm_i)
        return b

    def paux(self):
        b, self.aux_i = self._alloc(self.auxps, self.aux_i)
        return b

    def free(self, *bs):
        for b in bs:
            assert b.live
            b.live = False

    def wsetup(self, nslot):
        self.NSLOT = nslot
        self.wslots = [self.P.sb(f"wslot{i}", [128, 4096], BF16) for i in range(nslot)]
        self.wgroups = [self.P.group(f"w{i}") for i in range(nslot)]
        self.wblocks = []
        self.w_issued = 0
        self.w_used = 0

    def wview(self, slot, KC, ncols):
        return slot.t[:, 0:KC * ncols].rearrange("p (k n) -> p k n", k=KC)

    def wissue(self, upto):
        while self.w_issued < min(upto, len(self.wblocks)):
            i = self.w_issued
            KC, ncols, parts = self.wblocks[i]
            slot = self.wslots[i % self.NSLOT]
            g = self.wgroups[i % self.NSLOT]
            v = self.wview(slot, KC, ncols)
            extra = [self.wslots[(i - 2) % self.NSLOT]] if (len(parts) >= 4 and i >= 2) else []
            for pi, (d0, n, src) in enumerate(parts):
                self.P.dma("pool", g,
                           (lambda dst, s: (lambda e: e.dma_start(out=dst, in_=s)))(
                               v[:, :, d0:d0 + n], src.rearrange("(k p) n -> p k n", p=128)),
                           after=(extra if pi == 0 else []), writes=[slot], nbytes=KC * 128 * n * 4, nowaw=(pi > 0))
            self.w_issued += 1

    def wnext(self):
        i = self.w_used
        self.wissue(i + self.NSLOT)
        KC, ncols, parts = self.wblocks[i]
        slot = self.wslots[i % self.NSLOT]
        self.w_used += 1
        return slot, self.wview(slot, KC, ncols)


def build_program():
    nc = bass.Bass("TRN2", target_bir_lowering=False)
    din = lambda name, shape: nc.dram_tensor(name, list(shape), F32, kind="ExternalInput").ap()
    dout = lambda name, shape: nc.dram_tensor(name, list(shape), F32, kind="ExternalOutput").ap()
    xp = din("xp", [2048, 1024]); xsm = din("xsm", [64, 1024]); cv0 = din("cv0", [48, 1536])
    ssm0 = din("ssm0", [16, 1024, 128]); hg0 = din("hg0", [16, 1024, 128])
    pp = din("pp", [2048, 256]); psm = din("psm", [64, 256])
    wf1u = din("wf1u", [1024, 5632]); wf1d = din("wf1d", [2816, 1024]); win = din("win", [1024, 6672])
    wout = din("wout", [2048, 1024]); wf2u = din("wf2u", [1024, 5632]); wf2d = din("wf2d", [2816, 1024])
    wpg = din("wpg", [1024, 1024]); wpp = din("wpp", [256, 1024])
    cstd = din("cst", [128, NCST]); pard = din("par", [128, NPAR]); bcrd = din("bcr", [128, 32])
    yp = dout("yp", [2048, 1024]); ysm = dout("ysm", [64, 1024]); cvp = dout("cvp", [3, 1536])
    ssmp = dout("ssmp", [1024, 128]); hgp = dout("hgp", [1024, 128]); cvs = dout("cvs", [48, 1536])
    ssms = dout("ssms", [16, 1024, 128]); hgs = dout("hgs", [16, 1024, 128])

    with ExitStack() as st:
        k = K(nc, st)
        P = k.P
        WM = 576
        xT = P.sb("xT", [128, 8, WM]); xn = P.sb("xn", [128, 8, WM], BF16)
        BIGB = P.sb("BIGB", [128, 22, WM], BF16)
        FA = P.sb("FA", [128, 8, WM]); FB = P.sb("FB", [128, 4, 640])
        cst = P.sb("cst", [128, 128]); cstb = P.sb("cstb", [128, NCST], BF16)
        par = P.sb("par", [128, NPAR]); bcr = P.sb("bcr", [128, 32])
        lbt = P.sb("lbt", [128, 8]); oml = P.sb("oml", [128, 8]); noml = P.sb("noml", [128, 8]); abc = P.sb("abc", [128, 16])
        xin = [P.sb(f"xin{i}", [128, 1024]) for i in range(2)]
        pin = [P.sb("pin0", [128, 256])]
        yout = [P.sb("yout0", [128, 1024]), P.sb("yout1", [128, 1024])]
        pT = P.sb("pT", [128, 2, WM], BF16)
        S_ssm = P.sb("S_ssm", [128, 8, 128]); S_hg = P.sb("S_hg", [128, 8, 128])
        hTb = P.sb("hTb", [128, 8, 128], BF16); Shb2 = P.sb("Shb2", [128, 2, 2, 128], BF16)
        hist = P.sb("hist", [128, 12, 3])
        cvi = P.sb("cvi", [128, 128]); cvt = P.sb("cvt", [128, 48]); cvo = P.sb("cvo", [128, 128])
        XS = P.sb("XS", [128, 4096])
        sst_l = [Buf("sstA", XS.t[:, 0:1024].rearrange("p (b n) -> p b n", b=8)), Buf("sstB", XS.t[:, 1024:2048].rearrange("p (b n) -> p b n", b=8))]
        sstb = Buf("sstb", XS.t[:, 2048:3072].bitcast(BF16).rearrange("p (b n) -> p b n", b=16))
        xT_A = xT; xT_B = Buf("xTB", XS.t[:, :].rearrange("p (k w) -> p k w", k=8))
        vtok_l = [P.sb(f"vtok{i}", [128, 5, 128], BF16) for i in range(2)]
        dtt = P.sb("dtt", [128, 5, 16]); dta = P.sb("dta", [128, 5, 16]); nacs = P.sb("nacs", [128, 5, 16])
        dtd = P.sb("dtd", [128, 5, 16]); cdp = P.sb("cdp", [128, 4, 8]); cds = P.sb("cds", [128, 8, 16])
        t16 = P.sb("t16", [128, 16]); t16b = P.sb("t16b", [128, 16]); dtab = P.sb("dtab", [128, 5, 2, 16], BF16)
        BCT = P.sb("BCT", [128, 4, WM], BF16)
        Btok = P.sb("Btok", [128, 5, 2, 128], BF16); cbs = P.sb("cbs", [128, 5, 2, 128])
        EXPB = Buf("EXPB", XS.t[:, 3072:4096].bitcast(BF16))
        xdt_l = [P.sb(f"xdt{i}", [128, 512], BF16) for i in range(2)]; xdtd_l = [P.sb(f"xdtd{i}", [128, 512], BF16) for i in range(2)]
        Eh = [P.sb(f"Eh{i}", [128, 2, 128]) for i in range(2)]
        Mh = [P.sb(f"Mh{i}", [128, 2, 128], BF16) for i in range(2)]
        Ein_l = [P.sb(f"Ein{i}", [128, 128]) for i in range(2)]; yis_l = [P.sb(f"yis{i}", [128, 128]) for i in range(2)]
        y1_l = [P.sb(f"y1{i}", [128, 128]) for i in range(2)]
        SCR = P.sb("SCR", [128, 1024])
        acsTb_l = [P.sb(f"acsTb{i}", [128, 2, 128], BF16) for i in range(2)]; totb = P.sb("totb", [128, 2, 16], BF16)
        rs = P.sb("rs", [128, WM]); rs2 = P.sb("rs2", [128, WM])
        qe_l = [P.sb(f"qe{i}", [128, WM], BF16) for i in range(2)]; ke_l = [P.sb(f"ke{i}", [128, WM], BF16) for i in range(2)]
        kdT_l = [P.sb(f"kdT{i}", [128, WM], BF16) for i in range(2)]
        attm_l = [P.sb(f"attm{i}", [128, 128], BF16) for i in range(2)]; kdtok_l = [P.sb(f"kdtok{i}", [128, 128], BF16) for i in range(2)]
        dch_l = [P.sb(f"dch{i}", [128, 24]) for i in range(2)]
        xTc = P.sub(xT, 8); xTc_A = xTc; xTc_B = P.sub(xT_B, 8); guard = {"b": []}; FAc = P.sub(FA, 8); FBc = P.sub(FB, 4); BBc = P.sub(BIGB, 22); SCRh = P.sub(SCR, 2)
        ctr = {"xd": 0, "at": 0}
        S_hgc = P.sub(S_hg, 8); Shq = [P.sub(Shb2, 2), P.sub(Shb2, 2)]; S_ssmc = P.sub(S_ssm, 8); hTbc = P.sub(hTb, 8)
        k.mmps = [P.ps(f"pmm{i}", [128, 512]) for i in range(7)]
        k.auxps = k.mmps
        pb = P.ps("pauxb", [128, 1024], BF16)
        k.wsetup(4)

        g_c = [P.group("c0"), P.group("c1"), P.group("c2"), P.group("c3")]; g_x = [P.group("xin0"), P.group("xin1")]; g_p = [P.group("pin0"), P.group("pin1")]
        g_y = [P.group("yout0"), P.group("yout1")]; g_cvi = P.group("cvi"); g_cvo = P.group("cvo"); g_ss_l = [P.group("sstA"), P.group("sstB")]; g_yx = [P.group("yx0"), P.group("yx1")]
        g_so = P.group("stout")
        out_groups = g_y + g_yx + g_ss_l + [g_cvo, g_so]

        def ffn_blocks(wu, wd):
            bl = []
            for b in range(11):
                bl.append((8, 512, [(0, 256, wu[:, b * 256:(b + 1) * 256]), (256, 256, wu[:, 2816 + b * 256:2816 + (b + 1) * 256])]))
            for fo in range(8):
                bl.append((22, 128, [(0, 128, wd[:, fo * 128:(fo + 1) * 128])]))
            return bl

        def tile_blocks():
            bl = ffn_blocks(wf1u, wf1d)
            bl.append((8, 16, [(0, 16, win[:, WI_DT:WI_DT + 16])]))
            bl.append((8, 512, [(0, 512, win[:, WI_B:WI_B + 512])]))
            for kind, idx in MIX_ORDER:
                if kind == "gp":
                    g = idx
                    bl.append((8, 512, [(0, 512, win[:, WI_Z + g * 512:WI_Z + (g + 1) * 512])]))
                    bl.append((8, 512, [(0, 512, win[:, WI_XS + g * 512:WI_XS + (g + 1) * 512])]))
                elif kind == "h":
                    h = idx
                    bl.append((8, 512, [(0, 128, win[:, WI_Q + h * 128:WI_Q + (h + 1) * 128]),
                                        (128, 128, win[:, WI_FR + h * 128:WI_FR + (h + 1) * 128]),
                                        (256, 128, win[:, WI_OG + h * 128:WI_OG + (h + 1) * 128]),
                                        (384, 128, win[:, WI_IV + h * 128:WI_IV + (h + 1) * 128])]))
            for b in range(4):
                bl.append((16, 256, [(0, 256, wout[:, b * 256:(b + 1) * 256])]))
            bl += ffn_blocks(wf2u, wf2d)
            bl.append((2, 1024, [(0, 1024, wpp[:, :])]))
            for b in range(2):
                bl.append((8, 512, [(0, 512, wpg[:, b * 512:(b + 1) * 512])]))
            return bl

        NT = 4
        POOL_FK, POOL_M, POOL_YG, POOL_TAP0, POOL_QE = PF
        ACT_TAP0, DVE_KD, DVE_O = XF
        NATIVE_SIG = NSIG
        MIX_ORDER = [("gp", 0), ("gb", 0), ("gp", 1), ("gb", 1)] + [("h", h) for h in range(8)]
        for _ in range(NT):
            k.wblocks += tile_blocks()

        k.load(g_c[0], cst[:], cstd[:, C_ID:C_ID + 128], [cst]); k.load(g_c[3], cstb[:], cstd, [cstb], eng="pool"); k.load(g_c[1], par[:], pard, [par]); k.load(g_c[2], bcr[:], bcrd, [bcr])
        k.wissue(k.NSLOT)
        k.tt(lbt[:], par[:, P_LB0:P_LB0 + 8], par[:, P_LB1:P_LB1 + 8], ALU.subtract, [par], [lbt])
        k.act(lbt[:], lbt[:], AF.Sigmoid, [lbt], [lbt])
        k.ts(oml[:], lbt[:], -1.0, ALU.mult, [lbt], [oml], s2=1.0, op1=ALU.add)
        k.ts(noml[:], oml[:], -1.0, ALU.mult, [oml], [noml])
        k.act(abc[:], bcr[:, 16:32], AF.Exp, [bcr], [abc])
        k.ts(abc[:], abc[:], -1.0, ALU.mult, [abc], [abc])
        k.memset(hist[:], 0.0, [hist]); k.memset(S_ssm[:], 0.0, S_ssmc); k.memset(S_hg[:], 0.0, S_hgc)

        SelL = SCR[:, :].bitcast(BF16)
        k.vcopy(SelL[0:16, :].rearrange("p (h s) -> p h s", s=128), cstb[0:16, C_ID:C_ID + 16].unsqueeze(2).to_broadcast([16, 16, 128]), [cstb], SCRh)
        ident = cst[:, 0:128]
        identb = cstb[:, C_ID:C_ID + 128]
        onesb = cstb[:, C_ONES:C_ONES + 128]

        def colranges(W):
            return [(0, 512)] if W == 512 else [(0, 288), (288, W)]

        def linear(wv, slot, wc0, rhs, rhs_buf, KC, c0, c1):
            ps = k.pmm()
            for kc in range(KC):
                rb = rhs_buf[kc] if isinstance(rhs_buf, list) else rhs_buf
                k.mm(ps[:, 0:c1 - c0], wv[:, kc, wc0:wc0 + 128], rhs[:, kc, c0:c1], [slot, rb], [ps],
                     start=(kc == 0), stop=(kc == KC - 1), sig=(kc == KC - 1))
            return ps

        def rmsnorm_stats(src, srcb, nchunk, W, sqdst_c0, inv_n, out_rs):
            for i in range(nchunk):
                k.act(BIGB[:, sqdst_c0 + i, 0:W], src[:, i, 0:W], AF.Square, [srcb[i]], [BBc[sqdst_c0 + i]])
            for (c0, c1) in colranges(W):
                ps = k.pmm()
                for i in range(nchunk):
                    k.mm(ps[:, 0:c1 - c0], onesb, BIGB[:, sqdst_c0 + i, c0:c1], [cstb, BBc[sqdst_c0 + i]], [ps],
                         start=(i == 0), stop=(i == nchunk - 1), sig=(i == nchunk - 1))
                k.act(out_rs[:, c0:c1], ps[:, 0:c1 - c0], AF.Ln, [ps], [out_rs], bias=EPS, scale=inv_n)
                k.free(ps)
            k.act(out_rs[:, 0:W], out_rs[:, 0:W], AF.Exp, [out_rs], [out_rs], scale=-0.5)

        def norm_to_xn(pcol, W, sq0=0):
            rmsnorm_stats(xT, xTc, 8, W, sq0, 1.0 / 1024.0, rs)
            for i in range(8):
                k.stt(xn[:, i, 0:W], xT[:, i, 0:W], par[:, pcol + i:pcol + i + 1], rs[:, 0:W], ALU.mult, ALU.mult,
                      [xTc[i], par, rs], [xn])

        def ffn(pcol, W, sq0=0):
            norm_to_xn(pcol, W, sq0)
            hid = BIGB
            tgl = 0
            for b in range(11):
                slot, wv = k.wnext()
                for jj in range(2):
                    j = 2 * b + jj
                    for (c0, c1) in colranges(W):
                        n = c1 - c0
                        pg = linear(wv, slot, jj * 128, xn, xn, 8, c0, c1)
                        pu = linear(wv, slot, 256 + jj * 128, xn, xn, 8, c0, c1)
                        sg = FA[:, tgl, 0:n]; sgb = FAc[tgl]; tgl ^= 1
                        k.act(sg, pg[:, 0:n], AF.Silu, [pg], [sgb])
                        k.tt(hid[:, j, c0:c1], sg, pu[:, 0:n], ALU.mult, [sgb, pu], [BBc[j]])
                        k.free(pg, pu)
            for fo in range(8):
                slot, wv = k.wnext()
                for (c0, c1) in colranges(W):
                    n = c1 - c0
                    ps = linear(wv, slot, 0, hid, BBc, 22, c0, c1)
                    k.stt(xT[:, fo, c0:c1], ps[:, 0:n], 0.5, xT[:, fo, c0:c1], ALU.mult, ALU.add, [ps, xTc[fo]], [xTc[fo]])
                    k.free(ps)

        def load_tile(ti, W, part, xTt, xTct):
            blocks = [(128, i * 128, xp[ti * 512 + i * 128: ti * 512 + (i + 1) * 128, :], pp[ti * 512 + i * 128: ti * 512 + (i + 1) * 128, :]) for i in range(4)]
            if W > 512:
                blocks.append((64, 512, xsm, psm))
            for bi, (R, c0, xsrc, psrc) in enumerate(blocks):
                if part == "x":
                    xb = xin[bi % 2]
                    k.load(g_x[bi % 2], xb[0:R, :], xsrc, [xb])
                    for half in range(2):
                        ps = k.paux()
                        for q in range(4):
                            kc = half * 4 + q
                            k.tr(ps[:, q * 128:q * 128 + R], xb[0:R, kc * 128:(kc + 1) * 128], ident[0:R, 0:R], [xb, cst], [ps], sig=(q == 3))
                        k.acopy(xTt[:, half * 4:half * 4 + 4, c0:c0 + R],
                                ps[:, :].rearrange("p (q r) -> p q r", q=4)[:, :, 0:R], [ps], xTct[half * 4:half * 4 + 4] + guard["b"])
                        k.free(ps)
                else:
                    pbuf = pin[0]
                    k.load(g_p[0], pbuf[0:R, :], psrc, [pbuf])
                    ps = k.paux()
                    for q in range(2):
                        k.tr(ps[:, q * 128:q * 128 + R], pbuf[0:R, q * 128:(q + 1) * 128], ident[0:R, 0:R], [pbuf, cst], [ps], sig=(q == 1))
                    k.vcopy(pT[:, :, c0:c0 + R], ps[:, 0:256].rearrange("p (q r) -> p q r", q=2)[:, :, 0:R], [ps], [pT])
                    k.free(ps)

        def mixer(ti, W):
            has_s = W > 512
            first = (ti == 0)
            last = (ti == NT - 1)
            blocks = [(128, i * 128, i) for i in range(4)]
            if has_s:
                blocks.append((64, 512, 4))
            crs = colranges(W)
            mixed = BIGB
            norm_to_xn(P_NMIX, W)

            slot, wv = k.wnext()
            for (R, c0, bi) in blocks:
                smp = (bi == 4)
                ps = k.paux()
                for kc in range(8):
                    k.mm(ps[0:R, 0:16], xn[:, kc, c0:c0 + R], wv[:, kc, 0:16], [xn, slot], [ps], start=(kc == 0), stop=(kc == 7), sig=(kc == 7))
                k.tt(t16[0:R, :], ps[0:R, 0:16], bcr[0:R, 0:16], ALU.add, [ps, bcr], [t16])
                k.free(ps)
                k.act(t16[0:R, :], t16[0:R, :], AF.Exp, [t16], [t16])
                k.act(dtt[0:R, bi, :], t16[0:R, :], AF.Ln, [t16], [dtt], bias=1.0)
                k.tt(dta[0:R, bi, :], dtt[0:R, bi, :], abc[0:R, :], ALU.mult, [dtt, abc], [dta])
                k.vcopy(dtab[0:R, bi, 0, :], dta[0:R, bi, :], [dta], [dtab])
                k.tt(dtab[0:R, bi, 1, :], dta[0:R, bi, :], dtab[0:R, bi, 0, :], ALU.subtract, [dta, dtab], [dtab])
                Lm = cstb[0:R, C_LS:C_LS + R] if smp else cstb[0:R, C_L:C_L + R]
                On = cstb[0:R, C_ONES_S:C_ONES_S + R] if smp else cstb[0:R, C_ONES:C_ONES + R]
                ps2 = k.paux()
                k.mm(ps2[0:R, 0:16], Lm, dtab[0:R, bi, 0, :], [cstb, dtab], [ps2], start=True, stop=False, sig=False)
                k.mm(ps2[0:R, 0:16], Lm, dtab[0:R, bi, 1, :], [cstb, dtab], [ps2], start=False, stop=True, sig=False)
                k.mm(ps2[0:R, 16:32], On, dtab[0:R, bi, 0, :], [cstb, dtab], [ps2], start=True, stop=False, sig=False)
                k.mm(ps2[0:R, 16:32], On, dtab[0:R, bi, 1, :], [cstb, dtab], [ps2], start=False, stop=True)
                k.ts(nacs[0:R, bi, :], ps2[0:R, 0:16], -1.0, ALU.mult, [ps2], [nacs])
                k.tt(t16b[0:R, :], ps2[0:R, 16:32], nacs[0:R, bi, :], ALU.add, [ps2, nacs], [t16b])
                k.free(ps2)
                k.act(t16b[0:R, :], t16b[0:R, :], AF.Exp, [t16b], [t16b])
                k.tt(dtd[0:R, bi, :], dtt[0:R, bi, :], t16b[0:R, :], ALU.mult, [dtt, t16b], [dtd])
                ncol = 16 if smp else 2
                rhs_t = cstb[0:R, C_SEG:C_SEG + 16] if smp else cstb[0:R, C_ONES:C_ONES + 2]
                ptot = k.paux()
                k.mm(ptot[0:16, 0:ncol], dtab[0:R, bi, 0, :], rhs_t, [dtab, cstb], [ptot], start=True, stop=False, sig=False)
                k.mm(ptot[0:16, 0:ncol], dtab[0:R, bi, 1, :], rhs_t, [dtab, cstb], [ptot], start=False, stop=True)
                k.vcopy(totb[0:16, 0, 0:ncol], ptot[0:16, 0:ncol], [ptot], [totb])
                k.tt(totb[0:16, 1, 0:ncol], ptot[0:16, 0:ncol], totb[0:16, 0, 0:ncol], ALU.subtract, [ptot, totb], [totb])
                k.free(ptot)
                ps3 = k.paux()
                for h in range(16):
                    for part in range(2):
                        k.mm(ps3[(h % 2) * 64:(h % 2) * 64 + 64, (h // 2) * ncol:(h // 2) * ncol + ncol], SelL[0:16, h * 128:h * 128 + 64], totb[0:16, part, 0:ncol],
                             SCRh + [totb], [ps3], start=(part == 0), stop=(part == 1), sig=(h == 15 and part == 1))
                if not smp:
                    k.act(cdp[:, bi, :].unsqueeze(2), ps3[:, 0:16].rearrange("p (a b) -> p a b", b=2)[:, :, 0:1], AF.Exp, [ps3], [cdp])
                else:
                    k.act(cds[:, :, :], ps3[:, 0:128].rearrange("p (a b) -> p a b", b=16), AF.Exp, [ps3], [cds])
                k.free(ps3)

            def raw_sample_view(i):
                return FB[:, i, 515:627].rearrange("p (b j) -> p b j", j=7)

            def conv_stage(wv, slot, cis, dst, dstbufs, dst_c0):
                for i in range(4):
                    ci = cis[i]
                    fbb = FBc[i]; dstb = dstbufs[dst_c0 + i]
                    if has_s:
                        k.load(g_cvi, cvi[0:48, :], cv0[:, ci * 128:(ci + 1) * 128], [cvi])
                    for (c0, c1) in crs:
                        ps = linear(wv, slot, i * 128, xn, xn, 8, c0, c1)
                        p1 = min(c1, 512)
                        if p1 > c0:
                            k.acopy(FB[:, i, 3 + c0:3 + p1], ps[:, 0:p1 - c0], [ps], [fbb])
                        if c1 > 512:
                            k.acopy(raw_sample_view(i)[:, :, 3:7], ps[:, 512 - c0:576 - c0].rearrange("p (b j) -> p b j", j=4), [ps], [fbb])
                        k.free(ps)
                    k.vcopy(FB[:, i, 0:3], hist[:, ci, :], [hist], [fbb])
                    k.vcopy(hist[:, ci, :], FB[:, i, 512:515], [fbb], [hist])
                    if last:
                        pst = k.paux()
                        k.tr(pst[0:3, 0:128], hist[:, ci, :], ident, [hist, cst], [pst])
                        k.acopy(cvo[0:3, :], pst[0:3, 0:128], [pst], [cvo])
                        k.free(pst)
                        k.store(g_cvo, cvp[:, ci * 128:(ci + 1) * 128], cvo[0:3, :], [cvo])
                    if has_s:
                        pst = k.paux()
                        k.tr(pst[:, 0:48], cvi[0:48, :], ident[0:48, 0:48], [cvi, cst], [pst])
                        k.vcopy(raw_sample_view(i)[:, :, 0:3], pst[:, 0:48].rearrange("p (b j) -> p b j", j=3), [pst], [fbb])
                        k.free(pst)
                        k.vcopy(cvt[:, :].rearrange("p (b j) -> p b j", j=3), raw_sample_view(i)[:, :, 4:7], [fbb], [cvt])
                        pst = k.paux()
                        k.tr(pst[0:48, 0:128], cvt[:, :], ident, [cvt, cst], [pst])
                        k.acopy(cvo[0:48, :], pst[0:48, 0:128], [pst], [cvo])
                        k.free(pst)
                        k.store(g_cvo, cvs[:, ci * 128:(ci + 1) * 128], cvo[0:48, :], [cvo])
                    d = dst[:, dst_c0 + i, 0:512]
                    if ACT_TAP0:
                        k.act(d, FB[:, i, 0:512], AF.Identity, [fbb, par], [dstb], bias=par[:, P_CB + ci:P_CB + ci + 1], scale=par[:, P_CW + ci:P_CW + ci + 1])
                    else:
                        k.ts(d, FB[:, i, 0:512], par[:, P_CW + ci:P_CW + ci + 1], ALU.mult, [fbb, par], [dstb],
                             s2=par[:, P_CB + ci:P_CB + ci + 1], op1=ALU.add, eng=("pool" if POOL_TAP0 else "dve"))
                    for j in range(1, 4):
                        k.stt(d, FB[:, i, j:j + 512], par[:, P_CW + j * 12 + ci:P_CW + j * 12 + ci + 1], d, ALU.mult, ALU.add,
                              [fbb, par, dstb], [dstb])
                    if has_s:
                        d = dst[:, dst_c0 + i, 512:576].rearrange("p (b j) -> p b j", j=4)
                        rv = raw_sample_view(i)
                        k.ts(d, rv[:, :, 0:4], par[:, P_CW + ci:P_CW + ci + 1], ALU.mult, [fbb, par], [dstb],
                             s2=par[:, P_CB + ci:P_CB + ci + 1], op1=ALU.add)
                        for j in range(1, 4):
                            k.stt(d, rv[:, :, j:j + 4], par[:, P_CW + j * 12 + ci:P_CW + j * 12 + ci + 1], d, ALU.mult, ALU.add,
                                  [fbb, par, dstb], [dstb])

            slot, wv = k.wnext()
            conv_stage(wv, slot, [8, 9, 10, 11], FA, FAc, 0)
            for i in range(4):
                k.act(BCT[:, i, 0:W], FA[:, i, 0:W], AF.Silu, [FAc[i]], [BCT])
            for (R, c0, bi) in blocks:
                for g in range(2):
                    k.tr(pb[0:R, g * 128:(g + 1) * 128], BCT[:, g, c0:c0 + R], identb, [BCT, cstb], [pb], sig=(g == 1))
                k.acopy(Btok[0:R, bi, :, :], pb[0:R, 0:256].rearrange("p (g n) -> p g n", g=2), [pb], [Btok])
                ps = k.paux()
                for g in range(2):
                    k.mm(ps[0:R, g * 128:g * 128 + R], BCT[:, g, c0:c0 + R], BCT[:, 2 + g, c0:c0 + R], [BCT], [ps], sig=(g == 1))
                k.acopy(cbs[0:R, bi, :, 0:R], ps[0:R, 0:256].rearrange("p (g n) -> p g n", g=2)[:, :, 0:R], [ps], [cbs])
                k.free(ps)

            def ssd_proj(g):
                slot, wv = k.wnext()
                for i in range(4):
                    for (c0, c1) in crs:
                        ps = linear(wv, slot, i * 128, xn, xn, 8, c0, c1)
                        k.act(FA[:, i, c0:c1], ps[:, 0:c1 - c0], AF.Silu, [ps], [FAc[i]])
                        k.free(ps)
                slot, wv = k.wnext()
                conv_stage(wv, slot, [4 * g + i for i in range(4)], FA, FAc, 4)
                for i in range(4):
                    k.act(FA[:, 4 + i, 0:W], FA[:, 4 + i, 0:W], AF.Silu, [FAc[4 + i]], [FAc[4 + i]])
                if has_s:
                    k.tt(EXPB[0:64, :].rearrange("p (b n) -> p b n", b=16),
                         Btok[0:64, 4, g, :].unsqueeze(1).to_broadcast([64, 16, 128]),
                         cstb[0:64, C_SEG:C_SEG + 16].unsqueeze(2).to_broadcast([64, 16, 128]), ALU.mult, [Btok, cstb], [EXPB])
            def ssd_blocks(g):
                for (R, c0, bi) in blocks:
                    smp = (bi == 4)
                    have_state = smp or (not first) or (bi > 0)
                    pt = k.paux()
                    for i in range(4):
                        k.tr(pt[0:R, i * 128:(i + 1) * 128], FA[:, 4 + i, c0:c0 + R], ident, [FAc[4 + i], cst], [pt], sig=(i == 3))
                    xdt = xdt_l[ctr["xd"] % 2]; xdtd = xdtd_l[ctr["xd"] % 2]; ctr["xd"] += 1
                    k.tt(xdt[0:R, :].rearrange("p (h q) -> p h q", h=8), pt[0:R, :].rearrange("p (h q) -> p h q", h=8),
                         dtt[0:R, bi, 8 * g:8 * g + 8].unsqueeze(2).to_broadcast([R, 8, 64]), ALU.mult, [pt, dtt], [xdt])
                    k.tt(xdtd[0:R, :].rearrange("p (h q) -> p h q", h=8), pt[0:R, :].rearrange("p (h q) -> p h q", h=8),
                         dtd[0:R, bi, 8 * g:8 * g + 8].unsqueeze(2).to_broadcast([R, 8, 64]), ALU.mult, [pt, dtd], [xdtd])
                    k.free(pt)
                    Lm = cstb[0:R, C_LS:C_LS + R] if smp else cstb[0:R, C_L:C_L + R]
                    acsb = acsTb_l[ctr["xd"] % 2]
                    pacs = k.paux()
                    k.mm(pacs[0:16, 0:R], dtab[0:R, bi, 0, :], Lm, [dtab, cstb], [pacs], start=True, stop=False, sig=False)
                    k.mm(pacs[0:16, 0:R], dtab[0:R, bi, 1, :], Lm, [dtab, cstb], [pacs], start=False, stop=True)
                    k.vcopy(acsb[0:16, 0, 0:R], pacs[0:16, 0:R], [pacs], [acsb])
                    k.tt(acsb[0:16, 1, 0:R], pacs[0:16, 0:R], acsb[0:16, 0, 0:R], ALU.subtract, [pacs, acsb], [acsb])
                    k.free(pacs)
                    ngm = cstb[0:R, C_NEGMS:C_NEGMS + R] if smp else cstb[0:R, C_NEGM:C_NEGM + R]
                    for hp in range(4):
                        hp2 = 4 * g + hp
                        xsv = FA[:, 4 + hp, c0:c0 + R]
                        szv = FA[:, hp, c0:c0 + R]
                        ygv = FB[:, hp, c0:c0 + R]
                        dcol = par[:, P_DSK + hp2:P_DSK + hp2 + 1]
                        Ein = Ein_l[hp % 2]; yis = yis_l[hp % 2]; y1 = y1_l[hp % 2]
                        if smp:
                            for hb in range(2):
                                sst = sst_l[hb]
                                k.load(g_ss_l[hb], sst[:, :, :], ssm0[8 * hb:8 * hb + 8, hp2 * 128:(hp2 + 1) * 128, :].rearrange("b q n -> q b n"), [sst])
                                for bq in range(2):
                                    ptq = k.pmm()
                                    for b4 in range(4):
                                        k.tr(ptq[:, b4 * 128:(b4 + 1) * 128], sst[:, bq * 4 + b4, :], ident, [sst, cst], [ptq], sig=(b4 == 3))
                                    k.acopy(sstb[:, 8 * hb + bq * 4:8 * hb + bq * 4 + 4, :], ptq[:, :].rearrange("p (b n) -> p b n", b=4), [ptq], [sstb])
                                    k.free(ptq)
                        yp_ = k.pmm()
                        for q in range(2):
                            hl = 2 * hp + q
                            h = 8 * g + hl
                            pe_ = k.paux()
                            k.mm(pe_[0:R, 0:R], SelL[0:16, h * 128:h * 128 + R], acsb[0:16, 0, 0:R], SCRh + [acsb], [pe_], start=True, stop=False, sig=False)
                            k.mm(pe_[0:R, 0:R], SelL[0:16, h * 128:h * 128 + R], acsb[0:16, 1, 0:R], SCRh + [acsb], [pe_], start=False, stop=False, sig=False)
                            k.mm(pe_[0:R, 0:R], identb[0:R, 0:R], ngm, [cstb], [pe_], start=False, stop=True)
                            E = Eh[hp % 2]; M = Mh[hp % 2]
                            k.act(E[0:R, q, 0:R], pe_[0:R, 0:R], AF.Exp, [pe_, nacs], [E], bias=nacs[0:R, bi, h:h + 1])
                            k.free(pe_)
                        k.tt(M[0:R, :, 0:R], E[0:R, :, 0:R], cbs[0:R, bi, g, 0:R].unsqueeze(1).to_broadcast([R, 2, R]), ALU.mult, [E, cbs], [M], eng=("pool" if POOL_M else "dve"))
                        for q in range(2):
                            hl = 2 * hp + q
                            k.mm(yp_[q * 64:q * 64 + 64, 0:R], xdt[0:R, hl * 64:(hl + 1) * 64], M[0:R, q, 0:R], [xdt, M], [yp_], sig=True)
                        if have_state:
                            pyi = k.paux()
                            if not smp:
                                k.mm(pyi[:, 0:R], hTb[:, hp2, :], BCT[:, 2 + g, c0:c0 + R], [hTbc[hp2], BCT], [pyi])
                            else:
                                for b in range(16):
                                    k.mm(pyi[:, 4 * b:4 * b + 4], sstb[:, b, :], BCT[:, 2 + g, 512 + 4 * b:512 + 4 * b + 4], [sstb, BCT], [pyi], sig=(b == 15))
                            pe2 = k.paux()
                            for q in range(2):
                                hq = 2 * hp2 + q
                                k.mm(pe2[q * 64:q * 64 + 64, 0:R], SelL[0:16, hq * 128:hq * 128 + 64], acsb[0:16, 0, 0:R], SCRh + [acsb], [pe2], start=True, stop=False, sig=False)
                                k.mm(pe2[q * 64:q * 64 + 64, 0:R], SelL[0:16, hq * 128:hq * 128 + 64], acsb[0:16, 1, 0:R], SCRh + [acsb], [pe2], start=False, stop=True, sig=(q == 1))
                            k.act(Ein[:, 0:R], pe2[:, 0:R], AF.Exp, [pe2], [Ein])
                            k.free(pe2)
                            k.tt(yis[:, 0:R], pyi[:, 0:R], Ein[:, 0:R], ALU.mult, [pyi, Ein], [yis])
                            k.free(pyi)
                            k.tt(y1[:, 0:R], yp_[:, 0:R], yis[:, 0:R], ALU.add, [yp_, yis], [y1])
                            k.stt(y1[:, 0:R], xsv, dcol, y1[:, 0:R], ALU.mult, ALU.add, [FAc[4 + hp], par, y1], [y1])
                        else:
                            k.stt(y1[:, 0:R], xsv, dcol, yp_[:, 0:R], ALU.mult, ALU.add, [FAc[4 + hp], par, yp_], [y1])
                        k.free(yp_)
                        k.tt(ygv, y1[:, 0:R], szv, ALU.mult, [y1, FAc[hp]], [FBc[hp]], eng=("pool" if POOL_YG else "dve"))
                        if not smp:
                            pS = k.paux()
                            k.mm(pS[:, 0:128], xdtd[0:128, hp * 128:(hp + 1) * 128], Btok[:, bi, g, :], [xdtd, Btok], [pS])
                            k.stt(S_ssm[:, hp2, :], S_ssm[:, hp2, :], cdp[:, bi, hp2:hp2 + 1], pS[:, 0:128], ALU.mult, ALU.add, [S_ssmc[hp2], cdp, pS], [S_ssmc[hp2]])
                            k.free(pS)
                        else:
                            for hb in range(2):
                                sst = sst_l[hb]
                                k.tt(sst[:, :, :], sst[:, :, :], cds[:, hp2, 8 * hb:8 * hb + 8].unsqueeze(2).to_broadcast([128, 8, 128]), ALU.mult, [sst, cds], [sst])
                                for bq in range(2):
                                    pS = k.pmm()
                                    e0 = (8 * hb + 4 * bq) * 128
                                    k.mm(pS[:, 0:512], xdtd[0:64, hp * 128:(hp + 1) * 128], EXPB[0:64, e0:e0 + 512], [xdtd, EXPB], [pS])
                                    k.tt(sst[:, bq * 4:bq * 4 + 4, :], sst[:, bq * 4:bq * 4 + 4, :], pS[:, :].rearrange("p (b n) -> p b n", b=4), ALU.add, [sst, pS], [sst])
                                    k.free(pS)
                                k.store(g_ss_l[hb], ssms[8 * hb:8 * hb + 8, hp2 * 128:(hp2 + 1) * 128, :].rearrange("b q n -> q b n"), sst[:, :, :], [sst])
                    if not smp and not (last and bi == 3):
                        ptq = k.pmm()
                        for hp in range(4):
                            k.tr(ptq[:, hp * 128:(hp + 1) * 128], S_ssm[:, 4 * g + hp, :], ident, [S_ssmc[4 * g + hp], cst], [ptq], sig=(hp == 3))
                        k.acopy(hTb[:, 4 * g:4 * g + 4, :], ptq[:, :].rearrange("p (b n) -> p b n", b=4), [ptq], hTbc[4 * g:4 * g + 4])
                        k.free(ptq)
                rmsnorm_stats(FB, FBc, 4, W, 16, 1.0 / 512.0, rs2)
                for i in range(4):
                    k.stt(mixed[:, 4 * g + i, 0:W], FB[:, i, 0:W], par[:, P_SSDN + 4 * g + i:P_SSDN + 4 * g + i + 1], rs2[:, 0:W],
                          ALU.mult, ALU.mult, [FBc[i], par, rs2], [BBc[4 * g + i]])

            a = [FA[:, i, 0:W] for i in range(8)]
            A = FAc
            scan_mask = cstb[:, C_SCAN:C_SCAN + W]
            def hgrn_head(h):
                slot, wv = k.wnext()
                qe = qe_l[h % 2]; ke = ke_l[h % 2]; kdT = kdT_l[h % 2]; dch = dch_l[h % 2]; vtok = vtok_l[h % 2]
                i0, i1, i2, i3 = 4 * (h % 2), 4 * (h % 2) + 1, 4 * (h % 2) + 2, 4 * (h % 2) + 3
                hpar = h % 2; st = {"pp": 0}
                if not first:
                    k.acopy(Shb2[:, hpar, 0, :], S_hg[:, h, :], [S_hgc[h]], [Shq[hpar][0]])
                fa = lambda i_: FA[:, i_, 0:W]
                ob = FBc[h % 2]; o_ap = FB[:, h % 2, 0:W]
                sgb = FBc[2 + h % 2]; sog = FB[:, 2 + h % 2, 0:W]
                for (R, c0, bi) in blocks:
                    ps = k.pmm()
                    for kc in range(8):
                        k.mm(ps[0:R, 0:128], xn[:, kc, c0:c0 + R], wv[:, kc, 384:512], [xn, slot], [ps], start=(kc == 0), stop=(kc == 7), sig=(kc == 7))
                    k.vcopy(vtok[0:R, bi, :], ps[0:R, 0:128], [ps], [vtok])
                    k.free(ps)
                def el_sigmoid(dst_T, dst_ci, dst_buf, wi, keep):
                    live = []
                    for (c0, c1) in crs:
                        ps = linear(wv, slot, wi * 128, xn, xn, 8, c0, c1)
                        if NATIVE_SIG:
                            k.act(dst_T[:, dst_ci, c0:c1], ps[:, 0:c1 - c0], AF.Sigmoid, [ps], [dst_buf])
                        else:
                            k.act(dst_T[:, dst_ci, c0:c1], ps[:, 0:c1 - c0], AF.Exp, [ps], [dst_buf], scale=-1.0)
                        if keep:
                            live.append((ps, c0, c1))
                        else:
                            k.free(ps)
                    if not NATIVE_SIG:
                        dd = dst_T[:, dst_ci, 0:W]
                        k.act(dd, dd, AF.Ln, [dst_buf], [dst_buf], bias=1.0)
                        k.act(dd, dd, AF.Exp, [dst_buf], [dst_buf], scale=-1.0)
                    return live

                el_sigmoid(FA, i1, A[i1], 1, False)
                k.ts(fa(i2), fa(i1), oml[:, h:h + 1], ALU.mult, [A[i1], oml, lbt], [A[i2]], s2=lbt[:, h:h + 1], op1=ALU.add, eng=("pool" if POOL_FK else "dve"))
                k.act(fa(i2), fa(i2), AF.Ln, [A[i2]], [A[i2]])
                P.op("dve", (lambda o, m, d_: (lambda e: e.tensor_tensor_scan(out=o, data0=m, data1=d_, initial=0.0, op0=ALU.mult, op1=ALU.add)))(fa(i3), scan_mask, fa(i2)),
                     reads=[cstb, A[i2]], writes=[A[i3]], cost=120.0 + 2.2 * W)
                k.ts(fa(i1), fa(i1), noml[:, h:h + 1], ALU.mult, [A[i1], noml, oml], [A[i1]], s2=oml[:, h:h + 1], op1=ALU.add, eng=("pool" if POOL_FK else "dve"))
                for (ps, c0, c1) in el_sigmoid(FA, i0, A[i0], 0, True):
                    k.tt(FA[:, i0, c0:c1], FA[:, i0, c0:c1], ps[:, 0:c1 - c0], ALU.mult, [A[i0], ps], [A[i0]])
                    k.free(ps)
                k.act(fa(i2), fa(i3), AF.Exp, [A[i3]], [A[i2]])
                k.tt(qe[:, 0:W], fa(i0), fa(i2), ALU.mult, [A[i0], A[i2]], [qe], eng=("pool" if POOL_QE else "dve"))
                k.act(fa(i0), fa(i3), AF.Exp, [A[i3]], [A[i0]], scale=-1.0)
                k.tt(fa(i0), fa(i0), fa(i1), ALU.mult, [A[i0], A[i1]], [A[i0]])
                k.acopy(ke[:, 0:W], fa(i0), [A[i0]], [ke])
                bc_p = FA[:, i3, 0:512].rearrange("p (c q) -> p c q", q=64)
                k.act(dch[:, 0:8].unsqueeze(2), bc_p[:, :, 63:64], AF.Exp, [A[i3]], [dch])
                k.tt(kdT[:, 0:512].rearrange("p (c q) -> p c q", q=64), FA[:, i0, 0:512].rearrange("p (c q) -> p c q", q=64),
                     dch[:, 0:8].unsqueeze(2).to_broadcast([128, 8, 64]), ALU.mult, [A[i0], dch], [kdT])
                if has_s:
                    bc_s = FA[:, i3, 512:576].rearrange("p (c q) -> p c q", q=4)
                    k.act(dch[:, 8:24].unsqueeze(2), bc_s[:, :, 3:4], AF.Exp, [A[i3]], [dch])
                    k.tt(kdT[:, 512:576].rearrange("p (c q) -> p c q", q=4), FA[:, i0, 512:576].rearrange("p (c q) -> p c q", q=4),
                         dch[:, 8:24].unsqueeze(2).to_broadcast([128, 16, 4]), ALU.mult, [A[i0], dch], [kdT])
                for (ps, c0, c1) in el_sigmoid(FB, 2 + h % 2, sgb, 2, True):
                    k.tt(FB[:, 2 + h % 2, c0:c1], FB[:, 2 + h % 2, c0:c1], ps[:, 0:c1 - c0], ALU.mult, [sgb, ps], [sgb])
                    k.free(ps)
                for (R, c0, bi) in blocks:
                    smp = (bi == 4)
                    attm = attm_l[ctr["at"] % 2]; kdtok = kdtok_l[ctr["at"] % 2]; ctr["at"] += 1
                    pa = k.paux()
                    k.mm(pa[0:R, 0:R], ke[:, c0:c0 + R], qe[:, c0:c0 + R], [ke, qe], [pa])
                    msk = cstb[0:R, C_LS:C_LS + R] if smp else cstb[0:R, C_MASK2:C_MASK2 + R]
                    k.tt(attm[0:R, 0:R], pa[0:R, 0:R], msk, ALU.mult, [pa, cstb], [attm])
                    k.free(pa)
                    k.tr(pb[0:R, 512:640], kdT[:, c0:c0 + R], identb, [kdT, cstb], [pb])
                    (k.vcopy if DVE_KD else k.acopy)(kdtok[0:R, :], pb[0:R, 512:640], [pb], [kdtok])
                    if not smp:
                        have0 = (not first) or (bi > 0)
                        pSs = []
                        for sc in range(2):
                            r0 = sc * 64
                            pS = k.paux()
                            k.mm(pS[:, 0:128], kdtok[r0:r0 + 64, :], vtok[r0:r0 + 64, bi, :], [kdtok, vtok], [pS])
                            pSs.append(pS)
                        po = k.paux()
                        k.mm(po[:, 0:128], vtok[0:128, bi, :], attm[:, :], [vtok, attm], [po], start=True, stop=False, sig=(not have0))
                        if have0:
                            k.mm(po[:, 0:64], Shb2[:, hpar, st["pp"], :], qe[:, c0:c0 + 64], [Shq[hpar][st["pp"]], qe], [po], start=False, stop=False, sig=True)
                        for sc in range(2):
                            dcol_ = dch[:, 2 * bi + sc:2 * bi + sc + 1]
                            if sc == 1:
                                k.mm(po[:, 64:128], Shb2[:, hpar, st["pp"], :], qe[:, c0 + 64:c0 + 128], [Shq[hpar][st["pp"]], qe], [po], start=False, stop=True, sig=True)
                            if not (last and bi == 3 and sc == 1):
                                k.stt(Shb2[:, hpar, 1 - st["pp"], :], S_hg[:, h, :], dcol_, pSs[sc][:, 0:128], ALU.mult, ALU.add, [S_hgc[h], dch, pSs[sc]], [Shq[hpar][1 - st["pp"]]])
                                st["pp"] ^= 1
                            k.stt(S_hg[:, h, :], S_hg[:, h, :], dcol_, pSs[sc][:, 0:128], ALU.mult, ALU.add, [S_hgc[h], dch, pSs[sc]], [S_hgc[h]])
                            k.free(pSs[sc])
                    else:
                        for hb in range(2):
                            k.load(g_ss_l[hb], sst_l[hb][:, :, :], hg0[8 * hb:8 * hb + 8, h * 128:(h + 1) * 128, :].rearrange("b q n -> q b n"), [sst_l[hb]])
                            k.acopy(sstb[:, 8 * hb:8 * hb + 8, :], sst_l[hb][:, :, :], [sst_l[hb]], [sstb])
                        po = k.paux()
                        k.mm(po[:, 0:64], vtok[0:64, 4, :], attm[0:64, 0:64], [vtok, attm], [po], start=True, stop=False, sig=False)
                        for b in range(16):
                            k.mm(po[:, 4 * b:4 * b + 4], sstb[:, b, :], qe[:, 512 + 4 * b:512 + 4 * b + 4], [sstb, qe], [po], start=False, stop=(b == 15), sig=(b == 15))
                        k.tt(EXPB[0:64, :].rearrange("p (b n) -> p b n", b=16),
                             vtok[0:64, 4, :].unsqueeze(1).to_broadcast([64, 16, 128]),
                             cstb[0:64, C_SEG:C_SEG + 16].unsqueeze(2).to_broadcast([64, 16, 128]), ALU.mult, [vtok, cstb], [EXPB])
                        for hb in range(2):
                            sst = sst_l[hb]
                            k.tt(sst[:, :, :], sst[:, :, :], dch[:, 8 + 8 * hb:16 + 8 * hb].unsqueeze(2).to_broadcast([128, 8, 128]), ALU.mult, [sst, dch], [sst])
                            for bq in range(2):
                                pS = k.pmm()
                                e0 = (8 * hb + 4 * bq) * 128
                                k.mm(pS[:, 0:512], kdtok[0:64, :], EXPB[0:64, e0:e0 + 512], [kdtok, EXPB], [pS])
                                k.tt(sst[:, bq * 4:bq * 4 + 4, :], sst[:, bq * 4:bq * 4 + 4, :], pS[:, :].rearrange("p (b n) -> p b n", b=4), ALU.add, [sst, pS], [sst])
                                k.free(pS)
                            k.store(g_ss_l[hb], hgs[8 * hb:8 * hb + 8, h * 128:(h + 1) * 128, :].rearrange("b q n -> q b n"), sst[:, :, :], [sst])
                    (k.vcopy if DVE_O else k.acopy)(FB[:, h % 2, c0:c0 + R], po[:, 0:R], [po], [ob])
                    k.free(po)
                k.act(BIGB[:, 16 + h % 2, 0:W], o_ap, AF.Square, [ob], [BBc[16 + h % 2]])
                rsh = rs2 if h % 2 == 0 else rs
                for (c0, c1) in crs:
                    ps = k.pmm()
                    k.mm(ps[:, 0:c1 - c0], onesb, BIGB[:, 16 + h % 2, c0:c1], [cstb, BBc[16 + h % 2]], [ps])
                    k.act(rsh[:, c0:c1], ps[:, 0:c1 - c0], AF.Ln, [ps], [rsh], bias=EPS, scale=1.0 / 128.0)
                    k.free(ps)
                k.act(rsh[:, 0:W], rsh[:, 0:W], AF.Exp, [rsh], [rsh], scale=-0.5)
                k.stt(o_ap, o_ap, par[:, P_HGN + h:P_HGN + h + 1], rsh[:, 0:W], ALU.mult, ALU.mult, [ob, par, rsh], [ob])
                k.tt(mixed[:, 8 + h, 0:W], o_ap, sog, ALU.mult, [ob, sgb], [BBc[8 + h]])
            for kind, idx in MIX_ORDER:
                {"gp": ssd_proj, "gb": ssd_blocks, "h": hgrn_head}[kind](idx)
            if last:
                k.store(g_so, ssmp.rearrange("(a q) n -> q a n", q=128), S_ssm[:, :, :], S_ssmc)
                k.store(g_so, hgp.rearrange("(a q) n -> q a n", q=128), S_hg[:, :, :], S_hgc)

            for b in range(4):
                slot, wv = k.wnext()
                for f2 in range(2):
                    fo = 2 * b + f2
                    for (c0, c1) in crs:
                        ps = linear(wv, slot, f2 * 128, mixed, BBc, 16, c0, c1)
                        k.tt(xT[:, fo, c0:c1], xT[:, fo, c0:c1], ps[:, 0:c1 - c0], ALU.add, [xTc[fo], ps], [xTc[fo]])
                        k.free(ps)

        def ple_and_out(ti, W):
            crs = colranges(W)
            norm_to_xn(P_NPLE, W)
            slot, wv = k.wnext()
            for fo in range(8):
                for (c0, c1) in crs:
                    ps = linear(wv, slot, fo * 128, pT, pT, 2, c0, c1)
                    k.acopy(FA[:, fo, c0:c1], ps[:, 0:c1 - c0], [ps], [FAc[fo]])
                    k.free(ps)
            rmsnorm_stats(FA, FAc, 8, W, 0, 1.0 / 1024.0, rs2)
            for b in range(2):
                slot, wv = k.wnext()
                for f4 in range(4):
                    fo = 4 * b + f4
                    ge, en = 2 * (fo % 2), 2 * (fo % 2) + 1
                    k.stt(FB[:, en, 0:W], FA[:, fo, 0:W], par[:, P_PPOST + fo:P_PPOST + fo + 1], rs2[:, 0:W], ALU.mult, ALU.mult, [FAc[fo], par, rs2], [FBc[en]])
                    for (c0, c1) in crs:
                        ps = linear(wv, slot, f4 * 128, xn, xn, 8, c0, c1)
                        k.act(FB[:, ge, c0:c1], ps[:, 0:c1 - c0], AF.Sigmoid, [ps], [FBc[ge]])
                        k.free(ps)
                    k.tt(FB[:, en, 0:W], FB[:, en, 0:W], FB[:, ge, 0:W], ALU.mult, [FBc[en], FBc[ge]], [FBc[en]])
                    k.tt(xT[:, fo, 0:W], xT[:, fo, 0:W], FB[:, en, 0:W], ALU.add, [xTc[fo], FBc[en]], [xTc[fo]])
            rmsnorm_stats(xT, xTc, 8, W, 0, 1.0 / 1024.0, rs2)
            for i in range(8):
                k.stt(FA[:, i, 0:W], xT[:, i, 0:W], par[:, P_NFIN + i:P_NFIN + i + 1], rs2[:, 0:W], ALU.mult, ALU.mult, [xTc[i], par, rs2], [FAc[i]])
            blocks = [(128, i * 128, yp[ti * 512 + i * 128: ti * 512 + (i + 1) * 128, :]) for i in range(4)]
            if W > 512:
                blocks.append((64, 512, ysm))
            for bi, (R, c0, dst) in enumerate(blocks):
                yb, ygrp = [(yout[0], g_y[0]), (yout[1], g_y[1])][bi % 2]
                for half in range(2):
                    ps = k.paux()
                    for q in range(4):
                        kc = half * 4 + q
                        k.tr(ps[0:R, q * 128:(q + 1) * 128], FA[:, kc, c0:c0 + R], ident, [FAc[kc], cst], [ps], sig=(q == 3))
                    k.acopy(yb[0:R, half * 512:(half + 1) * 512], ps[0:R, 0:512], [ps], [yb])
                    k.free(ps)
                k.store(ygrp, dst, yb[0:R, :], [yb])

        bufs = lambda ti_: (xT_A, xTc_A) if ti_ % 2 == 0 else (xT_B, xTc_B)
        Wof = lambda ti_: 576 if ti_ == 0 else 512
        load_tile(0, Wof(0), "x", *bufs(0))
        for ti in range(NT):
            W = Wof(ti)
            xT, xTc = bufs(ti)
            load_tile(ti, W, "p", xT, xTc)
            ffn(P_NF1, W, sq0=8)
            mixer(ti, W)
            if ti + 1 < NT:
                guard["b"] = [sst_l[0], sst_l[1], sstb, EXPB] if ti + 1 == 1 else []
                load_tile(ti + 1, Wof(ti + 1), "x", *bufs(ti + 1))
                guard["b"] = []
            ffn(P_NF2, W)
            ple_and_out(ti, W)
        P.wait_all("sp", out_groups)
        P.emit()
    return nc


_NC_CACHE = {}


def kernel(**inp):
    f = lambda a: np.ascontiguousarray(np.asarray(a, dtype=np.float32))
    g = {k_: f(v) for k_, v in inp.items()}
    cst = make_consts()
    par = np.zeros((128, NPAR), np.float32)
    par[:, P_NF1:P_NF1 + 8] = fm(g["norm_ffn1"][0]); par[:, P_NMIX:P_NMIX + 8] = fm(g["norm_mix"][0])
    par[:, P_NF2:P_NF2 + 8] = fm(g["norm_ffn2"][0]); par[:, P_NPLE:P_NPLE + 8] = fm(g["norm_ple"][0])
    par[:, P_PPOST:P_PPOST + 8] = fm(g["ple_post_norm"][0]); par[:, P_NFIN:P_NFIN + 8] = fm(g["norm_final"])
    par[:, P_SSDN:P_SSDN + 8] = fm(g["ssd_norm"][0]); par[:, P_HGN:P_HGN + 8] = fm(g["hg_norm"][0])
    for j in range(4):
        par[:, P_CW + j * 12:P_CW + (j + 1) * 12] = fm(g["conv_w"][0, j])
    par[:, P_CB:P_CB + 12] = fm(g["conv_b"][0])
    par[:, P_LB0:P_LB0 + 8] = fm(g["hg_lb_logits"][0]); par[:, P_LB1:P_LB1 + 8] = fm(g["hg_lb_logits"][1])
    par[:, P_DSK:P_DSK + 8] = fm(np.repeat(g["d_skip"][0], 64))
    bcr = np.zeros((128, 32), np.float32)
    bcr[:, 0:16] = g["dt_bias"][0][None, :]; bcr[:, 16:32] = g["a_log"][0][None, :]
    shared = dict(wf1u=g["w_ffn1_up"][0], wf1d=g["w_ffn1_down"][0], win=g["w_in"][0], wout=g["w_out"][0],
                  wf2u=g["w_ffn2_up"][0], wf2d=g["w_ffn2_down"][0], wpg=g["w_ple_gate"][0], wpp=g["w_ple_proj"][0],
                  cst=cst, par=par, bcr=bcr)
    in_maps = []
    for c in range(NCORES):
        bs = slice(16 * c, 16 * c + 16)
        m = dict(shared)
        m.update(xp=g["x_prompt"][c], xsm=g["x_sample"][bs].reshape(64, 1024),
                 cv0=g["state_conv"][0, bs].reshape(48, 1536),
                 ssm0=g["state_ssm"][0, bs].reshape(16, 1024, 128), hg0=g["state_hgrn"][0, bs].reshape(16, 1024, 128),
                 pp=g["p_prompt"][0, c], psm=g["p_sample"][0, bs].reshape(64, 256))
        in_maps.append({k_: np.ascontiguousarray(v) for k_, v in m.items()})
    if "nc" not in _NC_CACHE:
        _NC_CACHE["nc"] = build_program()
    nc = _NC_CACHE["nc"]
    res = run_bass_kernel_spmd(nc, in_maps, core_ids=list(range(NCORES)))
    R = res.results
    cat = lambda name: [np.asarray(R[c][name], np.float32) for c in range(NCORES)]
    y_prompt = np.stack(cat("yp"), 0)
    y_sample = np.concatenate([a.reshape(16, 4, 1024) for a in cat("ysm")], 0)
    conv_p = np.stack(cat("cvp"), 0)[None]
    ssm_p = np.stack([a.reshape(16, 64, 128) for a in cat("ssmp")], 0)[None]
    hg_p = np.stack([a.reshape(8, 128, 128) for a in cat("hgp")], 0)[None]
    conv_s = np.concatenate([a.reshape(16, 3, 1536) for a in cat("cvs")], 0)[None]
    ssm_s = np.concatenate([a.reshape(16, 16, 64, 128) for a in cat("ssms")], 0)[None]
    hg_s = np.concatenate([a.reshape(16, 8, 128, 128) for a in cat("hgs")], 0)[None]
    return (y_prompt, y_sample, conv_p, ssm_p, hg_p, conv_s, ssm_s, hg_s)
```
